# Optimizing a Trainium2 kernel written in Bass

```python
import jax, jax.numpy as jnp
from jax import lax
import numpy as np

D_MODEL = 2048
BATCH = 2
SEQ = 8192
DEPTH = 2

CHUNK = 64
N_MEM = 256
GLA_HEADS = 4
GLA_DK = D_MODEL // 2
GLA_DV = D_MODEL
GLA_HK = GLA_DK // GLA_HEADS
GLA_HV = GLA_DV // GLA_HEADS
GLA_RANK = 16
GLA_TAU = 16.0
CONV_WIDTH = D_MODEL
CONV_K = 31
XA_HEADS = 4
XA_HEAD_DIM = 128
XA_WIDTH = XA_HEADS * XA_HEAD_DIM
D_FF = 5632
FFN_RES = 0.5
N_BRANCH = 2
EPS = 1e-6
SPLITS = (GLA_DK, GLA_DK, GLA_DV, GLA_RANK, GLA_DV, 2 * CONV_WIDTH, N_BRANCH * D_MODEL)
D_IN = 2 * GLA_DK + 2 * GLA_DV + GLA_RANK + 2 * CONV_WIDTH + N_BRANCH * D_MODEL

kernel_name = "gla_conformer_gated_hybrid"


def rmsnorm(x, g):
    xf = x.astype(jnp.float32)
    y = xf * lax.rsqrt(jnp.mean(xf * xf, axis=-1, keepdims=True) + EPS)
    return (y * g.astype(jnp.float32)).astype(x.dtype)


def layernorm(x, g, b):
    xf = x.astype(jnp.float32)
    mu = jnp.mean(xf, axis=-1, keepdims=True)
    xc = xf - mu
    y = xc * lax.rsqrt(jnp.mean(xc * xc, axis=-1, keepdims=True) + EPS)
    return (y * g.astype(jnp.float32) + b.astype(jnp.float32)).astype(x.dtype)


def swiglu_half(x, norm_g, w_in, w_out):
    h = rmsnorm(x, norm_g)
    a, b = jnp.split(h @ w_in, 2, axis=-1)
    return x + FFN_RES * ((jax.nn.silu(a) * b) @ w_out)


def gla_chunked(q, k, v, log_a):
    B, T, H, dk = q.shape
    dv = v.shape[-1]
    n = T // CHUNK

    def to_chunks(t):
        return t.reshape(B, n, CHUNK, H, t.shape[-1]).transpose(1, 0, 3, 2, 4)

    mask = jnp.tril(jnp.ones((CHUNK, CHUNK), dtype=bool))[:, :, None]

    def step(S, inp):
        qi, ki, vi, ai = inp
        qf = qi.astype(jnp.float32)
        kf = ki.astype(jnp.float32)
        vf = vi.astype(jnp.float32)
        b = jnp.cumsum(ai.astype(jnp.float32), axis=2)
        o_inter = jnp.einsum('bhcd,bhde->bhce', qf * jnp.exp(b), S)
        diff = b[:, :, :, None, :] - b[:, :, None, :, :]
        decay = jnp.exp(jnp.where(mask, diff, -jnp.inf))
        scores = jnp.einsum('bhid,bhjd,bhijd->bhij', qf, kf, decay)
        o = o_inter + jnp.einsum('bhij,bhje->bhie', scores, vf)
        b_last = b[:, :, -1:, :]
        S_new = (jnp.exp(b_last[:, :, 0, :])[..., None] * S
                 + jnp.einsum('bhcd,bhce->bhde', kf * jnp.exp(b_last - b), vf))
        return S_new, o

    S0 = jnp.zeros((B, H, dk, dv), jnp.float32)
    _, o = lax.scan(step, S0, (to_chunks(q), to_chunks(k), to_chunks(v), to_chunks(log_a)))
    return o.transpose(1, 0, 3, 2, 4).reshape(B, T, H, dv)


def mixing(x, norm_g, w_in, gate_w2, gate_b, gla_norm_g, gla_proj,
           conv_w, conv_b, conv_ln_g, conv_ln_b, conv_proj, branch_gate_b, w_out):
    B, T, _ = x.shape
    h = rmsnorm(x, norm_g)
    z = h @ w_in
    q, k, v, r, g, u, gts = jnp.split(z, np.cumsum(SPLITS)[:-1].tolist(), axis=-1)
    log_a = jax.nn.log_sigmoid((r @ gate_w2 + gate_b).astype(jnp.float32)) / GLA_TAU
    q = q.reshape(B, T, GLA_HEADS, GLA_HK) * (GLA_HK ** -0.5)
    k = k.reshape(B, T, GLA_HEADS, GLA_HK)
    v = v.reshape(B, T, GLA_HEADS, GLA_HV)
    log_a = log_a.reshape(B, T, GLA_HEADS, GLA_HK)
    o = gla_chunked(q, k, v, log_a)
    o = o * lax.rsqrt(jnp.mean(o * o, axis=-1, keepdims=True) + EPS)
    o = (o.reshape(B, T, GLA_DV) * gla_norm_g.astype(jnp.float32)).astype(x.dtype)
    y_a = (o * jax.nn.silu(g)) @ gla_proj
    ua, ug = jnp.split(u, 2, axis=-1)
    c = ua * jax.nn.sigmoid(ug)
    c = lax.conv_general_dilated(
        c, conv_w[:, None, :], window_strides=(1,), padding=[(CONV_K - 1, 0)],
        dimension_numbers=('NWC', 'WIO', 'NWC'), feature_group_count=CONV_WIDTH) + conv_b
    c = jax.nn.silu(layernorm(c, conv_ln_g, conv_ln_b))
    y_b = c @ conv_proj
    gates = jax.nn.sigmoid(gts + branch_gate_b).reshape(B, T, N_BRANCH, D_MODEL)
    m = gates[:, :, 0, :] * y_a + gates[:, :, 1, :] * y_b
    return x + m @ w_out


def memory_xattn(x, mem, norm_g, mem_norm_g, w_q, w_kv, w_out):
    B, T, _ = x.shape
    h = rmsnorm(x, norm_g)
    mh = rmsnorm(mem, mem_norm_g)
    q = (h @ w_q).reshape(B, T, XA_HEADS, XA_HEAD_DIM)
    k, v = jnp.split(mh @ w_kv, 2, axis=-1)
    k = k.reshape(B, N_MEM, XA_HEADS, XA_HEAD_DIM)
    v = v.reshape(B, N_MEM, XA_HEADS, XA_HEAD_DIM)
    s = jnp.einsum('bthd,bmhd->bhtm', q, k).astype(jnp.float32) * (XA_HEAD_DIM ** -0.5)
    p = jax.nn.softmax(s, axis=-1).astype(x.dtype)
    o = jnp.einsum('bhtm,bmhd->bthd', p, v).reshape(B, T, XA_WIDTH)
    return x + o @ w_out


def setup_inputs(seed: int = 0) -> dict:
    key = jax.random.key(seed)
    ks = iter(jax.random.split(key, 32))
    f32 = jnp.float32
    L, D = DEPTH, D_MODEL

    def w(shape, fan_in):
        return jax.random.normal(next(ks), shape, f32) * (fan_in ** -0.5)

    def gain(shape):
        return 1.0 + 0.01 * jax.random.normal(next(ks), shape, f32)

    def bias(shape, s=0.01):
        return s * jax.random.normal(next(ks), shape, f32)

    return {
        "x": jax.random.normal(next(ks), (BATCH, SEQ, D), f32),
        "mem": jax.random.normal(next(ks), (BATCH, N_MEM, D), f32),
        "ffn1_norm": gain((L, D)),
        "ffn1_w_in": w((L, D, 2 * D_FF), D),
        "ffn1_w_out": w((L, D_FF, D), D_FF),
        "mix_norm": gain((L, D)),
        "mix_w_in": w((L, D, D_IN), D),
        "gla_gate_w2": w((L, GLA_RANK, GLA_DK), GLA_RANK),
        "gla_gate_b": bias((L, GLA_DK), 0.1),
        "gla_out_norm": gain((L, GLA_DV)),
        "gla_proj": w((L, GLA_DV, D), GLA_DV),
        "conv_w": w((L, CONV_K, CONV_WIDTH), CONV_K),
        "conv_b": bias((L, CONV_WIDTH)),
        "conv_ln_g": gain((L, CONV_WIDTH)),
        "conv_ln_b": bias((L, CONV_WIDTH)),
        "conv_proj": w((L, CONV_WIDTH, D), CONV_WIDTH),
        "branch_gate_b": bias((L, N_BRANCH * D)),
        "mix_w_out": w((L, D, D), D),
        "xa_norm": gain((L, D)),
        "xa_mem_norm": gain((L, D)),
        "xa_w_q": w((L, D, XA_WIDTH), D),
        "xa_w_kv": w((L, D, 2 * XA_WIDTH), D),
        "xa_w_out": w((L, XA_WIDTH, D), XA_WIDTH),
        "ffn2_norm": gain((L, D)),
        "ffn2_w_in": w((L, D, 2 * D_FF), D),
        "ffn2_w_out": w((L, D_FF, D), D_FF),
        "final_norm": gain((D,)),
    }


def reference(x, mem, ffn1_norm, ffn1_w_in, ffn1_w_out, mix_norm, mix_w_in, gla_gate_w2,
              gla_gate_b, gla_out_norm, gla_proj, conv_w, conv_b, conv_ln_g, conv_ln_b,
              conv_proj, branch_gate_b, mix_w_out, xa_norm, xa_mem_norm, xa_w_q, xa_w_kv,
              xa_w_out, ffn2_norm, ffn2_w_in, ffn2_w_out, final_norm):
    for l in range(DEPTH):
        x = swiglu_half(x, ffn1_norm[l], ffn1_w_in[l], ffn1_w_out[l])
        x = mixing(x, mix_norm[l], mix_w_in[l], gla_gate_w2[l], gla_gate_b[l], gla_out_norm[l],
                   gla_proj[l], conv_w[l], conv_b[l], conv_ln_g[l], conv_ln_b[l], conv_proj[l],
                   branch_gate_b[l], mix_w_out[l])
        x = memory_xattn(x, mem, xa_norm[l], xa_mem_norm[l], xa_w_q[l], xa_w_kv[l], xa_w_out[l])
        x = swiglu_half(x, ffn2_norm[l], ffn2_w_in[l], ffn2_w_out[l])
    return rmsnorm(x, final_norm)
```

```python
import numpy as np
import concourse.bass as bass
import concourse.mybir as mybir
from concourse.bass_utils import run_bass_kernel_spmd
from contextlib import ExitStack

F32 = mybir.dt.float32
BF16 = mybir.dt.bfloat16
AF = mybir.ActivationFunctionType
ALU = mybir.AluOpType
AXX = mybir.AxisListType.X

L = 2
D = 2048
DFF = 5632
DIN = 14352
NCORE = 8
T = 2048
TB = 1024
NH = TB // 512
NBLK = T // TB
NCHUNK = T // 64
HALO = 32
SLOT = 4096
NSLOT = 8
LOOKAHEAD = 6
EPS = 1e-6
C_Q, C_K, C_V, C_R, C_G, C_UA, C_UG, C_G0, C_G1 = 0, 1024, 2048, 4096, 4112, 6160, 8208, 10256, 12304


class Buf:
    __slots__ = ("name", "w", "r")

    def __init__(self, name=""):
        self.name = name
        self.w = None
        self.r = {}


class Tile:
    def __init__(self, t, name=""):
        self.t = t
        self.b = Buf(name)
        self.subs = {}

    def __getitem__(self, idx):
        return self.t[idx]

    def s(self, *key):
        if key not in self.subs:
            nb = Buf(f"{self.b.name}{key}")
            nb.w = self.b.w
            nb.r = dict(self.b.r)
            self.subs[key] = nb
        return self.subs[key]

    def all(self):
        return [self.b] + list(self.subs.values())


def flat(xs):
    out = []
    for x in xs:
        if isinstance(x, Tile):
            out.extend(x.all())
        elif isinstance(x, (list, tuple)):
            out.extend(flat(x))
        else:
            out.append(x)
    return out


class Sched:
    ENG = ("pe", "act", "dve", "pool", "sp")

    def __init__(self, sem_names, pools=None):
        self.cnt = {k: 0 for k in sem_names}
        self.pools = pools or {}
        self.pool_idx = {k: 0 for k in self.pools}
        self.known = {e: {} for e in self.ENG}
        self.stream = {e: [] for e in self.ENG}
        self.noinc = {e: False for e in self.ENG}

    def _deps(self, eng, reads, writes):
        deps = {}

        def add(k, v):
            if eng == "pe" and k == "pe":
                return
            if deps.get(k, 0) < v:
                deps[k] = v

        for b in reads:
            if b.w is not None:
                add(*b.w)
        for b in writes:
            if b.w is not None:
                add(*b.w)
            for k, v in b.r.items():
                add(k, v)
        return deps

    def waits(self, eng, deps):
        kn = self.known[eng]
        for k, v in deps.items():
            if kn.get(k, 0) < v:
                kn[k] = v
                self.stream[eng].append(("w", k, v))

    @staticmethod
    def _mark(tok, reads, writes):
        k, v = tok
        for b in reads:
            if b.r.get(k, 0) < v:
                b.r[k] = v
        for b in writes:
            b.w = tok
            b.r = {}

    def I(self, eng, fn, reads=(), writes=(), inc=True):
        reads = flat(reads)
        writes = flat(writes)
        self.waits(eng, self._deps(eng, reads, writes))
        if inc:
            self.cnt[eng] += 1
            v = self.cnt[eng]
            self.stream[eng].append(("i", fn, eng, 1))
            self.noinc[eng] = False
        else:
            v = self.cnt[eng] + 1
            self.stream[eng].append(("n", fn))
            self.noinc[eng] = True
        self._mark((eng, v), reads, writes)

    def X(self, eng, semk, amount, fn, reads=(), writes=()):
        reads = flat(reads)
        writes = flat(writes)
        self.waits(eng, self._deps(eng, reads, writes))
        self.cnt[semk] += amount
        self.stream[eng].append(("i", fn, semk, amount))
        self._mark((semk, self.cnt[semk]), reads, writes)
        return (semk, self.cnt[semk])

    def D(self, q, semk, out, in_, reads=(), writes=(), extra_deps=None, **kw):
        reads = flat(reads)
        writes = flat(writes)
        deps = self._deps(q, reads, writes)
        if semk in self.pools:
            pname = semk
            lst = self.pools[pname]
            semk = lst[self.pool_idx[pname] % len(lst)]
            self.pool_idx[pname] += 1
            if self.cnt[semk] > 0:
                deps[semk] = max(deps.get(semk, 0), self.cnt[semk])
        if extra_deps:
            for k, v in extra_deps.items():
                if deps.get(k, 0) < v:
                    deps[k] = v
        self.waits(q, deps)
        self.cnt[semk] += 16
        v = self.cnt[semk]
        self.stream[q].append(("i", lambda e: e.dma_start(out=out, in_=in_, **kw), semk, 16))
        self._mark((semk, v), reads, writes)
        return (semk, v)

    def barrier(self, engs=("pe", "act", "dve", "sp"), skip=("pool",), skip_w=True):
        toks = {k: v for k, v in self.cnt.items() if v > 0 and k not in skip and not (skip_w and k.startswith("wq"))}
        for e in engs:
            self.waits(e, toks)

    def replay(self, eng, e, sems):
        for it in self.stream[eng]:
            if it[0] == "w":
                e.wait_ge(sems[it[1]], it[2])
            elif it[0] == "i":
                it[1](e).then_inc(sems[it[2]], it[3])
            else:
                it[1](e)


class Prog:
    def __init__(self, plan=None, debug=None):
        self.plan = plan
        self.dry = plan is None
        self.debug = debug or {}
        self.slab_ids = {}
        self.slab_specs = []
        self.wseq = []
        self.wfree = []
        self.wtok = []
        self.w_emitted = 0
        self.w_req = 0

    def slab_id(self, spec):
        if spec not in self.slab_ids:
            self.slab_ids[spec] = len(self.slab_specs)
            self.slab_specs.append(spec)
        return self.slab_ids[spec]

    def _emit_wdma(self, n):
        sid = self.plan["seq"][n]
        slot = self.wslots[n % NSLOT]
        deps = self.plan["free"][n]
        self.S.D("pool", f"wq{n % NSLOT}", slot.t[:, :], self.wall[sid], extra_deps=deps)

    def wload(self, name, l, r0, nkc, c0, ncols):
        spec = (name, l, r0, nkc, c0, ncols)
        assert nkc * ncols <= SLOT
        sid = self.slab_id(spec)
        n = self.w_req
        self.w_req += 1
        slot = self.wslots[n % NSLOT]
        if self.dry:
            self.wseq.append(sid)
            free = {}
            if slot.b.w is not None:
                free[slot.b.w[0]] = slot.b.w[1]
            for k, v in slot.b.r.items():
                if free.get(k, 0) < v:
                    free[k] = v
            for k in list(free):
                if k.startswith("wq"):
                    free.pop(k)
            self.wfree.append(free)
            self.S.cnt[f"wq{n % NSLOT}"] += 16
        else:
            assert self.plan["seq"][n] == sid
            while self.w_emitted < min(len(self.plan["seq"]), n + 1 + LOOKAHEAD):
                self._emit_wdma(self.w_emitted)
                self.w_emitted += 1
        slot.b.w = (f"wq{n % NSLOT}", 16 * (n // NSLOT + 1))
        slot.b.r = {}
        view = slot.t[:, 0:nkc * ncols].rearrange("p (k c) -> p k c", k=nkc)
        return view, slot.b

    def sb(self, es, name, shape, dt):
        self.uid = getattr(self, "uid", 0) + 1
        name = f"{name}_{self.uid}"
        t = es.enter_context(self.nc.sbuf_tensor(name, list(shape), dt))
        return Tile(t, name)

    def pbank(self, pool):
        i = self.pidx[pool]
        self.pidx[pool] = (i + 1) % len(self.ppool[pool])
        return self.ps[self.ppool[pool][i]]

    def set_pools(self, **pools):
        self.ppool = {k: list(v) for k, v in pools.items()}
        self.pidx = {k: 0 for k in pools}

    def dbuf(self, key):
        if key not in self.dbufs:
            self.dbufs[key] = Buf(str(key))
        return self.dbufs[key]

    def vec(self, name, l=None):
        off, w = self.vec_off[(name, l)]
        return self.vecs.t[:, off:off + w]

    def build(self):
        nc = bass.Bass("TRN2", target_bir_lowering=False)
        self.nc = nc
        pools = {"dma_ld": [f"ld{i:02d}" for i in range(12)], "dma_st": [f"st{i:02d}" for i in range(8)]}
        sem_names = ["pe", "act", "dve", "pool", "sp", "cc"] + [f"wq{i}" for i in range(NSLOT)] + pools["dma_ld"] + pools["dma_st"]
        self.S = S = Sched(sem_names, pools)
        self.dbufs = {}
        dt = nc.dram_tensor
        self.x_in = dt("x_in", [T, D], F32, kind="ExternalInput").ap()
        self.mem_in = dt("mem_in", [256, D], F32, kind="ExternalInput").ap()
        self.vec_off, nvc = vec_layout()
        self.vecs_d = dt("vecs", [128, nvc], F32, kind="ExternalInput").ap()
        self.cst_d = dt("cst", [128, 128 + 64 + TB], F32, kind="ExternalInput").ap()
        self.gnb_d = dt("gnb", [L, 128, D], F32, kind="ExternalInput").ap()
        self.gw2_d = dt("gw2", [L, 16, 1024], F32, kind="ExternalInput").ap()
        nslab = max(1, self.plan["nslab"]) if not self.dry else 1
        self.wall = dt("wall", [nslab, 128, SLOT], F32, kind="ExternalInput").ap()
        self.out_d = dt("out", [T, D], F32, kind="ExternalOutput").ap()
        ik = "ExternalOutput" if self.debug.get("dump") else "Internal"
        self.xT = dt("xT", [D, T], F32, kind=ik).ap()
        if self.debug.get("upto", "all") not in ("s0", "none", "s0only"):
            self.qT_d = dt("qT_s", [1024, T], BF16, kind=ik).ap()
            self.kT_d = dt("kT_s", [1024, T], BF16, kind=ik).ap()
            self.kTok_d = dt("kTok_s", [T, 1024], BF16, kind=ik).ap()
            self.vTok_d = dt("vTok_s", [T, D], BF16, kind=ik).ap()
            self.sgn_d = dt("sgn_s", [T, D], BF16, kind=ik).ap()
            self.cT_d = dt("cT_s", [D, HALO + T], F32, kind=ik).ap()
            self.g0T_d = dt("g0T_s", [D, T], BF16, kind=ik).ap()
            self.g1T_d = dt("g1T_s", [D, T], BF16, kind=ik).ap()
            self.dec_d = dt("dec_s", [128, 8, NCHUNK], F32, kind=ik).ap()
            self.exS_in = [[dt(f"exS_in{l}_{q}", [256, 512], F32, kind="Internal").ap() for q in range(4)] for l in range(L)]
            self.exS_out = [[dt(f"exS_out{l}_{q}", [4 * 256, 512], F32, kind="Internal").ap() for q in range(4)] for l in range(L)]
            self.exM_in = [dt(f"exM_in{l}", [D + 128, 32], F32, kind="Internal").ap() for l in range(L)]
            self.exM_out = [dt(f"exM_out{l}", [4 * (D + 128), 32], F32, kind="Internal").ap() for l in range(L)]
            self.st_d = dt("st_s", [128, 8, 512], F32, kind=ik).ap()
        if self.debug.get("dump"):
            self.dbg_og = dt("dbg_og", [D, T], BF16, kind="ExternalOutput").ap()
            self.dbg_cn = dt("dbg_cn", [D, T], BF16, kind="ExternalOutput").ap()
            self.dbg_m = dt("dbg_m", [D, T], BF16, kind="ExternalOutput").ap()
        self.dbg_d = None
        if self.debug.get("dbg_shape"):
            self.dbg_d = dt("dbg", list(self.debug["dbg_shape"]), F32, kind="ExternalOutput").ap()

        with ExitStack() as es:
            self.sems = {k: es.enter_context(nc.semaphore(k)) for k in sem_names}
            self.ps = [Tile(es.enter_context(nc.psum_tensor(f"ps{i}", [128, 512], F32)), f"ps{i}") for i in range(8)]
            self.wslots = [self.sb(es, f"wslot{i}", [128, SLOT], BF16) for i in range(NSLOT)]
            self.vecs = self.sb(es, "vecs_sb", [128, nvc], F32)
            self.cst = self.sb(es, "cst_sb", [128, 128 + 64 + TB], F32)
            self.identb = self.sb(es, "identb", [128, 128], BF16)
            self.onesb = self.sb(es, "onesb", [128, 128], BF16)
            self.kmT = [self.sb(es, f"kmT{l}", [128, 4, 256], BF16) for l in range(L)]
            self.vm = [self.sb(es, f"vm{l}", [128, 2, 512], BF16) for l in range(L)]
            block = es.enter_context(nc.Block())
            self.set_pools(a=range(8))
            S.D("sp", "dma_ld", self.vecs.t[:, :], self.vecs_d, writes=[self.vecs])
            S.D("sp", "dma_ld", self.cst.t[:, :], self.cst_d, writes=[self.cst])
            self.ident = self.cst.t[:, 0:128]
            self.mask64 = self.cst.t[0:64, 128:192]
            self.cmask = self.cst.t[:, 192:192 + TB]
            S.I("dve", lambda e: e.tensor_copy(out=self.identb.t[:, :], in_=self.ident), [self.cst], [self.identb])
            S.I("dve", lambda e: e.memset(self.onesb.t[:, :], 1.0), [], [self.onesb])

            if self.debug.get("upto", "all") != "all":
                junk = self.sb(es, "junk", [128, 64], F32)
                for ap in (self.x_in[0:128, 0:8], self.mem_in[0:128, 0:8], self.gnb_d[0, :, 0:8], self.gw2_d[0, :, 0:8],
                           self.wall[0, :, 0:8]):
                    S.D("sp", "dma_ld", junk.t[0:ap.shape[0], 0:8], ap, writes=[junk])
            self.body(es)

            S.barrier(engs=("pe", "act", "dve", "sp", "pool"), skip=(), skip_w=False)
            if not self.dry:
                @block.sync
                def _(e):
                    S.replay("sp", e, self.sems)

                @block.tensor
                def _(e):
                    S.replay("pe", e, self.sems)

                @block.scalar
                def _(e):
                    S.replay("act", e, self.sems)

                @block.vector
                def _(e):
                    S.replay("dve", e, self.sems)

                @block.gpsimd
                def _(e):
                    S.replay("pool", e, self.sems)
        for e in Sched.ENG:
            assert not S.noinc[e], e
        if self.dry:
            return {"seq": self.wseq, "free": self.wfree, "nslab": len(self.slab_specs), "specs": self.slab_specs}
        return nc

    def body(self, es):
        upto = self.debug.get("upto", "all")
        if upto == "none":
            return
        self.stage_s0()
        if upto == "s0only":
            return
        if upto == "s0":
            return self.stage_final()
        if upto != "ffn1":
            self.stage_mem()
        nl = self.debug.get("layers", L)
        for l in range(nl):
            for blk in range(NBLK):
                self.stage_ffn(l, blk, "ffn1")
                if upto == "ffn1":
                    continue
                self.stage_mix1(l, blk)
            if upto == "ffn1":
                break
            self.stage_pass1(l)
            self.stage_exchange(l)
            for blk in range(NBLK):
                self.stage_mix2(l, blk)
                if upto == "mix2":
                    continue
                self.stage_xattn(l, blk)
                if upto == "xattn":
                    continue
                self.stage_ffn(l, blk, "ffn2")
            if upto in ("mix2", "xattn"):
                break
        self.stage_final()

    def stage_s0(self):
        S, nc = self.S, self.nc
        self.set_pools(a=range(8))
        with ExitStack() as es:
            xin = [self.sb(es, f"s0_x{i}", [128, 4, D], F32) for i in range(1)]
            xo = [self.sb(es, f"s0_o{i}", [128, 16, 512], F32) for i in range(2)]
            for g in range(T // 512):
                xi, xt = xin[0], xo[g % 2]
                S.D("sp", "dma_ld", xi.t[:, :, :], self.x_in[g * 512:(g + 1) * 512, :].rearrange("(j p) d -> p j d", p=128),
                    writes=[xi])
                for k in range(16):
                    pb = self.pbank("a")
                    for j in range(4):
                        S.I("pe", lambda e, pb=pb, xi=xi, j=j, k=k: e.transpose(
                            out=pb.t[:, j * 128:(j + 1) * 128], in_=xi.t[:, j, k * 128:(k + 1) * 128], identity=self.ident),
                            [xi, self.cst], [pb], inc=(j == 3))
                    eng = "act" if k % 2 else "dve"
                    if eng == "act":
                        S.I("act", lambda e, pb=pb, xt=xt, k=k: e.copy(out=xt.t[:, k, :], in_=pb.t[:, :]), [pb], [xt])
                    else:
                        S.I("dve", lambda e, pb=pb, xt=xt, k=k: e.tensor_copy(out=xt.t[:, k, :], in_=pb.t[:, :]), [pb], [xt])
                S.D("sp", "dma_st", self.xT[:, g * 512:(g + 1) * 512].rearrange("(k p) t -> p k t", p=128), xt.t[:, :, :],
                    reads=[xt], writes=[self.dbuf(("xT", g))])
            S.barrier()

    def rstd_bc(self, xt, h, sq, rs, nchunks=16, scale=1.0 / D):
        S = self.S
        pb = self.pbank("a")
        for k in range(nchunks):
            sqk = sq[k % len(sq)]
            S.I("act", lambda e, k=k, sqk=sqk: e.activation(out=sqk.t[:, :], in_=xt.t[:, k, h * 512:(h + 1) * 512], func=AF.Square),
                [xt.s(k, h)], [sqk])
            S.I("pe", lambda e, k=k, pb=pb, sqk=sqk: e.matmul(out=pb.t[:, :], lhsT=self.onesb.t[:, :], rhs=sqk.t[:, :],
                                                              start=(k == 0), stop=(k == nchunks - 1)),
                [sqk, self.onesb], [pb], inc=True)
        S.I("act", lambda e, pb=pb: e.activation(out=rs.t[:, :], in_=pb.t[:, :], func=AF.Sqrt, scale=scale,
                                                 bias=self.vec("eps")), [pb, self.vecs], [rs])
        S.I("dve", lambda e: e.reciprocal(out=rs.t[:, :], in_=rs.t[:, :]), [], [rs])


    def load_x(self, xt, blk, h0, nh):
        for h in range(nh):
            gi = blk * NH + h0 + h
            self.S.D("sp", "dma_ld", xt.t[:, :, h * 512:(h + 1) * 512],
                     self.xT[:, gi * 512:(gi + 1) * 512].rearrange("(k p) t -> p k t", p=128),
                     reads=[self.dbuf(("xT", gi))], writes=[xt.s(k, h) for k in range(16)])

    def store_x(self, xt, blk, h0, nh):
        for h in range(nh):
            gi = blk * NH + h0 + h
            self.S.D("sp", "dma_st", self.xT[:, gi * 512:(gi + 1) * 512].rearrange("(k p) t -> p k t", p=128),
                     xt.t[:, :, h * 512:(h + 1) * 512], reads=[xt.s(k, h) for k in range(16)], writes=[self.dbuf(("xT", gi))])

    def make_hT(self, xt, hT, g, nh, sq, rs, hoff=0):
        S = self.S
        for h in range(nh):
            self.rstd_bc(xt, h, sq, rs)
            ho = hoff + h
            for k in range(16):
                S.I("dve", lambda e, k=k, h=h, ho=ho: e.scalar_tensor_tensor(
                    out=hT.t[:, k, ho * 512:(ho + 1) * 512], in0=xt.t[:, k, h * 512:(h + 1) * 512], scalar=g[:, k:k + 1],
                    in1=rs.t[:, :], op0=ALU.mult, op1=ALU.mult), [xt.s(k, h), rs, self.vecs], [hT.s(k, ho)])

    def mm_group(self, pb, lhs_fn, rhs_fn, nk, reads, M=128, N=512, po=0):
        for k in range(nk):
            self.S.I("pe", lambda e, k=k: e.matmul(out=pb.t[po:po + M, 0:N], lhsT=lhs_fn(k), rhs=rhs_fn(k),
                                                  start=(k == 0), stop=(k == nk - 1)),
                     reads, [pb], inc=(k == nk - 1))


    def ld(self, dst_ap, src_ap, tile):
        self.S.D("sp", "dma_ld", dst_ap, src_ap, writes=[tile])

    def st(self, dst_ap, src_ap, tile):
        self.S.D("sp", "dma_st", dst_ap, src_ap, reads=[tile])

    def bview(self, pb):
        return pb.t[:, :].bitcast(BF16)

    def fm_out(self, W, Wb, c, hT, nh, pool="a", hoff=0):
        S = self.S
        pbs = [self.pbank(pool) for _ in range(nh)]
        for k in range(16):
            for h in range(nh):
                ho = hoff + h
                S.I("pe", lambda e, k=k, h=h, ho=ho, p=pbs[h]: e.matmul(
                    out=p.t[:, :], lhsT=W[:, k, c * 128:(c + 1) * 128], rhs=hT.t[:, k, ho * 512:(ho + 1) * 512],
                    start=(k == 0), stop=(k == 15)), [Wb, hT.s(k, ho)], [pbs[h]], inc=(k == 15))
        return pbs

    def stage_mem(self):
        S = self.S
        self.set_pools(a=range(8))
        with ExitStack() as es:
            mt = self.sb(es, "mem_x", [128, 2, D], F32)
            mn = self.sb(es, "mem_n", [128, 2, D], F32)
            junk = self.sb(es, "mem_j", [128, D], BF16)
            ssq = self.sb(es, "mem_ss", [128, 2], F32)
            mhT = [self.sb(es, f"mem_hT{l}", [128, 16, 256], BF16) for l in range(L)]
            self.ld(mt.t[:, :, :], self.mem_in.rearrange("(j p) d -> p j d", p=128), mt)
            for j in range(2):
                S.I("act", lambda e, j=j: e.activation(out=junk.t[:, :], in_=mt.t[:, j, :], func=AF.Square,
                                                        accum_out=ssq.t[:, j:j + 1]), [mt], [junk, ssq])
            S.I("dve", lambda e: e.tensor_scalar(out=ssq.t[:, :], in0=ssq.t[:, :], scalar1=1.0 / D, scalar2=EPS,
                                                 op0=ALU.mult, op1=ALU.add), [], [ssq])
            S.I("act", lambda e: e.activation(out=ssq.t[:, :], in_=ssq.t[:, :], func=AF.Sqrt), [], [ssq])
            S.I("dve", lambda e: e.reciprocal(out=ssq.t[:, :], in_=ssq.t[:, :]), [], [ssq])
            for j in range(2):
                S.I("dve", lambda e, j=j: e.tensor_scalar(out=mn.t[:, j, :], in0=mt.t[:, j, :], scalar1=ssq.t[:, j:j + 1],
                                                          scalar2=None, op0=ALU.mult), [mt, ssq], [mn])
            for k in range(16):
                pb = self.pbank("a")
                for j in range(2):
                    S.I("pe", lambda e, pb=pb, j=j, k=k: e.transpose(out=pb.t[:, j * 128:(j + 1) * 128],
                                                                     in_=mn.t[:, j, k * 128:(k + 1) * 128], identity=self.ident),
                        [mn, self.cst], [pb], inc=(j == 1))
                for l in range(L):
                    g = self.vec("xa_mem_norm", l)
                    S.I("dve", lambda e, pb=pb, l=l, k=k, g=g: e.tensor_scalar(out=mhT[l].t[:, k, :], in0=pb.t[:, 0:256],
                                                                              scalar1=g[:, k:k + 1], scalar2=None, op0=ALU.mult),
                        [pb, self.vecs], [mhT[l]])
            for l in range(L):
                for s2 in range(2):
                    W, Wb = self.wload("xa_w_kv", l, 0, 16, s2 * 256, 256)
                    for c in range(2):
                        hd = 2 * s2 + c
                        pb = self.pbank("a")
                        for k in range(16):
                            S.I("pe", lambda e, W=W, k=k, c=c, pb=pb, l=l: e.matmul(
                                out=pb.t[:, 0:256], lhsT=W[:, k, c * 128:(c + 1) * 128], rhs=mhT[l].t[:, k, :],
                                start=(k == 0), stop=(k == 15)), [Wb, mhT[l]], [pb], inc=(k == 15))
                        S.I("act", lambda e, pb=pb, l=l, hd=hd: e.copy(out=self.kmT[l].t[:, hd, :], in_=pb.t[:, 0:256]),
                            [pb], [self.kmT[l]])
                for s2 in range(2):
                    W, Wb = self.wload("xa_w_kv", l, 0, 16, 512 + s2 * 256, 256)
                    for mc in range(2):
                        pb = self.pbank("a")
                        for k in range(16):
                            S.I("pe", lambda e, W=W, k=k, mc=mc, pb=pb, l=l: e.matmul(
                                out=pb.t[:, 0:256], lhsT=mhT[l].t[:, k, mc * 128:(mc + 1) * 128], rhs=W[:, k, :],
                                start=(k == 0), stop=(k == 15)), [Wb, mhT[l]], [pb], inc=(k == 15))
                        S.I("act", lambda e, pb=pb, l=l, mc=mc, s2=s2: e.copy(
                            out=self.vm[l].t[:, mc, s2 * 256:(s2 + 1) * 256], in_=pb.t[:, 0:256]), [pb], [self.vm[l]])
            S.barrier()

    def stage_mix1(self, l, blk):
        S = self.S
        tok0 = blk * TB
        NT = TB // 128
        with ExitStack() as es:
            hT = self.sb(es, "m1_h", [128, 16, TB], BF16)
            ebq = self.sb(es, "m1_ebq", [128, 8, TB], BF16)
            ebk = self.sb(es, "m1_ebk", [128, 8, TB], BF16)
            self.set_pools(a=range(0, 6), t=range(6, 8))
            with ExitStack() as es2:
                xt = self.sb(es2, "m1_x", [128, 16, 512], F32)
                sq = [self.sb(es2, f"m1_sq{i}", [128, 512], BF16) for i in range(4)]
                rs = self.sb(es2, "m1_rs", [128, 512], F32)
                for h in range(NH):
                    self.load_x(xt, blk, h, 1)
                    self.make_hT(xt, hT, self.vec("mix_norm", l), 1, sq, rs, hoff=h)
                S.barrier()
            gw2 = self.sb(es, "m1_gw2", [16, 1024], F32)
            gnb = self.sb(es, "m1_gnb", [128, D], F32)
            ngb = self.sb(es, "m1_ngb", [128, 8], F32)
            rT = self.sb(es, "m1_rT", [16, TB], F32)
            dect = self.sb(es, "m1_dec", [128, 8, TB // 64], F32)
            self.ld(gw2.t[:, :], self.gw2_d[l], gw2)
            self.ld(gnb.t[:, :], self.gnb_d[l], gnb)
            S.I("dve", lambda e: e.tensor_scalar(out=ngb.t[:, :], in0=self.vec("gla_gate_b", l), scalar1=-1.0, scalar2=None,
                                                 op0=ALU.mult), [self.vecs], [ngb])
            R, Rb = self.wload("mix_w_in", l, 0, 16, C_R, 16)
            for h in range(NH):
                pb = self.pbank("a")
                for k in range(16):
                    S.I("pe", lambda e, k=k, h=h, pb=pb: e.matmul(out=pb.t[0:16, :], lhsT=R[:, k, 0:16],
                                                                  rhs=hT.t[:, k, h * 512:(h + 1) * 512],
                                                                  start=(k == 0), stop=(k == 15)), [Rb, hT], [pb], inc=(k == 15))
                S.I("act", lambda e, h=h, pb=pb: e.copy(out=rT.t[0:16, h * 512:(h + 1) * 512], in_=pb.t[0:16, :]), [pb], [rT])
            with ExitStack() as es2:
                vtl = [self.sb(es2, f"m1_vt{i}", [128, NT, 256], BF16) for i in range(2)]
                stl = [self.sb(es2, f"m1_st{i}", [128, 2, 256], F32) for i in range(2)]
                spt = [self.sb(es2, f"m1_sp{i}", [128, TB], F32) for i in range(2)]
                cum = [self.sb(es2, f"m1_cum{i}", [128, TB], F32) for i in range(2)]
                et = [self.sb(es2, f"m1_et{i}", [128, 512], F32) for i in range(2)]
                def decay(dkc):
                    sp, cm = spt[dkc % 2], cum[dkc % 2]
                    for h in range(NH):
                        pb = self.pbank("a")
                        ee = et[h % 2]
                        S.I("pe", lambda e, dkc=dkc, h=h, pb=pb: e.matmul(
                            out=pb.t[:, :], lhsT=gw2.t[0:16, dkc * 128:(dkc + 1) * 128], rhs=rT.t[0:16, h * 512:(h + 1) * 512],
                            start=True, stop=True), [gw2, rT], [pb])
                        S.I("act", lambda e, dkc=dkc, pb=pb, ee=ee: e.activation(out=ee.t[:, :], in_=pb.t[:, :], func=AF.Exp,
                                                                                 scale=-1.0, bias=ngb.t[:, dkc:dkc + 1]),
                            [pb, ngb], [ee])
                        S.I("act", lambda e, h=h, sp=sp, ee=ee: e.activation(out=sp.t[:, h * 512:(h + 1) * 512], in_=ee.t[:, :],
                                                                             func=AF.Ln, scale=1.0, bias=self.vec("one")),
                            [ee, self.vecs], [sp])
                    S.I("dve", lambda e, sp=sp, cm=cm: e.tensor_tensor_scan(out=cm.t[:, :], data0=self.cmask, data1=sp.t[:, :],
                                                                            initial=0.0, op0=ALU.mult, op1=ALU.add),
                        [sp, self.cst], [cm])
                    S.I("act", lambda e, cm=cm, dkc=dkc: e.activation(out=ebq.t[:, dkc, :], in_=cm.t[:, :], func=AF.Exp,
                                                                      scale=-1.0 / 16), [cm], [ebq.s(dkc)])
                    S.I("act", lambda e, cm=cm, dkc=dkc: e.activation(out=ebk.t[:, dkc, :], in_=cm.t[:, :], func=AF.Exp,
                                                                      scale=1.0 / 16), [cm], [ebk.s(dkc)])
                    S.I("act", lambda e, cm=cm, dkc=dkc: e.activation(
                        out=dect.t[:, dkc, :], in_=cm.t[:, :].rearrange("p (c t) -> p c t", t=64)[:, :, 63], func=AF.Exp,
                        scale=-1.0 / 16), [cm], [dect])

                ti = 0
                for which in ("v", "g"):
                    cbase = C_V if which == "v" else C_G
                    dst = self.vTok_d if which == "v" else self.sgn_d
                    for s8 in range(8):
                        if which == "v":
                            decay(s8)
                        W, Wb = self.wload("mix_w_in", l, 0, 16, cbase + s8 * 256, 256)
                        vt = vtl[s8 % 2]
                        for jp in range(NT // 2):
                            pb = self.pbank("a")
                            for jj in range(2):
                                j = 2 * jp + jj
                                for k in range(16):
                                    S.I("pe", lambda e, W=W, k=k, j=j, jj=jj, pb=pb: e.matmul(
                                        out=pb.t[:, jj * 256:(jj + 1) * 256], lhsT=hT.t[:, k, j * 128:(j + 1) * 128], rhs=W[:, k, :],
                                        start=(k == 0), stop=(k == 15)), [Wb, hT], [pb], inc=(k == 15))
                            if which == "v":
                                S.I("act", lambda e, vt=vt, jp=jp, pb=pb: e.copy(
                                    out=vt.t[:, 2 * jp:2 * jp + 2, :], in_=pb.t[:, :].rearrange("p (j c) -> p j c", j=2)),
                                    [pb], [vt.s(jp)])
                            else:
                                tt = stl[ti % 2]
                                ti += 1
                                S.I("act", lambda e, tt=tt, pb=pb: e.activation(
                                    out=tt.t[:, :, :], in_=pb.t[:, :].rearrange("p (j c) -> p j c", j=2), func=AF.Silu), [pb], [tt])
                                for jj in range(2):
                                    S.I("dve", lambda e, tt=tt, vt=vt, jp=jp, jj=jj, s8=s8: e.tensor_tensor(
                                        out=vt.t[:, 2 * jp + jj, :], in0=tt.t[:, jj, :], in1=gnb.t[:, s8 * 256:(s8 + 1) * 256],
                                        op=ALU.mult), [tt, gnb], [vt.s(jp)])
                        self.st(dst[tok0:tok0 + TB, s8 * 256:(s8 + 1) * 256].rearrange("(j p) c -> p j c", p=128), vt.t[:, :, :], vt)
                    if which == "v":
                        self.st(self.dec_d[:, :, blk * (TB // 64):(blk + 1) * (TB // 64)], dect.t[:, :, :], dect)
                S.barrier()
            with ExitStack() as es2:
                ob = [self.sb(es2, f"m1_ob{i}", [128, 512], BF16) for i in range(4)]
                ktl = [self.sb(es2, f"m1_kt{i}", [128, 4, 128], BF16) for i in range(2)]
                oi = 0
                for which in ("q", "k"):
                    cbase = C_Q if which == "q" else C_K
                    for s4 in range(4):
                        W, Wb = self.wload("mix_w_in", l, 0, 16, cbase + s4 * 256, 256)
                        for c in range(2):
                            dkc = 2 * s4 + c
                            pbs = self.fm_out(W, Wb, c, hT, NH)
                            for h in range(NH):
                                o = ob[oi % 4]
                                oi += 1
                                cols = slice(tok0 + h * 512, tok0 + (h + 1) * 512)
                                if which == "q":
                                    S.I("dve", lambda e, o=o, p=pbs[h], dkc=dkc, h=h: e.scalar_tensor_tensor(
                                        out=o.t[:, :], in0=p.t[:, :], scalar=0.0625, in1=ebq.t[:, dkc, h * 512:(h + 1) * 512],
                                        op0=ALU.mult, op1=ALU.mult), [pbs[h], ebq.s(dkc)], [o])
                                    self.st(self.qT_d[dkc * 128:(dkc + 1) * 128, cols], o.t[:, :], o)
                                else:
                                    S.I("dve", lambda e, o=o, p=pbs[h], dkc=dkc, h=h: e.tensor_tensor(
                                        out=o.t[:, :], in0=p.t[:, :], in1=ebk.t[:, dkc, h * 512:(h + 1) * 512], op=ALU.mult),
                                        [pbs[h], ebk.s(dkc)], [o])
                                    self.st(self.kT_d[dkc * 128:(dkc + 1) * 128, cols], o.t[:, :], o)
                                    pt = self.pbank("t")
                                    ptb = self.bview(pt)
                                    for j in range(4):
                                        S.I("pe", lambda e, ptb=ptb, o=o, j=j: e.transpose(
                                            out=ptb[:, j * 128:(j + 1) * 128], in_=o.t[:, j * 128:(j + 1) * 128],
                                            identity=self.identb.t[:, :]), [o, self.identb], [pt], inc=(j == 3))
                                    kt = ktl[oi % 2]
                                    S.I("act", lambda e, kt=kt, ptb=ptb: e.copy(
                                        out=kt.t[:, :, :], in_=ptb[:, 0:512].rearrange("p (j d) -> p j d", j=4)), [pt], [kt])
                                    self.st(self.kTok_d[tok0 + h * 512:tok0 + (h + 1) * 512, dkc * 128:(dkc + 1) * 128]
                                            .rearrange("(j p) d -> p j d", p=128), kt.t[:, :, :], kt)
                S.barrier()
            with ExitStack() as es2:
                ctl = [self.sb(es2, f"m1_ct{i}", [128, 512], F32) for i in range(4)]
                sgt = [self.sb(es2, f"m1_sg{i}", [128, 512], F32) for i in range(2)]
                gtl = [self.sb(es2, f"m1_gt{i}", [128, 512], BF16) for i in range(4)]
                ci = 0
                for s8 in range(8):
                    UA, UAb = self.wload("mix_w_in", l, 0, 16, C_UA + s8 * 256, 256)
                    UG, UGb = self.wload("mix_w_in", l, 0, 16, C_UG + s8 * 256, 256)
                    for c in range(2):
                        ch = 2 * s8 + c
                        pa = self.fm_out(UA, UAb, c, hT, NH)
                        pg = self.fm_out(UG, UGb, c, hT, NH)
                        for h in range(NH):
                            sg, ct = sgt[ci % 2], ctl[ci % 4]
                            ci += 1
                            S.I("act", lambda e, sg=sg, p=pg[h]: e.activation(out=sg.t[:, :], in_=p.t[:, :], func=AF.Sigmoid),
                                [pg[h]], [sg])
                            S.I("dve", lambda e, sg=sg, ct=ct, p=pa[h]: e.tensor_tensor(out=ct.t[:, :], in0=sg.t[:, :], in1=p.t[:, :],
                                                                                          op=ALU.mult), [sg, pa[h]], [ct])
                            self.st(self.cT_d[ch * 128:(ch + 1) * 128, HALO + tok0 + h * 512:HALO + tok0 + (h + 1) * 512],
                                    ct.t[:, :], ct)
                bgb = self.vec("branch_gate_b", l)
                for gi_, (cb_, dst) in enumerate(((C_G0, self.g0T_d), (C_G1, self.g1T_d))):
                    for s8 in range(8):
                        W, Wb = self.wload("mix_w_in", l, 0, 16, cb_ + s8 * 256, 256)
                        for c in range(2):
                            ch = 2 * s8 + c
                            pbs = self.fm_out(W, Wb, c, hT, NH)
                            for h in range(NH):
                                gt = gtl[ci % 4]
                                ci += 1
                                S.I("act", lambda e, gt=gt, p=pbs[h], ch=ch, gi_=gi_: e.activation(
                                    out=gt.t[:, :], in_=p.t[:, :], func=AF.Sigmoid, bias=bgb[:, gi_ * 16 + ch:gi_ * 16 + ch + 1],
                                    scale=1.0), [pbs[h], self.vecs], [gt])
                                self.st(dst[ch * 128:(ch + 1) * 128, tok0 + h * 512:tok0 + (h + 1) * 512], gt.t[:, :], gt)
                S.barrier()
            S.barrier()

    def stage_pass1(self, l):
        S = self.S
        self.set_pools(a=range(8))
        with ExitStack() as es:
            U = self.sb(es, "p1_U", [128, 8, 512], F32)
            dec = self.sb(es, "p1_dec", [128, 8, NCHUNK], F32)
            ktl = [self.sb(es, f"p1_k{i}", [64, 1024], BF16) for i in range(3)]
            vtl = [self.sb(es, f"p1_v{i}", [64, D], BF16) for i in range(3)]
            dtot = self.sb(es, "p1_dt", [128, 8], F32)
            tail = self.sb(es, "p1_tail", [128, 16, 30], F32)
            self.ld(dec.t[:, :, :], self.dec_d, dec)
            for c in range(NCHUNK):
                kt, vt = ktl[c % 3], vtl[c % 3]
                self.ld(kt.t[:, :], self.kTok_d[c * 64:(c + 1) * 64, :], kt)
                self.ld(vt.t[:, :], self.vTok_d[c * 64:(c + 1) * 64, :], vt)
                for i in range(8):
                    hd, half = i // 2, i % 2
                    pb = self.pbank("a")
                    S.I("pe", lambda e, kt=kt, vt=vt, hd=hd, half=half, pb=pb: e.matmul(
                        out=pb.t[:, :], lhsT=kt.t[0:64, hd * 256 + half * 128:hd * 256 + (half + 1) * 128],
                        rhs=vt.t[0:64, hd * 512:(hd + 1) * 512], start=True, stop=True), [kt, vt], [pb])
                    if c == 0:
                        S.I("dve", lambda e, i=i, pb=pb: e.tensor_copy(out=U.t[:, i, :], in_=pb.t[:, :]), [pb], [U.s(i)])
                    else:
                        S.I("dve", lambda e, i=i, pb=pb, c=c: e.scalar_tensor_tensor(
                            out=U.t[:, i, :], in0=U.t[:, i, :], scalar=dec.t[:, i, c - 1:c], in1=pb.t[:, :],
                            op0=ALU.mult, op1=ALU.add), [pb, dec], [U.s(i)])
            for i in range(8):
                S.I("dve", lambda e, i=i: e.tensor_scalar(out=U.t[:, i, :], in0=U.t[:, i, :], scalar1=dec.t[:, i, NCHUNK - 1:NCHUNK],
                                                          scalar2=None, op0=ALU.mult), [dec], [U.s(i)])
            for q in range(4):
                self.st(self.exS_in[l][q].rearrange("(i p) e -> p i e", p=128), U.t[:, 2 * q:2 * q + 2, :], U)
            S.I("dve", lambda e: e.tensor_reduce(out=dtot.t[:, :], in_=dec.t[:, :, :], axis=AXX, op=ALU.mult), [dec], [dtot])
            self.st(self.exM_in[l][D:D + 128, 0:8], dtot.t[:, :], dtot)
            self.ld(tail.t[:, :, :], self.cT_d[:, HALO + T - 30:HALO + T].rearrange("(k p) c -> p k c", p=128), tail)
            self.st(self.exM_in[l][0:D, 0:30].rearrange("(k p) c -> p k c", p=128), tail.t[:, :, :], tail)
            S.barrier()

    def _exchange_zero(self):
        S = self.S
        with ExitStack() as es:
            Sin = self.sb(es, "ex_S", [128, 8, 512], F32)
            hl = self.sb(es, "ex_h", [128, 16, 30], F32)
            S.I("dve", lambda e: e.memset(Sin.t[:, :, :], 0.0), [], [Sin])
            S.I("dve", lambda e: e.memset(hl.t[:, :, :], 0.0), [], [hl])
            self.st(self.st_d, Sin.t[:, :, :], Sin)
            self.st(self.cT_d[:, 2:32].rearrange("(k q) c -> q k c", q=128), hl.t[:, :, :], hl)
            S.barrier()

    def stage_exchange(self, l):
        S = self.S
        groups = [[0, 1, 2, 3], [4, 5, 6, 7]]
        noexch = bool(self.debug.get("noexch"))
        if noexch:
            return self._exchange_zero()
        S.waits("pool", {k: v for k, v in S.cnt.items() if v > 0 and (k in ("dve", "act", "pe", "sp") or k.startswith("st"))})
        pairs = [(self.exS_in[l][q], self.exS_out[l][q]) for q in range(4)] + [(self.exM_in[l], self.exM_out[l])]
        for src, dst in pairs:
            S.X("pool", "cc", 1, lambda e, src=src, dst=dst: e.collective_compute(
                "AllGather", ALU.bypass, replica_groups=groups, ins=[src], outs=[dst]))
        ccv = {"cc": S.cnt["cc"]}
        for eng in ("sp", "dve", "act", "pe"):
            S.waits(eng, ccv)
        R = D + 128
        rmask, rmaskc, roh = self.vec("rmask"), self.vec("rmaskc"), self.vec("ronehot")
        with ExitStack() as es:
            Dp = self.sb(es, "ex_D", [128, 4, 8], F32)
            av = self.sb(es, "ex_a", [128, 4, 8], F32)
            Sin = self.sb(es, "ex_S", [128, 8, 512], F32)
            tl = [self.sb(es, f"ex_t{i}", [128, 8, 512], F32) for i in range(2)]
            hl = self.sb(es, "ex_h", [128, 16, 30], F32)
            ht = [self.sb(es, f"ex_ht{i}", [128, 16, 30], F32) for i in range(2)]
            for p in range(3):
                self.ld(Dp.t[:, p, :], self.exM_out[l][p * R + D:p * R + D + 128, 0:8], Dp)
                S.I("dve", lambda e, p=p: e.tensor_scalar(out=av.t[:, p, :], in0=Dp.t[:, p, :], scalar1=rmask[:, p:p + 1],
                                                          scalar2=rmaskc[:, p:p + 1], op0=ALU.mult, op1=ALU.add),
                    [Dp, self.vecs], [av])
            S.I("dve", lambda e: e.memset(Sin.t[:, :, :], 0.0), [], [Sin])
            S.I("dve", lambda e: e.memset(hl.t[:, :, :], 0.0), [], [hl])
            for p in range(3):
                t = tl[p % 2]
                for q in range(4):
                    self.ld(t.t[:, 2 * q:2 * q + 2, :], self.exS_out[l][q][p * 256:(p + 1) * 256, :].rearrange("(i q) e -> q i e", q=128), t)
                for i in range(8):
                    S.I("dve", lambda e, t=t, i=i, p=p: e.tensor_scalar(out=t.t[:, i, :], in0=t.t[:, i, :], scalar1=rmask[:, p:p + 1],
                                                                        scalar2=None, op0=ALU.mult), [self.vecs], [t.s(i)])
                    S.I("dve", lambda e, t=t, i=i, p=p: e.scalar_tensor_tensor(
                        out=Sin.t[:, i, :], in0=Sin.t[:, i, :], scalar=av.t[:, p, i:i + 1], in1=t.t[:, i, :],
                        op0=ALU.mult, op1=ALU.add), [t.s(i), av], [Sin.s(i)])
                h = ht[p % 2]
                self.ld(h.t[:, :, :], self.exM_out[l][p * R:p * R + D, 0:30].rearrange("(k q) c -> q k c", q=128), h)
                S.I("dve", lambda e, h=h, p=p: e.scalar_tensor_tensor(out=hl.t[:, :, :], in0=h.t[:, :, :], scalar=roh[:, p:p + 1],
                                                                      in1=hl.t[:, :, :], op0=ALU.mult, op1=ALU.add),
                    [h, self.vecs], [hl])
            self.st(self.st_d, Sin.t[:, :, :], Sin)
            self.st(self.cT_d[:, 2:32].rearrange("(k q) c -> q k c", q=128), hl.t[:, :, :], hl)
            S.barrier()


    def stage_mix2(self, l, blk):
        S = self.S
        tok0 = blk * TB
        c0 = blk * (TB // 64)
        with ExitStack() as es:
            ogT = self.sb(es, "m2_og", [128, 16, TB], BF16)
            with ExitStack() as es2:
                U = self.sb(es2, "m2_U", [128, 8, 512], F32)
                Sbf = self.sb(es2, "m2_Sbf", [128, 8, 512], BF16)
                dec = self.sb(es2, "m2_dec", [128, 8, NCHUNK], F32)
                qpl = [self.sb(es2, f"m2_q{i}", [128, 8, 128], BF16) for i in range(2)]
                kpl = [self.sb(es2, f"m2_k{i}", [128, 8, 128], BF16) for i in range(2)]
                ktl = [self.sb(es2, f"m2_kt{i}", [64, 2, 1024], BF16) for i in range(2)]
                vtl = [self.sb(es2, f"m2_vt{i}", [64, 2, D], BF16) for i in range(2)]
                sgl = [self.sb(es2, f"m2_sg{i}", [128, D], BF16) for i in range(2)]
                onl = [self.sb(es2, f"m2_on{i}", [128, D], BF16) for i in range(2)]
                scl = [self.sb(es2, f"m2_sc{i}", [64, 4, 64], BF16) for i in range(4)]
                junk = self.sb(es2, "m2_junk", [128, 512], BF16)
                ssl = [self.sb(es2, f"m2_ss{i}", [128, 4], F32) for i in range(2)]
                self.set_pools(s=[4], kv=[5, 6, 7], t=[4])
                po = [self.ps[hd] for hd in range(4)]
                self.ld(dec.t[:, :, :], self.dec_d, dec)
                self.ld(U.t[:, :, :], self.st_d, U)
                for i in range(8):
                    if blk == 0:
                        S.I("act", lambda e, i=i: e.copy(out=Sbf.t[:, i, :], in_=U.t[:, i, :]), [U], [Sbf.s(i)])
                    else:
                        S.I("act", lambda e, i=i: e.activation(out=Sbf.t[:, i, :], in_=U.t[:, i, :], func=AF.Identity,
                                                               scale=dec.t[:, i, c0 - 1:c0]), [U, dec], [Sbf.s(i)])
                sci = 0
                for pr in range(TB // 128):
                    qp, kp, kt, vt, sg, on, ss = qpl[pr % 2], kpl[pr % 2], ktl[pr % 2], vtl[pr % 2], sgl[pr % 2], onl[pr % 2], ssl[pr % 2]
                    t0 = tok0 + pr * 128
                    self.ld(qp.t[:, :, :], self.qT_d[:, t0:t0 + 128].rearrange("(k p) t -> p k t", p=128), qp)
                    self.ld(kp.t[:, :, :], self.kT_d[:, t0:t0 + 128].rearrange("(k p) t -> p k t", p=128), kp)
                    self.ld(kt.t[:, :, :], self.kTok_d[t0:t0 + 128, :].rearrange("(c p) d -> p c d", p=64), kt)
                    self.ld(vt.t[:, :, :], self.vTok_d[t0:t0 + 128, :].rearrange("(c p) d -> p c d", p=64), vt)
                    self.ld(sg.t[:, :], self.sgn_d[t0:t0 + 128, :], sg)
                    for ci in range(2):
                        c = c0 + 2 * pr + ci
                        lo = ci * 64
                        psb = self.pbank("s")
                        for hd in range(4):
                            for half in range(2):
                                dkc = 2 * hd + half
                                S.I("pe", lambda e, psb=psb, kp=kp, qp=qp, dkc=dkc, lo=lo, half=half, hd=hd: e.matmul(
                                    out=psb.t[0:64, hd * 64:(hd + 1) * 64], lhsT=kp.t[:, dkc, lo:lo + 64], rhs=qp.t[:, dkc, lo:lo + 64],
                                    start=(half == 0), stop=(half == 1)), [kp, qp], [psb], inc=(half == 1))
                        sc = scl[sci % 4]
                        sci += 1
                        S.I("dve", lambda e, sc=sc, psb=psb: e.tensor_tensor(
                            out=sc.t[:, :, :], in0=psb.t[0:64, 0:256].rearrange("p (h i) -> p h i", h=4),
                            in1=self.mask64.unsqueeze(1).to_broadcast([64, 4, 64]), op=ALU.mult), [psb, self.cst], [sc])
                        for hd in range(4):
                            pob = po[hd]
                            for half in range(2):
                                dkc = 2 * hd + half
                                S.I("pe", lambda e, pob=pob, qp=qp, dkc=dkc, lo=lo, half=half: e.matmul(
                                    out=pob.t[lo:lo + 64, :], lhsT=qp.t[:, dkc, lo:lo + 64], rhs=Sbf.t[:, dkc, :],
                                    start=(half == 0), stop=False), [qp, Sbf.s(dkc)], [pob], inc=False)
                            S.I("pe", lambda e, pob=pob, sc=sc, vt=vt, ci=ci, hd=hd, lo=lo: e.matmul(
                                out=pob.t[lo:lo + 64, :], lhsT=sc.t[:, hd, :], rhs=vt.t[0:64, ci, hd * 512:(hd + 1) * 512],
                                start=False, stop=True), [sc, vt], [pob], inc=True)
                        for hd in range(4):
                            for half in range(2):
                                i = 2 * hd + half
                                pkv = self.pbank("kv")
                                S.I("pe", lambda e, pkv=pkv, kt=kt, vt=vt, ci=ci, hd=hd, half=half: e.matmul(
                                    out=pkv.t[:, :], lhsT=kt.t[0:64, ci, hd * 256 + half * 128:hd * 256 + (half + 1) * 128],
                                    rhs=vt.t[0:64, ci, hd * 512:(hd + 1) * 512], start=True, stop=True), [kt, vt], [pkv])
                                if c == 0:
                                    S.I("dve", lambda e, i=i, pkv=pkv: e.tensor_tensor(out=U.t[:, i, :], in0=U.t[:, i, :], in1=pkv.t[:, :],
                                                                                       op=ALU.add), [pkv], [U.s(i)])
                                else:
                                    S.I("dve", lambda e, i=i, pkv=pkv, c=c: e.scalar_tensor_tensor(
                                        out=U.t[:, i, :], in0=U.t[:, i, :], scalar=dec.t[:, i, c - 1:c], in1=pkv.t[:, :],
                                        op0=ALU.mult, op1=ALU.add), [pkv, dec], [U.s(i)])
                                S.I("act", lambda e, i=i, c=c: e.activation(out=Sbf.t[:, i, :], in_=U.t[:, i, :], func=AF.Identity,
                                                                            scale=dec.t[:, i, c:c + 1]), [U.s(i), dec], [Sbf.s(i)])
                    for hd in range(4):
                        S.I("act", lambda e, hd=hd, ss=ss: e.activation(out=junk.t[:, :], in_=po[hd].t[:, :], func=AF.Square,
                                                                        accum_out=ss.t[:, hd:hd + 1]), [po[hd]], [junk, ss])
                    S.I("dve", lambda e, ss=ss: e.tensor_scalar(out=ss.t[:, :], in0=ss.t[:, :], scalar1=1.0 / 512, scalar2=EPS,
                                                                op0=ALU.mult, op1=ALU.add), [], [ss])
                    S.I("act", lambda e, ss=ss: e.activation(out=ss.t[:, :], in_=ss.t[:, :], func=AF.Sqrt), [], [ss])
                    S.I("dve", lambda e, ss=ss: e.reciprocal(out=ss.t[:, :], in_=ss.t[:, :]), [], [ss])
                    for hd in range(4):
                        S.I("dve", lambda e, hd=hd, ss=ss, on=on, sg=sg: e.scalar_tensor_tensor(
                            out=on.t[:, hd * 512:(hd + 1) * 512], in0=po[hd].t[:, :], scalar=ss.t[:, hd:hd + 1],
                            in1=sg.t[:, hd * 512:(hd + 1) * 512], op0=ALU.mult, op1=ALU.mult), [po[hd], ss, sg], [on.s(hd)])
                    for e4 in range(4):
                        pt = self.pbank("t")
                        ptb = self.bview(pt)
                        for q in range(4):
                            ec = e4 * 4 + q
                            S.I("pe", lambda e, ptb=ptb, on=on, q=q, ec=ec: e.transpose(
                                out=ptb[:, q * 128:(q + 1) * 128], in_=on.t[:, ec * 128:(ec + 1) * 128], identity=self.identb.t[:, :]),
                                [on.s(e4), self.identb], [pt], inc=(q == 3))
                        S.I("act", lambda e, ptb=ptb, e4=e4, pr=pr: e.copy(
                            out=ogT.t[:, e4 * 4:(e4 + 1) * 4, pr * 128:(pr + 1) * 128],
                            in_=ptb[:, 0:512].rearrange("p (q t) -> p q t", q=4)), [pt], [ogT])
                self.st(self.st_d, U.t[:, :, :], U)
                if self.debug.get("dump"):
                    self.st(self.dbg_og[:, tok0:tok0 + TB].rearrange("(k p) t -> p k t", p=128), ogT.t[:, :, :], ogT)
                S.barrier()
            cw, cb = self.vec("conv_w", l), self.vec("conv_b", l)
            lng, lnb = self.vec("conv_ln_g", l), self.vec("conv_ln_b", l)
            for h in range(NH):
                t0 = tok0 + h * 512
                with ExitStack() as es3:
                    cn = self.sb(es3, "m2_cn", [128, 16, 512], BF16)
                    with ExitStack() as es4:
                        co = self.sb(es4, "m2_co", [128, 16, 512], F32)
                        cin = [self.sb(es4, f"m2_ci{i}", [128, 30 + 512], F32) for i in range(4)]
                        cbf = [self.sb(es4, f"m2_cb{i}", [128, 512], BF16) for i in range(2)]
                        csq = [self.sb(es4, f"m2_cq{i}", [128, 512], BF16) for i in range(2)]
                        mt = self.sb(es4, "m2_mt", [128, 512], F32)
                        vr = self.sb(es4, "m2_vr", [128, 512], F32)
                        self.set_pools(st=[0, 1], c=range(2, 8))
                        ps_s, ps_q = self.pbank("st"), self.pbank("st")
                        cbl = [self.sb(es4, f"m2_cbh{i}", [128, 30 + 512], BF16) for i in range(2)]
                        dgl = [self.sb(es4, f"m2_dg{i}", [128, 31, 128], BF16) for i in range(2)]
                        for k in range(16):
                            ci_, cbh, dg = cin[k % 4], cbl[k % 2], dgl[k % 2]
                            self.ld(ci_.t[:, :], self.cT_d[k * 128:(k + 1) * 128, HALO + t0 - 30:HALO + t0 + 512], ci_)
                            S.I("dve", lambda e, ci_=ci_, cbh=cbh: e.tensor_copy(out=cbh.t[:, :], in_=ci_.t[:, :]), [ci_], [cbh])
                            for tap in range(31):
                                if tap % 2:
                                    S.I("act", lambda e, dg=dg, tap=tap, k=k: e.activation(
                                        out=dg.t[:, tap, :], in_=self.identb.t[:, :], func=AF.Copy,
                                        scale=cw[:, k * 31 + tap:k * 31 + tap + 1]), [self.identb, self.vecs], [dg.s(tap)])
                                else:
                                    S.I("dve", lambda e, dg=dg, tap=tap, k=k: e.tensor_scalar(
                                        out=dg.t[:, tap, :], in0=self.identb.t[:, :], scalar1=cw[:, k * 31 + tap:k * 31 + tap + 1],
                                        scalar2=None, op0=ALU.mult), [self.identb, self.vecs], [dg.s(tap)])
                            pc = self.pbank("c")
                            for tap in range(31):
                                S.I("pe", lambda e, pc=pc, dg=dg, cbh=cbh, tap=tap: e.matmul(
                                    out=pc.t[:, :], lhsT=dg.t[:, tap, :], rhs=cbh.t[:, tap:tap + 512],
                                    start=(tap == 0), stop=(tap == 30)), [dg.s(tap), cbh], [pc], inc=(tap == 30))
                            S.I("act", lambda e, pc=pc, k=k: e.activation(out=co.t[:, k, :], in_=pc.t[:, :], func=AF.Identity,
                                                                          bias=cb[:, k:k + 1], scale=1.0), [pc, self.vecs], [co.s(k)])
                            b1, b2 = cbf[k % 2], csq[k % 2]
                            S.I("dve", lambda e, k=k, b1=b1: e.tensor_copy(out=b1.t[:, :], in_=co.t[:, k, :]), [co.s(k)], [b1])
                            S.I("act", lambda e, k=k, b2=b2: e.activation(out=b2.t[:, :], in_=co.t[:, k, :], func=AF.Square),
                                [co.s(k)], [b2])
                            S.I("pe", lambda e, k=k, b1=b1: e.matmul(out=ps_s.t[:, :], lhsT=self.onesb.t[:, :], rhs=b1.t[:, :],
                                                                     start=(k == 0), stop=(k == 15)), [b1, self.onesb], [ps_s])
                            S.I("pe", lambda e, k=k, b2=b2: e.matmul(out=ps_q.t[:, :], lhsT=self.onesb.t[:, :], rhs=b2.t[:, :],
                                                                     start=(k == 0), stop=(k == 15)), [b2, self.onesb], [ps_q])
                        S.I("act", lambda e: e.activation(out=mt.t[:, :], in_=ps_s.t[:, :], func=AF.Copy, scale=1.0 / D), [ps_s], [mt])
                        S.I("dve", lambda e: e.tensor_tensor(out=vr.t[:, :], in0=mt.t[:, :], in1=mt.t[:, :], op=ALU.mult), [mt], [vr])
                        S.I("dve", lambda e: e.scalar_tensor_tensor(out=vr.t[:, :], in0=ps_q.t[:, :], scalar=1.0 / D, in1=vr.t[:, :],
                                                                    op0=ALU.mult, op1=ALU.subtract), [ps_q], [vr])
                        S.I("act", lambda e: e.activation(out=vr.t[:, :], in_=vr.t[:, :], func=AF.Sqrt, bias=self.vec("eps"), scale=1.0),
                            [self.vecs], [vr])
                        S.I("dve", lambda e: e.reciprocal(out=vr.t[:, :], in_=vr.t[:, :]), [], [vr])
                        for k in range(16):
                            S.I("dve", lambda e, k=k: e.tensor_tensor(out=co.t[:, k, :], in0=co.t[:, k, :], in1=mt.t[:, :],
                                                                      op=ALU.subtract), [mt], [co.s(k)])
                            S.I("dve", lambda e, k=k: e.tensor_tensor(out=co.t[:, k, :], in0=co.t[:, k, :], in1=vr.t[:, :],
                                                                      op=ALU.mult), [vr], [co.s(k)])
                            S.I("act", lambda e, k=k: e.activation(out=cn.t[:, k, :], in_=co.t[:, k, :], func=AF.Silu,
                                                                   scale=lng[:, k:k + 1], bias=lnb[:, k:k + 1]),
                                [co.s(k), self.vecs], [cn.s(k)])
                        S.barrier()
                    m = self.sb(es3, "m2_m", [128, 16, 512], BF16)
                    xt = self.sb(es3, "m2_x", [128, 16, 512], F32)
                    g0l = [self.sb(es3, f"m2_g0{i}", [128, 512], BF16) for i in range(2)]
                    g1l = [self.sb(es3, f"m2_g1{i}", [128, 512], BF16) for i in range(2)]
                    t1l = [self.sb(es3, f"m2_t1{i}", [128, 512], F32) for i in range(2)]
                    t2l = [self.sb(es3, f"m2_t2{i}", [128, 512], F32) for i in range(2)]
                    self.set_pools(a=range(8))
                    self.load_x(xt, blk, h, 1)
                    for s8 in range(8):
                        GP, GPb = self.wload("gla_proj", l, 0, 16, s8 * 256, 256)
                        CP, CPb = self.wload("conv_proj", l, 0, 16, s8 * 256, 256)
                        for c in range(2):
                            j = 2 * s8 + c
                            pa = self.fm_out(GP, GPb, c, ogT, 1, hoff=h)[0]
                            pb = self.fm_out(CP, CPb, c, cn, 1)[0]
                            g0, g1, t1, t2 = g0l[j % 2], g1l[j % 2], t1l[j % 2], t2l[j % 2]
                            self.ld(g0.t[:, :], self.g0T_d[j * 128:(j + 1) * 128, t0:t0 + 512], g0)
                            self.ld(g1.t[:, :], self.g1T_d[j * 128:(j + 1) * 128, t0:t0 + 512], g1)
                            S.I("dve", lambda e, t1=t1, pa=pa, g0=g0: e.tensor_tensor(out=t1.t[:, :], in0=pa.t[:, :], in1=g0.t[:, :],
                                                                                      op=ALU.mult), [pa, g0], [t1])
                            S.I("dve", lambda e, t2=t2, pb=pb, g1=g1: e.tensor_tensor(out=t2.t[:, :], in0=pb.t[:, :], in1=g1.t[:, :],
                                                                                      op=ALU.mult), [pb, g1], [t2])
                            S.I("dve", lambda e, t1=t1, t2=t2, j=j: e.tensor_tensor(out=m.t[:, j, :], in0=t1.t[:, :], in1=t2.t[:, :],
                                                                                    op=ALU.add), [t1, t2], [m.s(j, 0)])
                    if self.debug.get("dump"):
                        self.st(self.dbg_cn[:, t0:t0 + 512].rearrange("(k p) t -> p k t", p=128), cn.t[:, :, :], cn)
                        self.st(self.dbg_m[:, t0:t0 + 512].rearrange("(k p) t -> p k t", p=128), m.t[:, :, :], m)
                    for s8 in range(8):
                        WO, WOb = self.wload("mix_w_out", l, 0, 16, s8 * 256, 256)
                        for c in range(2):
                            j = 2 * s8 + c
                            py = self.fm_out(WO, WOb, c, m, 1)[0]
                            S.I("dve", lambda e, j=j, py=py: e.tensor_tensor(out=xt.t[:, j, :], in0=py.t[:, :], in1=xt.t[:, j, :],
                                                                             op=ALU.add), [py], [xt.s(j, 0)])
                    self.store_x(xt, blk, h, 1)
                    S.barrier()
            S.barrier()

    def stage_xattn(self, l, blk):
        S = self.S
        for h in range(NH):
            with ExitStack() as es:
                xt = self.sb(es, "xa_x", [128, 16, 512], F32)
                hT = self.sb(es, "xa_h", [128, 16, 512], BF16)
                sq = [self.sb(es, f"xa_sq{i}", [128, 512], BF16) for i in range(4)]
                rs = self.sb(es, "xa_rs", [128, 512], F32)
                qT = self.sb(es, "xa_q", [128, 4, 512], BF16)
                pTl = [self.sb(es, f"xa_pT{i}", [128, 2, 512], BF16) for i in range(2)]
                oT = self.sb(es, "xa_o", [128, 4, 512], BF16)
                pel = [self.sb(es, f"xa_pe{i}", [128, 256], F32) for i in range(2)]
                pnl = [self.sb(es, f"xa_pn{i}", [128, 256], BF16) for i in range(2)]
                sml = [self.sb(es, f"xa_sm{i}", [128, 4], F32) for i in range(4)]
                self.set_pools(a=range(0, 4), s=range(4, 6), t=range(6, 8))
                self.load_x(xt, blk, h, 1)
                self.make_hT(xt, hT, self.vec("xa_norm", l), 1, sq, rs)
                for s2 in range(2):
                    W, Wb = self.wload("xa_w_q", l, 0, 16, s2 * 256, 256)
                    for c in range(2):
                        hd = 2 * s2 + c
                        pb = self.fm_out(W, Wb, c, hT, 1)[0]
                        S.I("act", lambda e, pb=pb, hd=hd: e.activation(out=qT.t[:, hd, :], in_=pb.t[:, :], func=AF.Copy,
                                                                        scale=128 ** -0.5), [pb], [qT.s(hd)])
                it = 0
                for hd in range(4):
                    pT = pTl[hd % 2]
                    for j in range(4):
                        psb = self.pbank("s")
                        pe_, pn, sm = pel[it % 2], pnl[it % 2], sml[it % 4]
                        it += 1
                        S.I("pe", lambda e, psb=psb, hd=hd, j=j: e.matmul(out=psb.t[:, 0:256], lhsT=qT.t[:, hd, j * 128:(j + 1) * 128],
                                                                          rhs=self.kmT[l].t[:, hd, :], start=True, stop=True),
                            [qT.s(hd), self.kmT[l]], [psb])
                        S.I("dve", lambda e, psb=psb, sm=sm: e.tensor_reduce(out=sm.t[:, 0:1], in_=psb.t[:, 0:256], axis=AXX, op=ALU.max,
                                                                             negate=True), [psb], [sm])
                        S.I("act", lambda e, psb=psb, sm=sm, pe_=pe_: e.activation(out=pe_.t[:, :], in_=psb.t[:, 0:256], func=AF.Exp,
                                                                                   bias=sm.t[:, 0:1], scale=1.0, accum_out=sm.t[:, 1:2]),
                            [psb, sm], [pe_, sm])
                        S.I("dve", lambda e, sm=sm: e.reciprocal(out=sm.t[:, 2:3], in_=sm.t[:, 1:2]), [], [sm])
                        S.I("dve", lambda e, sm=sm, pe_=pe_, pn=pn: e.tensor_scalar(out=pn.t[:, :], in0=pe_.t[:, :], scalar1=sm.t[:, 2:3],
                                                                                    scalar2=None, op0=ALU.mult), [pe_, sm], [pn])
                        pt = self.pbank("t")
                        ptb = self.bview(pt)
                        for mc in range(2):
                            S.I("pe", lambda e, ptb=ptb, pn=pn, mc=mc: e.transpose(out=ptb[:, mc * 128:(mc + 1) * 128],
                                                                                   in_=pn.t[:, mc * 128:(mc + 1) * 128],
                                                                                   identity=self.identb.t[:, :]),
                                [pn, self.identb], [pt], inc=(mc == 1))
                        S.I("act", lambda e, ptb=ptb, pT=pT, j=j: e.copy(out=pT.t[:, :, j * 128:(j + 1) * 128],
                                                                         in_=ptb[:, 0:256].rearrange("p (m t) -> p m t", m=2)),
                            [pt], [pT])
                    po = self.pbank("a")
                    for mc in range(2):
                        S.I("pe", lambda e, po=po, mc=mc, hd=hd, pT=pT: e.matmul(out=po.t[:, :], lhsT=self.vm[l].t[:, mc, hd * 128:(hd + 1) * 128],
                                                                                 rhs=pT.t[:, mc, :], start=(mc == 0), stop=(mc == 1)),
                            [self.vm[l], pT], [po], inc=(mc == 1))
                    S.I("dve", lambda e, po=po, hd=hd: e.tensor_copy(out=oT.t[:, hd, :], in_=po.t[:, :]), [po], [oT.s(hd)])
                for s2 in range(2):
                    W, Wb = self.wload("xa_w_out", l, 0, 4, s2 * 1024, 1024)
                    for jj in range(8):
                        j = s2 * 8 + jj
                        py = self.pbank("a")
                        for kc in range(4):
                            S.I("pe", lambda e, py=py, W=W, kc=kc, jj=jj: e.matmul(out=py.t[:, :], lhsT=W[:, kc, jj * 128:(jj + 1) * 128],
                                                                                   rhs=oT.t[:, kc, :], start=(kc == 0), stop=(kc == 3)),
                                [Wb, oT.s(kc)], [py], inc=(kc == 3))
                        S.I("dve", lambda e, j=j, py=py: e.tensor_tensor(out=xt.t[:, j, :], in0=py.t[:, :], in1=xt.t[:, j, :], op=ALU.add),
                            [py], [xt.s(j, 0)])
                self.store_x(xt, blk, h, 1)
                S.barrier()

    def stage_ffn(self, l, blk, which):
        S = self.S
        wi, wo, gn = which + "_w_in", which + "_w_out", which + "_norm"
        with ExitStack() as es:
            xt = self.sb(es, "ffn_x", [128, 16, TB], F32)
            hT = self.sb(es, "ffn_h", [128, 16, TB], BF16)
            sq = [self.sb(es, f"ffn_sq{i}", [128, 512], BF16) for i in range(4)]
            rs = self.sb(es, "ffn_rs", [128, 512], F32)
            act = [self.sb(es, f"ffn_act{i}", [128, 4, TB], BF16) for i in range(2)]
            tmp = [self.sb(es, f"ffn_tmp{i}", [128, 512], F32) for i in range(2)]
            self.set_pools(a=range(0, 4), y=range(4, 8))
            self.load_x(xt, blk, 0, NH)
            self.make_hT(xt, hT, self.vec(gn, l), NH, sq, rs)
            ngrp = DFF // 512
            ti = 0
            for gi in range(ngrp):
                at = act[gi % 2]
                for sgi in range(2):
                    A, Ab = self.wload(wi, l, 0, 16, (2 * gi + sgi) * 256, 256)
                    B, Bb = self.wload(wi, l, 0, 16, DFF + (2 * gi + sgi) * 256, 256)
                    for c in range(2):
                        cc = 2 * sgi + c
                        pa = [self.pbank("a") for _ in range(NH)]
                        pb = [self.pbank("a") for _ in range(NH)]
                        for W, Wb, pp in ((A, Ab, pa), (B, Bb, pb)):
                            for k in range(16):
                                for h in range(NH):
                                    S.I("pe", lambda e, W=W, k=k, h=h, c=c, p=pp[h]: e.matmul(
                                        out=p.t[:, :], lhsT=W[:, k, c * 128:(c + 1) * 128], rhs=hT.t[:, k, h * 512:(h + 1) * 512],
                                        start=(k == 0), stop=(k == 15)), [Wb, hT.s(k, h)], [pp[h]], inc=(k == 15))
                        for h in range(NH):
                            tt = tmp[ti % 2]
                            ti += 1
                            S.I("act", lambda e, tt=tt, p=pa[h]: e.activation(out=tt.t[:, :], in_=p.t[:, :], func=AF.Silu),
                                [pa[h]], [tt])
                            S.I("dve", lambda e, tt=tt, p=pb[h], cc=cc, h=h, at=at: e.tensor_tensor(
                                out=at.t[:, cc, h * 512:(h + 1) * 512], in0=tt.t[:, :], in1=p.t[:, :], op=ALU.mult),
                                [tt, pb[h]], [at.s(cc, h)])
                Os = [self.wload(wo, l, (2 * gi + sgi) * 256, 2, 0, D) for sgi in range(2)]
                for j in range(16):
                    for h in range(NH):
                        py = self.pbank("y")
                        for kc in range(4):
                            O, Ob = Os[kc // 2]
                            S.I("pe", lambda e, O=O, kc=kc, j=j, h=h, py=py, at=at: e.matmul(
                                out=py.t[:, :], lhsT=O[:, kc % 2, j * 128:(j + 1) * 128], rhs=at.t[:, kc, h * 512:(h + 1) * 512],
                                start=(kc == 0), stop=(kc == 3)), [Ob, at.s(kc, h)], [py], inc=(kc == 3))
                        S.I("dve", lambda e, j=j, h=h, py=py: e.scalar_tensor_tensor(
                            out=xt.t[:, j, h * 512:(h + 1) * 512], in0=py.t[:, :], scalar=0.5,
                            in1=xt.t[:, j, h * 512:(h + 1) * 512], op0=ALU.mult, op1=ALU.add), [py], [xt.s(j, h)])
            self.store_x(xt, blk, 0, NH)
            S.barrier()

    def stage_final(self):
        S = self.S
        self.set_pools(a=range(8))
        with ExitStack() as es:
            xt = self.sb(es, "fin_x", [128, 16, 512], F32)
            sq = [self.sb(es, f"fin_sq{i}", [128, 512], BF16) for i in range(4)]
            rs = self.sb(es, "fin_rs", [128, 512], F32)
            yt = self.sb(es, "fin_y", [128, 16, 512], F32)
            ot = self.sb(es, "fin_o", [128, 4, D], F32)
            g = self.vec("final_norm")
            for gi in range(T // 512):
                S.D("sp", "dma_ld", xt.t[:, :, :], self.xT[:, gi * 512:(gi + 1) * 512].rearrange("(k p) t -> p k t", p=128),
                    reads=[self.dbuf(("xT", gi))], writes=[xt])
                self.rstd_bc(xt, 0, sq, rs)
                for k in range(16):
                    S.I("dve", lambda e, k=k: e.scalar_tensor_tensor(out=yt.t[:, k, :], in0=xt.t[:, k, :], scalar=g[:, k:k + 1],
                                                                     in1=rs.t[:, :], op0=ALU.mult, op1=ALU.mult),
                        [xt, rs, self.vecs], [yt])
                for j in range(4):
                    for kk in range(4):
                        pb = self.pbank("a")
                        for q in range(4):
                            k = kk * 4 + q
                            S.I("pe", lambda e, pb=pb, j=j, k=k, q=q: e.transpose(
                                out=pb.t[:, q * 128:(q + 1) * 128], in_=yt.t[:, k, j * 128:(j + 1) * 128], identity=self.ident),
                                [yt, self.cst], [pb], inc=(q == 3))
                        if kk % 2:
                            S.I("act", lambda e, pb=pb, j=j, kk=kk: e.copy(out=ot.t[:, j, kk * 512:(kk + 1) * 512], in_=pb.t[:, :]),
                                [pb], [ot])
                        else:
                            S.I("dve", lambda e, pb=pb, j=j, kk=kk: e.tensor_copy(out=ot.t[:, j, kk * 512:(kk + 1) * 512], in_=pb.t[:, :]),
                                [pb], [ot])
                S.D("sp", "dma_st", self.out_d[gi * 512:(gi + 1) * 512, :].rearrange("(j p) d -> p j d", p=128), ot.t[:, :, :],
                    reads=[ot], writes=[self.dbuf(("out", gi))])
            S.barrier()


VEC_ITEMS = [("ffn1_norm", 16), ("mix_norm", 16), ("xa_norm", 16), ("ffn2_norm", 16), ("xa_mem_norm", 16),
             ("conv_b", 16), ("conv_ln_g", 16), ("conv_ln_b", 16), ("gla_gate_b", 8), ("branch_gate_b", 32),
             ("conv_w", 16 * 31)]


def vec_layout():
    off = {}
    o = 0
    for l in range(L):
        for n, w in VEC_ITEMS:
            off[(n, l)] = (o, w)
            o += w
    for n, w in [("final_norm", 16), ("eps", 1), ("one", 1), ("rmask", 4), ("rmaskc", 4), ("ronehot", 4)]:
        off[(n, None)] = (o, w)
        o += w
    return off, o


def pm(v, k):
    return np.ascontiguousarray(np.asarray(v, np.float32).reshape(k, 128).T)


def build_vecs(inp, rank):
    off, n = vec_layout()
    out = np.zeros((128, n), np.float32)
    for l in range(L):
        for name, w in VEC_ITEMS:
            o, _ = off[(name, l)]
            if name == "conv_w":
                cw = np.asarray(inp["conv_w"][l], np.float32)
                out[:, o:o + w] = cw.reshape(31, 16, 128).transpose(2, 1, 0).reshape(128, 16 * 31)
            else:
                out[:, o:o + w] = pm(inp[name][l], w)
    o, _ = off[("final_norm", None)]
    out[:, o:o + 16] = pm(inp["final_norm"], 16)
    out[:, off[("eps", None)][0]] = EPS
    out[:, off[("one", None)][0]] = 1.0
    for p in range(4):
        out[:, off[("rmask", None)][0] + p] = 1.0 if p < rank else 0.0
        out[:, off[("rmaskc", None)][0] + p] = 0.0 if p < rank else 1.0
        out[:, off[("ronehot", None)][0] + p] = 1.0 if p == rank - 1 else 0.0
    return out


def build_cst():
    c = np.zeros((128, 128 + 64 + TB), np.float32)
    c[:, 0:128] = np.eye(128, dtype=np.float32)
    jj, ii = np.meshgrid(np.arange(64), np.arange(64), indexing="ij")
    c[0:64, 128:192] = (jj <= ii).astype(np.float32)
    cm = np.ones(TB, np.float32)
    cm[0::64] = 0.0
    c[:, 192:] = cm[None, :]
    return c


def pack_slabs(inp, specs):
    wall = np.zeros((max(1, len(specs)), 128, SLOT), np.float32)
    for i, (name, l, r0, nkc, c0, ncols) in enumerate(specs):
        w = inp[name][l]
        blk = np.asarray(w[r0:r0 + nkc * 128, c0:c0 + ncols], np.float32)
        wall[i, :, :nkc * ncols] = blk.reshape(nkc, 128, ncols).transpose(1, 0, 2).reshape(128, nkc * ncols)
    return wall


_CACHE = {}


def get_program(debug=None):
    key = repr(sorted((debug or {}).items()))
    if key not in _CACHE:
        plan = Prog(plan=None, debug=debug).build()
        prog = Prog(plan=plan, debug=debug)
        nc = prog.build()
        _CACHE[key] = (nc, plan)
    return _CACHE[key]


def kernel(debug=None, ncores=NCORE, **inp):
    nc, plan = get_program(debug)
    x = np.asarray(inp["x"], np.float32)
    mem = np.asarray(inp["mem"], np.float32)
    wall = pack_slabs(inp, plan["specs"])
    cst = build_cst()
    gnb = np.ascontiguousarray(np.broadcast_to(np.asarray(inp["gla_out_norm"], np.float32)[:, None, :], (L, 128, D)))
    gw2 = np.ascontiguousarray(np.asarray(inp["gla_gate_w2"], np.float32))
    in_maps = []
    for c in range(ncores):
        b, r = c // 4, c % 4
        in_maps.append({
            "x_in": np.ascontiguousarray(x[b, r * T:(r + 1) * T, :]),
            "mem_in": np.ascontiguousarray(mem[b]),
            "vecs": build_vecs(inp, r),
            "cst": cst, "gnb": gnb, "gw2": gw2, "wall": wall,
        })
    res = run_bass_kernel_spmd(nc, in_maps, core_ids=list(range(ncores)))
    if debug and debug.get("raw"):
        return res.results
    out = np.zeros((2, 8192, D), np.float32)
    for c in range(ncores):
        b, r = c // 4, c % 4
        out[b, r * T:(r + 1) * T, :] = np.asarray(res.results[c]["out"], np.float32)
    return out
```

```python
import numpy as np
import concourse.bass as bass
import concourse.mybir as mybir
from concourse.bass_utils import run_bass_kernel_spmd
from contextlib import ExitStack

F32 = mybir.dt.float32
BF16 = mybir.dt.bfloat16
AF = mybir.ActivationFunctionType
ALU = mybir.AluOpType
AXX = mybir.AxisListType.X

L = 2
D = 2048
DFF = 5632
DIN = 14352
NCORE = 8
T = 2048
TB = 1024
NH = TB // 512
NBLK = T // TB
NCHUNK = T // 64
HALO = 32
SLOT = 4096
NSLOT = 8
LOOKAHEAD = 6
EPS = 1e-6
C_Q, C_K, C_V, C_R, C_G, C_UA, C_UG, C_G0, C_G1 = 0, 1024, 2048, 4096, 4112, 6160, 8208, 10256, 12304


class Buf:
    __slots__ = ("name", "w", "r")

    def __init__(self, name=""):
        self.name = name
        self.w = None
        self.r = {}


class Tile:
    def __init__(self, t, name=""):
        self.t = t
        self.b = Buf(name)
        self.subs = {}

    def __getitem__(self, idx):
        return self.t[idx]

    def s(self, *key):
        if key not in self.subs:
            nb = Buf(f"{self.b.name}{key}")
            nb.w = self.b.w
            nb.r = dict(self.b.r)
            self.subs[key] = nb
        return self.subs[key]

    def all(self):
        return [self.b] + list(self.subs.values())


def flat(xs):
    out = []
    for x in xs:
        if isinstance(x, Tile):
            out.extend(x.all())
        elif isinstance(x, (list, tuple)):
            out.extend(flat(x))
        else:
            out.append(x)
    return out


class Sched:
    ENG = ("pe", "act", "dve", "pool", "sp")

    def __init__(self, sem_names, pools=None):
        self.cnt = {k: 0 for k in sem_names}
        self.pools = pools or {}
        self.pool_idx = {k: 0 for k in self.pools}
        self.known = {e: {} for e in self.ENG}
        self.stream = {e: [] for e in self.ENG}
        self.noinc = {e: False for e in self.ENG}

    def _deps(self, eng, reads, writes):
        deps = {}

        def add(k, v):
            if eng == "pe" and k == "pe":
                return
            if deps.get(k, 0) < v:
                deps[k] = v

        for b in reads:
            if b.w is not None:
                add(*b.w)
        for b in writes:
            if b.w is not None:
                add(*b.w)
            for k, v in b.r.items():
                add(k, v)
        return deps

    def waits(self, eng, deps):
        kn = self.known[eng]
        for k, v in deps.items():
            if kn.get(k, 0) < v:
                kn[k] = v
                self.stream[eng].append(("w", k, v))

    @staticmethod
    def _mark(tok, reads, writes):
        k, v = tok
        for b in reads:
            if b.r.get(k, 0) < v:
                b.r[k] = v
        for b in writes:
            b.w = tok
            b.r = {}

    def I(self, eng, fn, reads=(), writes=(), inc=True):
        reads = flat(reads)
        writes = flat(writes)
        self.waits(eng, self._deps(eng, reads, writes))
        if inc:
            self.cnt[eng] += 1
            v = self.cnt[eng]
            self.stream[eng].append(("i", fn, eng, 1))
            self.noinc[eng] = False
        else:
            v = self.cnt[eng] + 1
            self.stream[eng].append(("n", fn))
            self.noinc[eng] = True
        self._mark((eng, v), reads, writes)

    def X(self, eng, semk, amount, fn, reads=(), writes=()):
        reads = flat(reads)
        writes = flat(writes)
        self.waits(eng, self._deps(eng, reads, writes))
        self.cnt[semk] += amount
        self.stream[eng].append(("i", fn, semk, amount))
        self._mark((semk, self.cnt[semk]), reads, writes)
        return (semk, self.cnt[semk])

    def D(self, q, semk, out, in_, reads=(), writes=(), extra_deps=None, **kw):
        reads = flat(reads)
        writes = flat(writes)
        deps = self._deps(q, reads, writes)
        if semk in self.pools:
            pname = semk
            lst = self.pools[pname]
            semk = lst[self.pool_idx[pname] % len(lst)]
            self.pool_idx[pname] += 1
            if self.cnt[semk] > 0:
                deps[semk] = max(deps.get(semk, 0), self.cnt[semk])
        if extra_deps:
            for k, v in extra_deps.items():
                if deps.get(k, 0) < v:
                    deps[k] = v
        self.waits(q, deps)
        self.cnt[semk] += 16
        v = self.cnt[semk]
        self.stream[q].append(("i", lambda e: e.dma_start(out=out, in_=in_, **kw), semk, 16))
        self._mark((semk, v), reads, writes)
        return (semk, v)

    def barrier(self, engs=("pe", "act", "dve", "sp"), skip=("pool",), skip_w=True):
        toks = {k: v for k, v in self.cnt.items() if v > 0 and k not in skip and not (skip_w and k.startswith("wq"))}
        for e in engs:
            self.waits(e, toks)

    def replay(self, eng, e, sems):
        for it in self.stream[eng]:
            if it[0] == "w":
                e.wait_ge(sems[it[1]], it[2])
            elif it[0] == "i":
                it[1](e).then_inc(sems[it[2]], it[3])
            else:
                it[1](e)


class Prog:
    def __init__(self, plan=None, debug=None):
        self.plan = plan
        self.dry = plan is None
        self.debug = debug or {}
        self.slab_ids = {}
        self.slab_specs = []
        self.wseq = []
        self.wfree = []
        self.wtok = []
        self.w_emitted = 0
        self.w_req = 0

    def slab_id(self, spec):
        if spec not in self.slab_ids:
            self.slab_ids[spec] = len(self.slab_specs)
            self.slab_specs.append(spec)
        return self.slab_ids[spec]

    def _emit_wdma(self, n):
        sid = self.plan["seq"][n]
        slot = self.wslots[n % NSLOT]
        deps = self.plan["free"][n]
        self.S.D("pool", f"wq{n % NSLOT}", slot.t[:, :], self.wall[sid], extra_deps=deps)

    def wload(self, name, l, r0, nkc, c0, ncols):
        spec = (name, l, r0, nkc, c0, ncols)
        assert nkc * ncols <= SLOT
        sid = self.slab_id(spec)
        n = self.w_req
        self.w_req += 1
        slot = self.wslots[n % NSLOT]
        if self.dry:
            self.wseq.append(sid)
            free = {}
            if slot.b.w is not None:
                free[slot.b.w[0]] = slot.b.w[1]
            for k, v in slot.b.r.items():
                if free.get(k, 0) < v:
                    free[k] = v
            for k in list(free):
                if k.startswith("wq"):
                    free.pop(k)
            self.wfree.append(free)
            self.S.cnt[f"wq{n % NSLOT}"] += 16
        else:
            assert self.plan["seq"][n] == sid
            while self.w_emitted < min(len(self.plan["seq"]), n + 1 + LOOKAHEAD):
                self._emit_wdma(self.w_emitted)
                self.w_emitted += 1
        slot.b.w = (f"wq{n % NSLOT}", 16 * (n // NSLOT + 1))
        slot.b.r = {}
        view = slot.t[:, 0:nkc * ncols].rearrange("p (k c) -> p k c", k=nkc)
        return view, slot.b

    def sb(self, es, name, shape, dt):
        self.uid = getattr(self, "uid", 0) + 1
        name = f"{name}_{self.uid}"
        t = es.enter_context(self.nc.sbuf_tensor(name, list(shape), dt))
        return Tile(t, name)

    def pbank(self, pool):
        i = self.pidx[pool]
        self.pidx[pool] = (i + 1) % len(self.ppool[pool])
        return self.ps[self.ppool[pool][i]]

    def set_pools(self, **pools):
        self.ppool = {k: list(v) for k, v in pools.items()}
        self.pidx = {k: 0 for k in pools}

    def dbuf(self, key):
        if key not in self.dbufs:
            self.dbufs[key] = Buf(str(key))
        return self.dbufs[key]

    def vec(self, name, l=None):
        off, w = self.vec_off[(name, l)]
        return self.vecs.t[:, off:off + w]

    def build(self):
        nc = bass.Bass("TRN2", target_bir_lowering=False)
        self.nc = nc
        pools = {"dma_ld": [f"ld{i:02d}" for i in range(12)], "dma_st": [f"st{i:02d}" for i in range(8)]}
        sem_names = ["pe", "act", "dve", "pool", "sp", "cc"] + [f"wq{i}" for i in range(NSLOT)] + pools["dma_ld"] + pools["dma_st"]
        self.S = S = Sched(sem_names, pools)
        self.dbufs = {}
        dt = nc.dram_tensor
        self.x_in = dt("x_in", [T, D], F32, kind="ExternalInput").ap()
        self.mem_in = dt("mem_in", [256, D], F32, kind="ExternalInput").ap()
        self.vec_off, nvc = vec_layout()
        self.vecs_d = dt("vecs", [128, nvc], F32, kind="ExternalInput").ap()
        self.cst_d = dt("cst", [128, 128 + 64 + TB], F32, kind="ExternalInput").ap()
        self.gnb_d = dt("gnb", [L, 128, D], F32, kind="ExternalInput").ap()
        self.gw2_d = dt("gw2", [L, 16, 1024], F32, kind="ExternalInput").ap()
        nslab = max(1, self.plan["nslab"]) if not self.dry else 1
        self.wall = dt("wall", [nslab, 128, SLOT], F32, kind="ExternalInput").ap()
        self.out_d = dt("out", [T, D], F32, kind="ExternalOutput").ap()
        ik = "ExternalOutput" if self.debug.get("dump") else "Internal"
        self.xT = dt("xT", [D, T], F32, kind=ik).ap()
        if self.debug.get("upto", "all") not in ("s0", "none", "s0only"):
            self.qT_d = dt("qT_s", [1024, T], BF16, kind=ik).ap()
            self.kT_d = dt("kT_s", [1024, T], BF16, kind=ik).ap()
            self.kTok_d = dt("kTok_s", [T, 1024], BF16, kind=ik).ap()
            self.vTok_d = dt("vTok_s", [T, D], BF16, kind=ik).ap()
            self.sgn_d = dt("sgn_s", [T, D], BF16, kind=ik).ap()
            self.cT_d = dt("cT_s", [D, HALO + T], F32, kind=ik).ap()
            self.g0T_d = dt("g0T_s", [D, T], BF16, kind=ik).ap()
            self.g1T_d = dt("g1T_s", [D, T], BF16, kind=ik).ap()
            self.dec_d = dt("dec_s", [128, 8, NCHUNK], F32, kind=ik).ap()
            self.exS_in = [[dt(f"exS_in{l}_{q}", [512, 512], F32, kind="Internal").ap() for q in range(2)] for l in range(L)]
            self.exS_out = [[dt(f"exS_out{l}_{q}", [4 * 512, 512], F32, kind="Internal").ap() for q in range(2)] for l in range(L)]
            self.exM_in = [dt(f"exM_in{l}", [D + 128, 32], F32, kind="Internal").ap() for l in range(L)]
            self.exM_out = [dt(f"exM_out{l}", [4 * (D + 128), 32], F32, kind="Internal").ap() for l in range(L)]
            self.st_d = dt("st_s", [128, 8, 512], F32, kind=ik).ap()
        if self.debug.get("dump"):
            self.dbg_og = dt("dbg_og", [D, T], BF16, kind="ExternalOutput").ap()
            self.dbg_cn = dt("dbg_cn", [D, T], BF16, kind="ExternalOutput").ap()
            self.dbg_m = dt("dbg_m", [D, T], BF16, kind="ExternalOutput").ap()
        self.dbg_d = None
        if self.debug.get("dbg_shape"):
            self.dbg_d = dt("dbg", list(self.debug["dbg_shape"]), F32, kind="ExternalOutput").ap()

        with ExitStack() as es:
            self.sems = {k: es.enter_context(nc.semaphore(k)) for k in sem_names}
            self.ps = [Tile(es.enter_context(nc.psum_tensor(f"ps{i}", [128, 512], F32)), f"ps{i}") for i in range(8)]
            self.wslots = [self.sb(es, f"wslot{i}", [128, SLOT], BF16) for i in range(NSLOT)]
            self.vecs = self.sb(es, "vecs_sb", [128, nvc], F32)
            self.cst = self.sb(es, "cst_sb", [128, 128 + 64 + TB], F32)
            self.identb = self.sb(es, "identb", [128, 128], BF16)
            self.onesb = self.sb(es, "onesb", [128, 128], BF16)
            self.kmT = [self.sb(es, f"kmT{l}", [128, 4, 256], BF16) for l in range(L)]
            self.vm = [self.sb(es, f"vm{l}", [128, 2, 512], BF16) for l in range(L)]
            block = es.enter_context(nc.Block())
            self.set_pools(a=range(8))
            S.D("sp", "dma_ld", self.vecs.t[:, :], self.vecs_d, writes=[self.vecs])
            S.D("sp", "dma_ld", self.cst.t[:, :], self.cst_d, writes=[self.cst])
            self.ident = self.cst.t[:, 0:128]
            self.mask64 = self.cst.t[0:64, 128:192]
            self.cmask = self.cst.t[:, 192:192 + TB]
            S.I("dve", lambda e: e.tensor_copy(out=self.identb.t[:, :], in_=self.ident), [self.cst], [self.identb])
            S.I("dve", lambda e: e.memset(self.onesb.t[:, :], 1.0), [], [self.onesb])

            if self.debug.get("upto", "all") != "all":
                junk = self.sb(es, "junk", [128, 64], F32)
                for ap in (self.x_in[0:128, 0:8], self.mem_in[0:128, 0:8], self.gnb_d[0, :, 0:8], self.gw2_d[0, :, 0:8],
                           self.wall[0, :, 0:8]):
                    S.D("sp", "dma_ld", junk.t[0:ap.shape[0], 0:8], ap, writes=[junk])
            self.body(es)

            S.barrier(engs=("pe", "act", "dve", "sp", "pool"), skip=(), skip_w=False)
            if not self.dry:
                @block.sync
                def _(e):
                    S.replay("sp", e, self.sems)

                @block.tensor
                def _(e):
                    S.replay("pe", e, self.sems)

                @block.scalar
                def _(e):
                    S.replay("act", e, self.sems)

                @block.vector
                def _(e):
                    S.replay("dve", e, self.sems)

                @block.gpsimd
                def _(e):
                    S.replay("pool", e, self.sems)
        for e in Sched.ENG:
            assert not S.noinc[e], e
        if self.dry:
            return {"seq": self.wseq, "free": self.wfree, "nslab": len(self.slab_specs), "specs": self.slab_specs}
        return nc

    def body(self, es):
        upto = self.debug.get("upto", "all")
        if upto == "none":
            return
        self.stage_s0()
        if upto == "s0only":
            return
        if upto == "s0":
            return self.stage_final()
        if upto != "ffn1":
            self.stage_mem()
        nl = self.debug.get("layers", L)
        for l in range(nl):
            for blk in range(NBLK):
                self.stage_ffn(l, blk, "ffn1")
                if upto == "ffn1":
                    continue
                self.stage_mix1(l, blk)
            if upto == "ffn1":
                break
            self.stage_pass1(l)
            self.stage_exchange(l)
            for blk in range(NBLK):
                self.stage_mix2(l, blk)
                if upto == "mix2":
                    continue
                self.stage_xattn(l, blk)
                if upto == "xattn":
                    continue
                self.stage_ffn(l, blk, "ffn2")
            if upto in ("mix2", "xattn"):
                break
        self.stage_final()

    def stage_s0(self):
        S, nc = self.S, self.nc
        self.set_pools(a=range(8))
        with ExitStack() as es:
            xin = [self.sb(es, f"s0_x{i}", [128, 4, D], F32) for i in range(1)]
            xo = [self.sb(es, f"s0_o{i}", [128, 16, 512], F32) for i in range(2)]
            for g in range(T // 512):
                xi, xt = xin[0], xo[g % 2]
                S.D("sp", "dma_ld", xi.t[:, :, :], self.x_in[g * 512:(g + 1) * 512, :].rearrange("(j p) d -> p j d", p=128),
                    writes=[xi])
                for k in range(16):
                    pb = self.pbank("a")
                    for j in range(4):
                        S.I("pe", lambda e, pb=pb, xi=xi, j=j, k=k: e.transpose(
                            out=pb.t[:, j * 128:(j + 1) * 128], in_=xi.t[:, j, k * 128:(k + 1) * 128], identity=self.ident),
                            [xi, self.cst], [pb], inc=(j == 3))
                    eng = "act" if k % 2 else "dve"
                    if eng == "act":
                        S.I("act", lambda e, pb=pb, xt=xt, k=k: e.copy(out=xt.t[:, k, :], in_=pb.t[:, :]), [pb], [xt])
                    else:
                        S.I("dve", lambda e, pb=pb, xt=xt, k=k: e.tensor_copy(out=xt.t[:, k, :], in_=pb.t[:, :]), [pb], [xt])
                S.D("sp", "dma_st", self.xT[:, g * 512:(g + 1) * 512].rearrange("(k p) t -> p k t", p=128), xt.t[:, :, :],
                    reads=[xt], writes=[self.dbuf(("xT", g))])
            S.barrier()

    def rstd_bc(self, xt, h, sq, rs, nchunks=16, scale=1.0 / D):
        S = self.S
        pb = self.pbank("a")
        for k in range(nchunks):
            sqk = sq[k % len(sq)]
            S.I("act", lambda e, k=k, sqk=sqk: e.activation(out=sqk.t[:, :], in_=xt.t[:, k, h * 512:(h + 1) * 512], func=AF.Square),
                [xt.s(k, h)], [sqk])
            S.I("pe", lambda e, k=k, pb=pb, sqk=sqk: e.matmul(out=pb.t[:, :], lhsT=self.onesb.t[:, :], rhs=sqk.t[:, :],
                                                              start=(k == 0), stop=(k == nchunks - 1)),
                [sqk, self.onesb], [pb], inc=True)
        S.I("act", lambda e, pb=pb: e.activation(out=rs.t[:, :], in_=pb.t[:, :], func=AF.Sqrt, scale=scale,
                                                 bias=self.vec("eps")), [pb, self.vecs], [rs])
        S.I("dve", lambda e: e.reciprocal(out=rs.t[:, :], in_=rs.t[:, :]), [], [rs])


    def load_x(self, xt, blk, h0, nh):
        for h in range(nh):
            gi = blk * NH + h0 + h
            self.S.D("sp", "dma_ld", xt.t[:, :, h * 512:(h + 1) * 512],
                     self.xT[:, gi * 512:(gi + 1) * 512].rearrange("(k p) t -> p k t", p=128),
                     reads=[self.dbuf(("xT", gi))], writes=[xt.s(k, h) for k in range(16)])

    def store_x(self, xt, blk, h0, nh):
        for h in range(nh):
            gi = blk * NH + h0 + h
            self.S.D("sp", "dma_st", self.xT[:, gi * 512:(gi + 1) * 512].rearrange("(k p) t -> p k t", p=128),
                     xt.t[:, :, h * 512:(h + 1) * 512], reads=[xt.s(k, h) for k in range(16)], writes=[self.dbuf(("xT", gi))])

    def make_hT(self, xt, hT, g, nh, sq, rs, hoff=0):
        S = self.S
        for h in range(nh):
            self.rstd_bc(xt, h, sq, rs)
            ho = hoff + h
            for k in range(16):
                S.I("dve", lambda e, k=k, h=h, ho=ho: e.scalar_tensor_tensor(
                    out=hT.t[:, k, ho * 512:(ho + 1) * 512], in0=xt.t[:, k, h * 512:(h + 1) * 512], scalar=g[:, k:k + 1],
                    in1=rs.t[:, :], op0=ALU.mult, op1=ALU.mult), [xt.s(k, h), rs, self.vecs], [hT.s(k, ho)])

    def mm_group(self, pb, lhs_fn, rhs_fn, nk, reads, M=128, N=512, po=0):
        for k in range(nk):
            self.S.I("pe", lambda e, k=k: e.matmul(out=pb.t[po:po + M, 0:N], lhsT=lhs_fn(k), rhs=rhs_fn(k),
                                                  start=(k == 0), stop=(k == nk - 1)),
                     reads, [pb], inc=(k == nk - 1))


    def ld(self, dst_ap, src_ap, tile):
        self.S.D("sp", "dma_ld", dst_ap, src_ap, writes=[tile])

    def st(self, dst_ap, src_ap, tile):
        self.S.D("sp", "dma_st", dst_ap, src_ap, reads=[tile])

    def bview(self, pb):
        return pb.t[:, :].bitcast(BF16)

    def fm_out(self, W, Wb, c, hT, nh, pool="a", hoff=0):
        S = self.S
        pbs = [self.pbank(pool) for _ in range(nh)]
        for k in range(16):
            for h in range(nh):
                ho = hoff + h
                S.I("pe", lambda e, k=k, h=h, ho=ho, p=pbs[h]: e.matmul(
                    out=p.t[:, :], lhsT=W[:, k, c * 128:(c + 1) * 128], rhs=hT.t[:, k, ho * 512:(ho + 1) * 512],
                    start=(k == 0), stop=(k == 15)), [Wb, hT.s(k, ho)], [pbs[h]], inc=(k == 15))
        return pbs

    def stage_mem(self):
        S = self.S
        self.set_pools(a=range(8))
        with ExitStack() as es:
            mt = self.sb(es, "mem_x", [128, 2, D], F32)
            mn = self.sb(es, "mem_n", [128, 2, D], F32)
            junk = self.sb(es, "mem_j", [128, D], BF16)
            ssq = self.sb(es, "mem_ss", [128, 2], F32)
            mhT = [self.sb(es, f"mem_hT{l}", [128, 16, 256], BF16) for l in range(L)]
            self.ld(mt.t[:, :, :], self.mem_in.rearrange("(j p) d -> p j d", p=128), mt)
            for j in range(2):
                S.I("act", lambda e, j=j: e.activation(out=junk.t[:, :], in_=mt.t[:, j, :], func=AF.Square,
                                                        accum_out=ssq.t[:, j:j + 1]), [mt], [junk, ssq])
            S.I("dve", lambda e: e.tensor_scalar(out=ssq.t[:, :], in0=ssq.t[:, :], scalar1=1.0 / D, scalar2=EPS,
                                                 op0=ALU.mult, op1=ALU.add), [], [ssq])
            S.I("act", lambda e: e.activation(out=ssq.t[:, :], in_=ssq.t[:, :], func=AF.Sqrt), [], [ssq])
            S.I("dve", lambda e: e.reciprocal(out=ssq.t[:, :], in_=ssq.t[:, :]), [], [ssq])
            for j in range(2):
                S.I("dve", lambda e, j=j: e.tensor_scalar(out=mn.t[:, j, :], in0=mt.t[:, j, :], scalar1=ssq.t[:, j:j + 1],
                                                          scalar2=None, op0=ALU.mult), [mt, ssq], [mn])
            for k in range(16):
                pb = self.pbank("a")
                for j in range(2):
                    S.I("pe", lambda e, pb=pb, j=j, k=k: e.transpose(out=pb.t[:, j * 128:(j + 1) * 128],
                                                                     in_=mn.t[:, j, k * 128:(k + 1) * 128], identity=self.ident),
                        [mn, self.cst], [pb], inc=(j == 1))
                for l in range(L):
                    g = self.vec("xa_mem_norm", l)
                    S.I("dve", lambda e, pb=pb, l=l, k=k, g=g: e.tensor_scalar(out=mhT[l].t[:, k, :], in0=pb.t[:, 0:256],
                                                                              scalar1=g[:, k:k + 1], scalar2=None, op0=ALU.mult),
                        [pb, self.vecs], [mhT[l]])
            for l in range(L):
                for s2 in range(2):
                    W, Wb = self.wload("xa_w_kv", l, 0, 16, s2 * 256, 256)
                    for c in range(2):
                        hd = 2 * s2 + c
                        pb = self.pbank("a")
                        for k in range(16):
                            S.I("pe", lambda e, W=W, k=k, c=c, pb=pb, l=l: e.matmul(
                                out=pb.t[:, 0:256], lhsT=W[:, k, c * 128:(c + 1) * 128], rhs=mhT[l].t[:, k, :],
                                start=(k == 0), stop=(k == 15)), [Wb, mhT[l]], [pb], inc=(k == 15))
                        S.I("act", lambda e, pb=pb, l=l, hd=hd: e.copy(out=self.kmT[l].t[:, hd, :], in_=pb.t[:, 0:256]),
                            [pb], [self.kmT[l]])
                for s2 in range(2):
                    W, Wb = self.wload("xa_w_kv", l, 0, 16, 512 + s2 * 256, 256)
                    for mc in range(2):
                        pb = self.pbank("a")
                        for k in range(16):
                            S.I("pe", lambda e, W=W, k=k, mc=mc, pb=pb, l=l: e.matmul(
                                out=pb.t[:, 0:256], lhsT=mhT[l].t[:, k, mc * 128:(mc + 1) * 128], rhs=W[:, k, :],
                                start=(k == 0), stop=(k == 15)), [Wb, mhT[l]], [pb], inc=(k == 15))
                        S.I("act", lambda e, pb=pb, l=l, mc=mc, s2=s2: e.copy(
                            out=self.vm[l].t[:, mc, s2 * 256:(s2 + 1) * 256], in_=pb.t[:, 0:256]), [pb], [self.vm[l]])
            S.barrier()

    def stage_mix1(self, l, blk):
        S = self.S
        tok0 = blk * TB
        NT = TB // 128
        with ExitStack() as es:
            hT = self.sb(es, "m1_h", [128, 16, TB], BF16)
            ebq = self.sb(es, "m1_ebq", [128, 8, TB], BF16)
            ebk = self.sb(es, "m1_ebk", [128, 8, TB], BF16)
            self.set_pools(a=range(0, 6), t=range(6, 8))
            with ExitStack() as es2:
                xt = self.sb(es2, "m1_x", [128, 16, 512], F32)
                sq = [self.sb(es2, f"m1_sq{i}", [128, 512], BF16) for i in range(4)]
                rs = self.sb(es2, "m1_rs", [128, 512], F32)
                for h in range(NH):
                    self.load_x(xt, blk, h, 1)
                    self.make_hT(xt, hT, self.vec("mix_norm", l), 1, sq, rs, hoff=h)
                S.barrier()
            gw2 = self.sb(es, "m1_gw2", [16, 1024], F32)
            gnb = self.sb(es, "m1_gnb", [128, D], F32)
            ngb = self.sb(es, "m1_ngb", [128, 8], F32)
            rT = self.sb(es, "m1_rT", [16, TB], F32)
            dect = self.sb(es, "m1_dec", [128, 8, TB // 64], F32)
            self.ld(gw2.t[:, :], self.gw2_d[l], gw2)
            self.ld(gnb.t[:, :], self.gnb_d[l], gnb)
            S.I("dve", lambda e: e.tensor_scalar(out=ngb.t[:, :], in0=self.vec("gla_gate_b", l), scalar1=-1.0, scalar2=None,
                                                 op0=ALU.mult), [self.vecs], [ngb])
            R, Rb = self.wload("mix_w_in", l, 0, 16, C_R, 16)
            for h in range(NH):
                pb = self.pbank("a")
                for k in range(16):
                    S.I("pe", lambda e, k=k, h=h, pb=pb: e.matmul(out=pb.t[0:16, :], lhsT=R[:, k, 0:16],
                                                                  rhs=hT.t[:, k, h * 512:(h + 1) * 512],
                                                                  start=(k == 0), stop=(k == 15)), [Rb, hT], [pb], inc=(k == 15))
                S.I("act", lambda e, h=h, pb=pb: e.copy(out=rT.t[0:16, h * 512:(h + 1) * 512], in_=pb.t[0:16, :]), [pb], [rT])
            with ExitStack() as es2:
                vtl = [self.sb(es2, f"m1_vt{i}", [128, NT, 256], BF16) for i in range(2)]
                stl = [self.sb(es2, f"m1_st{i}", [128, 2, 256], F32) for i in range(2)]
                spt = [self.sb(es2, f"m1_sp{i}", [128, TB], F32) for i in range(2)]
                cum = [self.sb(es2, f"m1_cum{i}", [128, TB], F32) for i in range(2)]
                et = [self.sb(es2, f"m1_et{i}", [128, 512], F32) for i in range(2)]
                def decay(dkc):
                    sp, cm = spt[dkc % 2], cum[dkc % 2]
                    for h in range(NH):
                        pb = self.pbank("a")
                        ee = et[h % 2]
                        S.I("pe", lambda e, dkc=dkc, h=h, pb=pb: e.matmul(
                            out=pb.t[:, :], lhsT=gw2.t[0:16, dkc * 128:(dkc + 1) * 128], rhs=rT.t[0:16, h * 512:(h + 1) * 512],
                            start=True, stop=True), [gw2, rT], [pb])
                        S.I("act", lambda e, dkc=dkc, pb=pb, ee=ee: e.activation(out=ee.t[:, :], in_=pb.t[:, :], func=AF.Exp,
                                                                                 scale=-1.0, bias=ngb.t[:, dkc:dkc + 1]),
                            [pb, ngb], [ee])
                        S.I("act", lambda e, h=h, sp=sp, ee=ee: e.activation(out=sp.t[:, h * 512:(h + 1) * 512], in_=ee.t[:, :],
                                                                             func=AF.Ln, scale=1.0, bias=self.vec("one")),
                            [ee, self.vecs], [sp])
                    S.I("dve", lambda e, sp=sp, cm=cm: e.tensor_tensor_scan(out=cm.t[:, :], data0=self.cmask, data1=sp.t[:, :],
                                                                            initial=0.0, op0=ALU.mult, op1=ALU.add),
                        [sp, self.cst], [cm])
                    S.I("act", lambda e, cm=cm, dkc=dkc: e.activation(out=ebq.t[:, dkc, :], in_=cm.t[:, :], func=AF.Exp,
                                                                      scale=-1.0 / 16), [cm], [ebq.s(dkc)])
                    S.I("act", lambda e, cm=cm, dkc=dkc: e.activation(out=ebk.t[:, dkc, :], in_=cm.t[:, :], func=AF.Exp,
                                                                      scale=1.0 / 16), [cm], [ebk.s(dkc)])
                    S.I("act", lambda e, cm=cm, dkc=dkc: e.activation(
                        out=dect.t[:, dkc, :], in_=cm.t[:, :].rearrange("p (c t) -> p c t", t=64)[:, :, 63], func=AF.Exp,
                        scale=-1.0 / 16), [cm], [dect])

                ti = 0
                for which in ("v", "g"):
                    cbase = C_V if which == "v" else C_G
                    dst = self.vTok_d if which == "v" else self.sgn_d
                    for s8 in range(8):
                        if which == "v":
                            decay(s8)
                        W, Wb = self.wload("mix_w_in", l, 0, 16, cbase + s8 * 256, 256)
                        vt = vtl[s8 % 2]
                        for jp in range(NT // 2):
                            pb = self.pbank("a")
                            for jj in range(2):
                                j = 2 * jp + jj
                                for k in range(16):
                                    S.I("pe", lambda e, W=W, k=k, j=j, jj=jj, pb=pb: e.matmul(
                                        out=pb.t[:, jj * 256:(jj + 1) * 256], lhsT=hT.t[:, k, j * 128:(j + 1) * 128], rhs=W[:, k, :],
                                        start=(k == 0), stop=(k == 15)), [Wb, hT], [pb], inc=(k == 15))
                            if which == "v":
                                S.I("act", lambda e, vt=vt, jp=jp, pb=pb: e.copy(
                                    out=vt.t[:, 2 * jp:2 * jp + 2, :], in_=pb.t[:, :].rearrange("p (j c) -> p j c", j=2)),
                                    [pb], [vt.s(jp)])
                            else:
                                tt = stl[ti % 2]
                                ti += 1
                                S.I("act", lambda e, tt=tt, pb=pb: e.activation(
                                    out=tt.t[:, :, :], in_=pb.t[:, :].rearrange("p (j c) -> p j c", j=2), func=AF.Silu), [pb], [tt])
                                for jj in range(2):
                                    S.I("dve", lambda e, tt=tt, vt=vt, jp=jp, jj=jj, s8=s8: e.tensor_tensor(
                                        out=vt.t[:, 2 * jp + jj, :], in0=tt.t[:, jj, :], in1=gnb.t[:, s8 * 256:(s8 + 1) * 256],
                                        op=ALU.mult), [tt, gnb], [vt.s(jp)])
                        self.st(dst[tok0:tok0 + TB, s8 * 256:(s8 + 1) * 256].rearrange("(j p) c -> p j c", p=128), vt.t[:, :, :], vt)
                    if which == "v":
                        self.st(self.dec_d[:, :, blk * (TB // 64):(blk + 1) * (TB // 64)], dect.t[:, :, :], dect)
                S.barrier()
            with ExitStack() as es2:
                ob = [self.sb(es2, f"m1_ob{i}", [128, 512], BF16) for i in range(4)]
                ktl = [self.sb(es2, f"m1_kt{i}", [128, 4, 128], BF16) for i in range(2)]
                oi = 0
                for which in ("q", "k"):
                    cbase = C_Q if which == "q" else C_K
                    for s4 in range(4):
                        W, Wb = self.wload("mix_w_in", l, 0, 16, cbase + s4 * 256, 256)
                        for c in range(2):
                            dkc = 2 * s4 + c
                            pbs = self.fm_out(W, Wb, c, hT, NH)
                            for h in range(NH):
                                o = ob[oi % 4]
                                oi += 1
                                cols = slice(tok0 + h * 512, tok0 + (h + 1) * 512)
                                if which == "q":
                                    S.I("dve", lambda e, o=o, p=pbs[h], dkc=dkc, h=h: e.scalar_tensor_tensor(
                                        out=o.t[:, :], in0=p.t[:, :], scalar=0.0625, in1=ebq.t[:, dkc, h * 512:(h + 1) * 512],
                                        op0=ALU.mult, op1=ALU.mult), [pbs[h], ebq.s(dkc)], [o])
                                    self.st(self.qT_d[dkc * 128:(dkc + 1) * 128, cols], o.t[:, :], o)
                                else:
                                    S.I("dve", lambda e, o=o, p=pbs[h], dkc=dkc, h=h: e.tensor_tensor(
                                        out=o.t[:, :], in0=p.t[:, :], in1=ebk.t[:, dkc, h * 512:(h + 1) * 512], op=ALU.mult),
                                        [pbs[h], ebk.s(dkc)], [o])
                                    self.st(self.kT_d[dkc * 128:(dkc + 1) * 128, cols], o.t[:, :], o)
                                    pt = self.pbank("t")
                                    ptb = self.bview(pt)
                                    for j in range(4):
                                        S.I("pe", lambda e, ptb=ptb, o=o, j=j: e.transpose(
                                            out=ptb[:, j * 128:(j + 1) * 128], in_=o.t[:, j * 128:(j + 1) * 128],
                                            identity=self.identb.t[:, :]), [o, self.identb], [pt], inc=(j == 3))
                                    kt = ktl[oi % 2]
                                    S.I("act", lambda e, kt=kt, ptb=ptb: e.copy(
                                        out=kt.t[:, :, :], in_=ptb[:, 0:512].rearrange("p (j d) -> p j d", j=4)), [pt], [kt])
                                    self.st(self.kTok_d[tok0 + h * 512:tok0 + (h + 1) * 512, dkc * 128:(dkc + 1) * 128]
                                            .rearrange("(j p) d -> p j d", p=128), kt.t[:, :, :], kt)
                S.barrier()
            with ExitStack() as es2:
                ctl = [self.sb(es2, f"m1_ct{i}", [128, 512], F32) for i in range(4)]
                sgt = [self.sb(es2, f"m1_sg{i}", [128, 512], F32) for i in range(2)]
                gtl = [self.sb(es2, f"m1_gt{i}", [128, 512], BF16) for i in range(4)]
                ci = 0
                for s8 in range(8):
                    UA, UAb = self.wload("mix_w_in", l, 0, 16, C_UA + s8 * 256, 256)
                    UG, UGb = self.wload("mix_w_in", l, 0, 16, C_UG + s8 * 256, 256)
                    for c in range(2):
                        ch = 2 * s8 + c
                        pa = self.fm_out(UA, UAb, c, hT, NH)
                        pg = self.fm_out(UG, UGb, c, hT, NH)
                        for h in range(NH):
                            sg, ct = sgt[ci % 2], ctl[ci % 4]
                            ci += 1
                            S.I("act", lambda e, sg=sg, p=pg[h]: e.activation(out=sg.t[:, :], in_=p.t[:, :], func=AF.Sigmoid),
                                [pg[h]], [sg])
                            S.I("dve", lambda e, sg=sg, ct=ct, p=pa[h]: e.tensor_tensor(out=ct.t[:, :], in0=sg.t[:, :], in1=p.t[:, :],
                                                                                          op=ALU.mult), [sg, pa[h]], [ct])
                            self.st(self.cT_d[ch * 128:(ch + 1) * 128, HALO + tok0 + h * 512:HALO + tok0 + (h + 1) * 512],
                                    ct.t[:, :], ct)
                bgb = self.vec("branch_gate_b", l)
                for gi_, (cb_, dst) in enumerate(((C_G0, self.g0T_d), (C_G1, self.g1T_d))):
                    for s8 in range(8):
                        W, Wb = self.wload("mix_w_in", l, 0, 16, cb_ + s8 * 256, 256)
                        for c in range(2):
                            ch = 2 * s8 + c
                            pbs = self.fm_out(W, Wb, c, hT, NH)
                            for h in range(NH):
                                gt = gtl[ci % 4]
                                ci += 1
                                S.I("act", lambda e, gt=gt, p=pbs[h], ch=ch, gi_=gi_: e.activation(
                                    out=gt.t[:, :], in_=p.t[:, :], func=AF.Sigmoid, bias=bgb[:, gi_ * 16 + ch:gi_ * 16 + ch + 1],
                                    scale=1.0), [pbs[h], self.vecs], [gt])
                                self.st(dst[ch * 128:(ch + 1) * 128, tok0 + h * 512:tok0 + (h + 1) * 512], gt.t[:, :], gt)
                S.barrier()
            S.barrier()

    def stage_pass1(self, l):
        S = self.S
        self.set_pools(a=range(8))
        with ExitStack() as es:
            U = self.sb(es, "p1_U", [128, 8, 512], F32)
            dec = self.sb(es, "p1_dec", [128, 8, NCHUNK], F32)
            ktl = [self.sb(es, f"p1_k{i}", [64, 1024], BF16) for i in range(3)]
            vtl = [self.sb(es, f"p1_v{i}", [64, D], BF16) for i in range(3)]
            dtot = self.sb(es, "p1_dt", [128, 8], F32)
            tail = self.sb(es, "p1_tail", [128, 16, 30], F32)
            self.ld(dec.t[:, :, :], self.dec_d, dec)
            for c in range(NCHUNK):
                kt, vt = ktl[c % 3], vtl[c % 3]
                self.ld(kt.t[:, :], self.kTok_d[c * 64:(c + 1) * 64, :], kt)
                self.ld(vt.t[:, :], self.vTok_d[c * 64:(c + 1) * 64, :], vt)
                for i in range(8):
                    hd, half = i // 2, i % 2
                    pb = self.pbank("a")
                    S.I("pe", lambda e, kt=kt, vt=vt, hd=hd, half=half, pb=pb: e.matmul(
                        out=pb.t[:, :], lhsT=kt.t[0:64, hd * 256 + half * 128:hd * 256 + (half + 1) * 128],
                        rhs=vt.t[0:64, hd * 512:(hd + 1) * 512], start=True, stop=True), [kt, vt], [pb])
                    if c == 0:
                        S.I("dve", lambda e, i=i, pb=pb: e.tensor_copy(out=U.t[:, i, :], in_=pb.t[:, :]), [pb], [U.s(i)])
                    else:
                        S.I("dve", lambda e, i=i, pb=pb, c=c: e.scalar_tensor_tensor(
                            out=U.t[:, i, :], in0=U.t[:, i, :], scalar=dec.t[:, i, c - 1:c], in1=pb.t[:, :],
                            op0=ALU.mult, op1=ALU.add), [pb, dec], [U.s(i)])
            for i in range(8):
                S.I("dve", lambda e, i=i: e.tensor_scalar(out=U.t[:, i, :], in0=U.t[:, i, :], scalar1=dec.t[:, i, NCHUNK - 1:NCHUNK],
                                                          scalar2=None, op0=ALU.mult), [dec], [U.s(i)])
            for q in range(2):
                self.st(self.exS_in[l][q].rearrange("(i p) e -> p i e", p=128), U.t[:, 4 * q:4 * q + 4, :], U)
            S.I("dve", lambda e: e.tensor_reduce(out=dtot.t[:, :], in_=dec.t[:, :, :], axis=AXX, op=ALU.mult), [dec], [dtot])
            self.st(self.exM_in[l][D:D + 128, 0:8], dtot.t[:, :], dtot)
            self.ld(tail.t[:, :, :], self.cT_d[:, HALO + T - 30:HALO + T].rearrange("(k p) c -> p k c", p=128), tail)
            self.st(self.exM_in[l][0:D, 0:30].rearrange("(k p) c -> p k c", p=128), tail.t[:, :, :], tail)
            S.barrier()

    def _exchange_zero(self):
        S = self.S
        with ExitStack() as es:
            Sin = self.sb(es, "ex_S", [128, 8, 512], F32)
            hl = self.sb(es, "ex_h", [128, 16, 30], F32)
            S.I("dve", lambda e: e.memset(Sin.t[:, :, :], 0.0), [], [Sin])
            S.I("dve", lambda e: e.memset(hl.t[:, :, :], 0.0), [], [hl])
            self.st(self.st_d, Sin.t[:, :, :], Sin)
            self.st(self.cT_d[:, 2:32].rearrange("(k q) c -> q k c", q=128), hl.t[:, :, :], hl)
            S.barrier()

    def stage_exchange(self, l):
        S = self.S
        groups = [[0, 1, 2, 3], [4, 5, 6, 7]]
        noexch = bool(self.debug.get("noexch"))
        if noexch:
            return self._exchange_zero()
        S.waits("pool", {k: v for k, v in S.cnt.items() if v > 0 and (k in ("dve", "act", "pe", "sp") or k.startswith("st"))})
        pairs = [(self.exS_in[l][q], self.exS_out[l][q]) for q in range(2)] + [(self.exM_in[l], self.exM_out[l])]
        for src, dst in pairs:
            S.X("pool", "cc", 1, lambda e, src=src, dst=dst: e.collective_compute(
                "AllGather", ALU.bypass, replica_groups=groups, ins=[src], outs=[dst]))
        ccv = {"cc": S.cnt["cc"]}
        for eng in ("sp", "dve", "act", "pe"):
            S.waits(eng, ccv)
        R = D + 128
        rmask, rmaskc, roh = self.vec("rmask"), self.vec("rmaskc"), self.vec("ronehot")
        with ExitStack() as es:
            Dp = self.sb(es, "ex_D", [128, 4, 8], F32)
            av = self.sb(es, "ex_a", [128, 4, 8], F32)
            Sin = self.sb(es, "ex_S", [128, 8, 512], F32)
            tl = [self.sb(es, f"ex_t{i}", [128, 8, 512], F32) for i in range(2)]
            hl = self.sb(es, "ex_h", [128, 16, 30], F32)
            ht = [self.sb(es, f"ex_ht{i}", [128, 16, 30], F32) for i in range(2)]
            for p in range(3):
                self.ld(Dp.t[:, p, :], self.exM_out[l][p * R + D:p * R + D + 128, 0:8], Dp)
                S.I("dve", lambda e, p=p: e.tensor_scalar(out=av.t[:, p, :], in0=Dp.t[:, p, :], scalar1=rmask[:, p:p + 1],
                                                          scalar2=rmaskc[:, p:p + 1], op0=ALU.mult, op1=ALU.add),
                    [Dp, self.vecs], [av])
            S.I("dve", lambda e: e.memset(Sin.t[:, :, :], 0.0), [], [Sin])
            S.I("dve", lambda e: e.memset(hl.t[:, :, :], 0.0), [], [hl])
            for p in range(3):
                t = tl[p % 2]
                for q in range(2):
                    self.ld(t.t[:, 4 * q:4 * q + 4, :], self.exS_out[l][q][p * 512:(p + 1) * 512, :].rearrange("(i q) e -> q i e", q=128), t)
                for i in range(8):
                    S.I("dve", lambda e, t=t, i=i, p=p: e.tensor_scalar(out=t.t[:, i, :], in0=t.t[:, i, :], scalar1=rmask[:, p:p + 1],
                                                                        scalar2=None, op0=ALU.mult), [self.vecs], [t.s(i)])
                    S.I("dve", lambda e, t=t, i=i, p=p: e.scalar_tensor_tensor(
                        out=Sin.t[:, i, :], in0=Sin.t[:, i, :], scalar=av.t[:, p, i:i + 1], in1=t.t[:, i, :],
                        op0=ALU.mult, op1=ALU.add), [t.s(i), av], [Sin.s(i)])
                h = ht[p % 2]
                self.ld(h.t[:, :, :], self.exM_out[l][p * R:p * R + D, 0:30].rearrange("(k q) c -> q k c", q=128), h)
                S.I("dve", lambda e, h=h, p=p: e.scalar_tensor_tensor(out=hl.t[:, :, :], in0=h.t[:, :, :], scalar=roh[:, p:p + 1],
                                                                      in1=hl.t[:, :, :], op0=ALU.mult, op1=ALU.add),
                    [h, self.vecs], [hl])
            self.st(self.st_d, Sin.t[:, :, :], Sin)
            self.st(self.cT_d[:, 2:32].rearrange("(k q) c -> q k c", q=128), hl.t[:, :, :], hl)
            S.barrier()


    def stage_mix2(self, l, blk):
        S = self.S
        tok0 = blk * TB
        c0 = blk * (TB // 64)
        with ExitStack() as es:
            ogT = self.sb(es, "m2_og", [128, 16, TB], BF16)
            with ExitStack() as es2:
                U = self.sb(es2, "m2_U", [128, 8, 512], F32)
                Sbfl = [self.sb(es2, f"m2_Sbf{i}", [128, 8, 512], BF16) for i in range(2)]
                dec = self.sb(es2, "m2_dec", [128, 8, NCHUNK], F32)
                qpl = [self.sb(es2, f"m2_q{i}", [128, 8, 128], BF16) for i in range(2)]
                kpl = [self.sb(es2, f"m2_k{i}", [128, 8, 128], BF16) for i in range(2)]
                ktl = [self.sb(es2, f"m2_kt{i}", [64, 2, 1024], BF16) for i in range(2)]
                vtl = [self.sb(es2, f"m2_vt{i}", [64, 2, D], BF16) for i in range(2)]
                sgl = [self.sb(es2, f"m2_sg{i}", [128, D], BF16) for i in range(2)]
                onl = [self.sb(es2, f"m2_on{i}", [128, D], BF16) for i in range(2)]
                scl = [self.sb(es2, f"m2_sc{i}", [64, 4, 64], BF16) for i in range(4)]
                junk = self.sb(es2, "m2_junk", [128, 512], BF16)
                ssl = [self.sb(es2, f"m2_ss{i}", [128, 4], F32) for i in range(2)]
                self.set_pools(s=[4], kv=[5, 6, 7], t=[4])
                po = [self.ps[hd] for hd in range(4)]
                self.ld(dec.t[:, :, :], self.dec_d, dec)
                self.ld(U.t[:, :, :], self.st_d, U)
                Sbf = Sbfl[(c0 - 1) % 2]
                for i in range(8):
                    if blk == 0:
                        S.I("act", lambda e, i=i, Sbf=Sbf: e.copy(out=Sbf.t[:, i, :], in_=U.t[:, i, :]), [U], [Sbf.s(i)])
                    else:
                        S.I("act", lambda e, i=i, Sbf=Sbf: e.activation(out=Sbf.t[:, i, :], in_=U.t[:, i, :], func=AF.Identity,
                                                                        scale=dec.t[:, i, c0 - 1:c0]), [U, dec], [Sbf.s(i)])
                sci = 0
                for pr in range(TB // 128):
                    qp, kp, kt, vt, sg, on, ss = qpl[pr % 2], kpl[pr % 2], ktl[pr % 2], vtl[pr % 2], sgl[pr % 2], onl[pr % 2], ssl[pr % 2]
                    t0 = tok0 + pr * 128
                    self.ld(qp.t[:, :, :], self.qT_d[:, t0:t0 + 128].rearrange("(k p) t -> p k t", p=128), qp)
                    self.ld(kp.t[:, :, :], self.kT_d[:, t0:t0 + 128].rearrange("(k p) t -> p k t", p=128), kp)
                    self.ld(kt.t[:, :, :], self.kTok_d[t0:t0 + 128, :].rearrange("(c p) d -> p c d", p=64), kt)
                    self.ld(vt.t[:, :, :], self.vTok_d[t0:t0 + 128, :].rearrange("(c p) d -> p c d", p=64), vt)
                    self.ld(sg.t[:, :], self.sgn_d[t0:t0 + 128, :], sg)
                    for ci in range(2):
                        c = c0 + 2 * pr + ci
                        lo = ci * 64
                        Sprev, Snew = Sbfl[(c - 1) % 2], Sbfl[c % 2]
                        psb = self.pbank("s")
                        for hd in range(4):
                            for half in range(2):
                                dkc = 2 * hd + half
                                S.I("pe", lambda e, psb=psb, kp=kp, qp=qp, dkc=dkc, lo=lo, half=half, hd=hd: e.matmul(
                                    out=psb.t[0:64, hd * 64:(hd + 1) * 64], lhsT=kp.t[:, dkc, lo:lo + 64], rhs=qp.t[:, dkc, lo:lo + 64],
                                    start=(half == 0), stop=(half == 1)), [kp, qp], [psb], inc=(half == 1))
                        sc = scl[sci % 4]
                        sci += 1
                        S.I("dve", lambda e, sc=sc, psb=psb: e.tensor_tensor(
                            out=sc.t[:, :, :], in0=psb.t[0:64, 0:256].rearrange("p (h i) -> p h i", h=4),
                            in1=self.mask64.unsqueeze(1).to_broadcast([64, 4, 64]), op=ALU.mult), [psb, self.cst], [sc])
                        for hd in range(4):
                            pob = po[hd]
                            for half in range(2):
                                dkc = 2 * hd + half
                                S.I("pe", lambda e, pob=pob, qp=qp, dkc=dkc, lo=lo, half=half, Sprev=Sprev: e.matmul(
                                    out=pob.t[lo:lo + 64, :], lhsT=qp.t[:, dkc, lo:lo + 64], rhs=Sprev.t[:, dkc, :],
                                    start=(half == 0), stop=False), [qp, Sprev.s(dkc)], [pob], inc=False)
                            S.I("pe", lambda e, pob=pob, sc=sc, vt=vt, ci=ci, hd=hd, lo=lo: e.matmul(
                                out=pob.t[lo:lo + 64, :], lhsT=sc.t[:, hd, :], rhs=vt.t[0:64, ci, hd * 512:(hd + 1) * 512],
                                start=False, stop=True), [sc, vt], [pob], inc=True)
                        for hd in range(4):
                            for half in range(2):
                                i = 2 * hd + half
                                pkv = self.pbank("kv")
                                S.I("pe", lambda e, pkv=pkv, kt=kt, vt=vt, ci=ci, hd=hd, half=half: e.matmul(
                                    out=pkv.t[:, :], lhsT=kt.t[0:64, ci, hd * 256 + half * 128:hd * 256 + (half + 1) * 128],
                                    rhs=vt.t[0:64, ci, hd * 512:(hd + 1) * 512], start=True, stop=True), [kt, vt], [pkv])
                                if c == 0:
                                    S.I("dve", lambda e, i=i, pkv=pkv: e.tensor_tensor(out=U.t[:, i, :], in0=U.t[:, i, :], in1=pkv.t[:, :],
                                                                                       op=ALU.add), [pkv], [U.s(i)])
                                else:
                                    S.I("dve", lambda e, i=i, pkv=pkv, c=c: e.scalar_tensor_tensor(
                                        out=U.t[:, i, :], in0=U.t[:, i, :], scalar=dec.t[:, i, c - 1:c], in1=pkv.t[:, :],
                                        op0=ALU.mult, op1=ALU.add), [pkv, dec], [U.s(i)])
                                S.I("act", lambda e, i=i, c=c, Snew=Snew: e.activation(out=Snew.t[:, i, :], in_=U.t[:, i, :], func=AF.Identity,
                                                                                       scale=dec.t[:, i, c:c + 1]), [U.s(i), dec], [Snew.s(i)])
                    for hd in range(4):
                        S.I("act", lambda e, hd=hd, ss=ss: e.activation(out=junk.t[:, :], in_=po[hd].t[:, :], func=AF.Square,
                                                                        accum_out=ss.t[:, hd:hd + 1]), [po[hd]], [junk, ss])
                    S.I("dve", lambda e, ss=ss: e.tensor_scalar(out=ss.t[:, :], in0=ss.t[:, :], scalar1=1.0 / 512, scalar2=EPS,
                                                                op0=ALU.mult, op1=ALU.add), [], [ss])
                    S.I("act", lambda e, ss=ss: e.activation(out=ss.t[:, :], in_=ss.t[:, :], func=AF.Sqrt), [], [ss])
                    S.I("dve", lambda e, ss=ss: e.reciprocal(out=ss.t[:, :], in_=ss.t[:, :]), [], [ss])
                    for hd in range(4):
                        S.I("dve", lambda e, hd=hd, ss=ss, on=on, sg=sg: e.scalar_tensor_tensor(
                            out=on.t[:, hd * 512:(hd + 1) * 512], in0=po[hd].t[:, :], scalar=ss.t[:, hd:hd + 1],
                            in1=sg.t[:, hd * 512:(hd + 1) * 512], op0=ALU.mult, op1=ALU.mult), [po[hd], ss, sg], [on.s(hd)])
                    for e4 in range(4):
                        pt = self.pbank("t")
                        ptb = self.bview(pt)
                        for q in range(4):
                            ec = e4 * 4 + q
                            S.I("pe", lambda e, ptb=ptb, on=on, q=q, ec=ec: e.transpose(
                                out=ptb[:, q * 128:(q + 1) * 128], in_=on.t[:, ec * 128:(ec + 1) * 128], identity=self.identb.t[:, :]),
                                [on.s(e4), self.identb], [pt], inc=(q == 3))
                        S.I("act", lambda e, ptb=ptb, e4=e4, pr=pr: e.copy(
                            out=ogT.t[:, e4 * 4:(e4 + 1) * 4, pr * 128:(pr + 1) * 128],
                            in_=ptb[:, 0:512].rearrange("p (q t) -> p q t", q=4)), [pt], [ogT])
                self.st(self.st_d, U.t[:, :, :], U)
                if self.debug.get("dump"):
                    self.st(self.dbg_og[:, tok0:tok0 + TB].rearrange("(k p) t -> p k t", p=128), ogT.t[:, :, :], ogT)
                S.barrier()
            cw, cb = self.vec("conv_w", l), self.vec("conv_b", l)
            lng, lnb = self.vec("conv_ln_g", l), self.vec("conv_ln_b", l)
            for h in range(NH):
                t0 = tok0 + h * 512
                with ExitStack() as es3:
                    cn = self.sb(es3, "m2_cn", [128, 16, 512], BF16)
                    with ExitStack() as es4:
                        co = self.sb(es4, "m2_co", [128, 16, 512], F32)
                        cin = [self.sb(es4, f"m2_ci{i}", [128, 30 + 512], F32) for i in range(4)]
                        cbf = [self.sb(es4, f"m2_cb{i}", [128, 512], BF16) for i in range(2)]
                        csq = [self.sb(es4, f"m2_cq{i}", [128, 512], BF16) for i in range(2)]
                        mt = self.sb(es4, "m2_mt", [128, 512], F32)
                        vr = self.sb(es4, "m2_vr", [128, 512], F32)
                        self.set_pools(st=[0, 1], c=range(2, 8))
                        ps_s, ps_q = self.pbank("st"), self.pbank("st")
                        cbl = [self.sb(es4, f"m2_cbh{i}", [128, 30 + 512], BF16) for i in range(2)]
                        dgl = [self.sb(es4, f"m2_dg{i}", [128, 31, 128], BF16) for i in range(2)]
                        for k in range(16):
                            ci_, cbh, dg = cin[k % 4], cbl[k % 2], dgl[k % 2]
                            self.ld(ci_.t[:, :], self.cT_d[k * 128:(k + 1) * 128, HALO + t0 - 30:HALO + t0 + 512], ci_)
                            S.I("dve", lambda e, ci_=ci_, cbh=cbh: e.tensor_copy(out=cbh.t[:, :], in_=ci_.t[:, :]), [ci_], [cbh])
                            for tap in range(31):
                                if tap % 2:
                                    S.I("act", lambda e, dg=dg, tap=tap, k=k: e.activation(
                                        out=dg.t[:, tap, :], in_=self.identb.t[:, :], func=AF.Copy,
                                        scale=cw[:, k * 31 + tap:k * 31 + tap + 1]), [self.identb, self.vecs], [dg.s(tap)])
                                else:
                                    S.I("dve", lambda e, dg=dg, tap=tap, k=k: e.tensor_scalar(
                                        out=dg.t[:, tap, :], in0=self.identb.t[:, :], scalar1=cw[:, k * 31 + tap:k * 31 + tap + 1],
                                        scalar2=None, op0=ALU.mult), [self.identb, self.vecs], [dg.s(tap)])
                            pc = self.pbank("c")
                            for tap in range(31):
                                S.I("pe", lambda e, pc=pc, dg=dg, cbh=cbh, tap=tap: e.matmul(
                                    out=pc.t[:, :], lhsT=dg.t[:, tap, :], rhs=cbh.t[:, tap:tap + 512],
                                    start=(tap == 0), stop=(tap == 30)), [dg.s(tap), cbh], [pc], inc=(tap == 30))
                            S.I("act", lambda e, pc=pc, k=k: e.activation(out=co.t[:, k, :], in_=pc.t[:, :], func=AF.Identity,
                                                                          bias=cb[:, k:k + 1], scale=1.0), [pc, self.vecs], [co.s(k)])
                            b1, b2 = cbf[k % 2], csq[k % 2]
                            S.I("dve", lambda e, k=k, b1=b1: e.tensor_copy(out=b1.t[:, :], in_=co.t[:, k, :]), [co.s(k)], [b1])
                            S.I("act", lambda e, k=k, b2=b2: e.activation(out=b2.t[:, :], in_=co.t[:, k, :], func=AF.Square),
                                [co.s(k)], [b2])
                            S.I("pe", lambda e, k=k, b1=b1: e.matmul(out=ps_s.t[:, :], lhsT=self.onesb.t[:, :], rhs=b1.t[:, :],
                                                                     start=(k == 0), stop=(k == 15)), [b1, self.onesb], [ps_s])
                            S.I("pe", lambda e, k=k, b2=b2: e.matmul(out=ps_q.t[:, :], lhsT=self.onesb.t[:, :], rhs=b2.t[:, :],
                                                                     start=(k == 0), stop=(k == 15)), [b2, self.onesb], [ps_q])
                        S.I("act", lambda e: e.activation(out=mt.t[:, :], in_=ps_s.t[:, :], func=AF.Copy, scale=1.0 / D), [ps_s], [mt])
                        S.I("dve", lambda e: e.tensor_tensor(out=vr.t[:, :], in0=mt.t[:, :], in1=mt.t[:, :], op=ALU.mult), [mt], [vr])
                        S.I("dve", lambda e: e.scalar_tensor_tensor(out=vr.t[:, :], in0=ps_q.t[:, :], scalar=1.0 / D, in1=vr.t[:, :],
                                                                    op0=ALU.mult, op1=ALU.subtract), [ps_q], [vr])
                        S.I("act", lambda e: e.activation(out=vr.t[:, :], in_=vr.t[:, :], func=AF.Sqrt, bias=self.vec("eps"), scale=1.0),
                            [self.vecs], [vr])
                        S.I("dve", lambda e: e.reciprocal(out=vr.t[:, :], in_=vr.t[:, :]), [], [vr])
                        for k in range(16):
                            S.I("dve", lambda e, k=k: e.tensor_tensor(out=co.t[:, k, :], in0=co.t[:, k, :], in1=mt.t[:, :],
                                                                      op=ALU.subtract), [mt], [co.s(k)])
                            S.I("dve", lambda e, k=k: e.tensor_tensor(out=co.t[:, k, :], in0=co.t[:, k, :], in1=vr.t[:, :],
                                                                      op=ALU.mult), [vr], [co.s(k)])
                            S.I("act", lambda e, k=k: e.activation(out=cn.t[:, k, :], in_=co.t[:, k, :], func=AF.Silu,
                                                                   scale=lng[:, k:k + 1], bias=lnb[:, k:k + 1]),
                                [co.s(k), self.vecs], [cn.s(k)])
                        S.barrier()
                    m = self.sb(es3, "m2_m", [128, 16, 512], BF16)
                    xt = self.sb(es3, "m2_x", [128, 16, 512], F32)
                    g0l = [self.sb(es3, f"m2_g0{i}", [128, 512], BF16) for i in range(2)]
                    g1l = [self.sb(es3, f"m2_g1{i}", [128, 512], BF16) for i in range(2)]
                    t1l = [self.sb(es3, f"m2_t1{i}", [128, 512], F32) for i in range(2)]
                    t2l = [self.sb(es3, f"m2_t2{i}", [128, 512], F32) for i in range(2)]
                    self.set_pools(a=range(8))
                    self.load_x(xt, blk, h, 1)
                    for s8 in range(8):
                        GP, GPb = self.wload("gla_proj", l, 0, 16, s8 * 256, 256)
                        CP, CPb = self.wload("conv_proj", l, 0, 16, s8 * 256, 256)
                        for c in range(2):
                            j = 2 * s8 + c
                            pa = self.fm_out(GP, GPb, c, ogT, 1, hoff=h)[0]
                            pb = self.fm_out(CP, CPb, c, cn, 1)[0]
                            g0, g1, t1, t2 = g0l[j % 2], g1l[j % 2], t1l[j % 2], t2l[j % 2]
                            self.ld(g0.t[:, :], self.g0T_d[j * 128:(j + 1) * 128, t0:t0 + 512], g0)
                            self.ld(g1.t[:, :], self.g1T_d[j * 128:(j + 1) * 128, t0:t0 + 512], g1)
                            S.I("dve", lambda e, t1=t1, pa=pa, g0=g0: e.tensor_tensor(out=t1.t[:, :], in0=pa.t[:, :], in1=g0.t[:, :],
                                                                                      op=ALU.mult), [pa, g0], [t1])
                            S.I("dve", lambda e, t2=t2, pb=pb, g1=g1: e.tensor_tensor(out=t2.t[:, :], in0=pb.t[:, :], in1=g1.t[:, :],
                                                                                      op=ALU.mult), [pb, g1], [t2])
                            S.I("dve", lambda e, t1=t1, t2=t2, j=j: e.tensor_tensor(out=m.t[:, j, :], in0=t1.t[:, :], in1=t2.t[:, :],
                                                                                    op=ALU.add), [t1, t2], [m.s(j, 0)])
                    if self.debug.get("dump"):
                        self.st(self.dbg_cn[:, t0:t0 + 512].rearrange("(k p) t -> p k t", p=128), cn.t[:, :, :], cn)
                        self.st(self.dbg_m[:, t0:t0 + 512].rearrange("(k p) t -> p k t", p=128), m.t[:, :, :], m)
                    for s8 in range(8):
                        WO, WOb = self.wload("mix_w_out", l, 0, 16, s8 * 256, 256)
                        for c in range(2):
                            j = 2 * s8 + c
                            py = self.fm_out(WO, WOb, c, m, 1)[0]
                            S.I("dve", lambda e, j=j, py=py: e.tensor_tensor(out=xt.t[:, j, :], in0=py.t[:, :], in1=xt.t[:, j, :],
                                                                             op=ALU.add), [py], [xt.s(j, 0)])
                    self.store_x(xt, blk, h, 1)
                    S.barrier()
            S.barrier()

    def stage_xattn(self, l, blk):
        S = self.S
        for h in range(NH):
            with ExitStack() as es:
                xt = self.sb(es, "xa_x", [128, 16, 512], F32)
                hT = self.sb(es, "xa_h", [128, 16, 512], BF16)
                sq = [self.sb(es, f"xa_sq{i}", [128, 512], BF16) for i in range(4)]
                rs = self.sb(es, "xa_rs", [128, 512], F32)
                qT = self.sb(es, "xa_q", [128, 4, 512], BF16)
                pTl = [self.sb(es, f"xa_pT{i}", [128, 2, 512], BF16) for i in range(2)]
                oT = self.sb(es, "xa_o", [128, 4, 512], BF16)
                pel = [self.sb(es, f"xa_pe{i}", [128, 256], F32) for i in range(2)]
                pnl = [self.sb(es, f"xa_pn{i}", [128, 256], BF16) for i in range(2)]
                sml = [self.sb(es, f"xa_sm{i}", [128, 4], F32) for i in range(4)]
                self.set_pools(a=range(0, 4), s=range(4, 6), t=range(6, 8))
                self.load_x(xt, blk, h, 1)
                self.make_hT(xt, hT, self.vec("xa_norm", l), 1, sq, rs)
                for s2 in range(2):
                    W, Wb = self.wload("xa_w_q", l, 0, 16, s2 * 256, 256)
                    for c in range(2):
                        hd = 2 * s2 + c
                        pb = self.fm_out(W, Wb, c, hT, 1)[0]
                        S.I("act", lambda e, pb=pb, hd=hd: e.activation(out=qT.t[:, hd, :], in_=pb.t[:, :], func=AF.Copy,
                                                                        scale=128 ** -0.5), [pb], [qT.s(hd)])
                it = 0
                for hd in range(4):
                    pT = pTl[hd % 2]
                    for j in range(4):
                        psb = self.pbank("s")
                        pe_, pn, sm = pel[it % 2], pnl[it % 2], sml[it % 4]
                        it += 1
                        S.I("pe", lambda e, psb=psb, hd=hd, j=j: e.matmul(out=psb.t[:, 0:256], lhsT=qT.t[:, hd, j * 128:(j + 1) * 128],
                                                                          rhs=self.kmT[l].t[:, hd, :], start=True, stop=True),
                            [qT.s(hd), self.kmT[l]], [psb])
                        S.I("dve", lambda e, psb=psb, sm=sm: e.tensor_reduce(out=sm.t[:, 0:1], in_=psb.t[:, 0:256], axis=AXX, op=ALU.max,
                                                                             negate=True), [psb], [sm])
                        S.I("act", lambda e, psb=psb, sm=sm, pe_=pe_: e.activation(out=pe_.t[:, :], in_=psb.t[:, 0:256], func=AF.Exp,
                                                                                   bias=sm.t[:, 0:1], scale=1.0, accum_out=sm.t[:, 1:2]),
                            [psb, sm], [pe_, sm])
                        S.I("dve", lambda e, sm=sm: e.reciprocal(out=sm.t[:, 2:3], in_=sm.t[:, 1:2]), [], [sm])
                        S.I("dve", lambda e, sm=sm, pe_=pe_, pn=pn: e.tensor_scalar(out=pn.t[:, :], in0=pe_.t[:, :], scalar1=sm.t[:, 2:3],
                                                                                    scalar2=None, op0=ALU.mult), [pe_, sm], [pn])
                        pt = self.pbank("t")
                        ptb = self.bview(pt)
                        for mc in range(2):
                            S.I("pe", lambda e, ptb=ptb, pn=pn, mc=mc: e.transpose(out=ptb[:, mc * 128:(mc + 1) * 128],
                                                                                   in_=pn.t[:, mc * 128:(mc + 1) * 128],
                                                                                   identity=self.identb.t[:, :]),
                                [pn, self.identb], [pt], inc=(mc == 1))
                        S.I("act", lambda e, ptb=ptb, pT=pT, j=j: e.copy(out=pT.t[:, :, j * 128:(j + 1) * 128],
                                                                         in_=ptb[:, 0:256].rearrange("p (m t) -> p m t", m=2)),
                            [pt], [pT])
                    po = self.pbank("a")
                    for mc in range(2):
                        S.I("pe", lambda e, po=po, mc=mc, hd=hd, pT=pT: e.matmul(out=po.t[:, :], lhsT=self.vm[l].t[:, mc, hd * 128:(hd + 1) * 128],
                                                                                 rhs=pT.t[:, mc, :], start=(mc == 0), stop=(mc == 1)),
                            [self.vm[l], pT], [po], inc=(mc == 1))
                    S.I("dve", lambda e, po=po, hd=hd: e.tensor_copy(out=oT.t[:, hd, :], in_=po.t[:, :]), [po], [oT.s(hd)])
                for s2 in range(2):
                    W, Wb = self.wload("xa_w_out", l, 0, 4, s2 * 1024, 1024)
                    for jj in range(8):
                        j = s2 * 8 + jj
                        py = self.pbank("a")
                        for kc in range(4):
                            S.I("pe", lambda e, py=py, W=W, kc=kc, jj=jj: e.matmul(out=py.t[:, :], lhsT=W[:, kc, jj * 128:(jj + 1) * 128],
                                                                                   rhs=oT.t[:, kc, :], start=(kc == 0), stop=(kc == 3)),
                                [Wb, oT.s(kc)], [py], inc=(kc == 3))
                        S.I("dve", lambda e, j=j, py=py: e.tensor_tensor(out=xt.t[:, j, :], in0=py.t[:, :], in1=xt.t[:, j, :], op=ALU.add),
                            [py], [xt.s(j, 0)])
                self.store_x(xt, blk, h, 1)
                S.barrier()

    def stage_ffn(self, l, blk, which):
        S = self.S
        wi, wo, gn = which + "_w_in", which + "_w_out", which + "_norm"
        with ExitStack() as es:
            xt = self.sb(es, "ffn_x", [128, 16, TB], F32)
            hT = self.sb(es, "ffn_h", [128, 16, TB], BF16)
            sq = [self.sb(es, f"ffn_sq{i}", [128, 512], BF16) for i in range(4)]
            rs = self.sb(es, "ffn_rs", [128, 512], F32)
            act = [self.sb(es, f"ffn_act{i}", [128, 4, TB], BF16) for i in range(2)]
            tmp = [self.sb(es, f"ffn_tmp{i}", [128, 512], F32) for i in range(2)]
            self.set_pools(a=range(0, 4), y=range(4, 8))
            self.load_x(xt, blk, 0, NH)
            self.make_hT(xt, hT, self.vec(gn, l), NH, sq, rs)
            ngrp = DFF // 512
            ti = 0
            for gi in range(ngrp):
                at = act[gi % 2]
                for sgi in range(2):
                    A, Ab = self.wload(wi, l, 0, 16, (2 * gi + sgi) * 256, 256)
                    B, Bb = self.wload(wi, l, 0, 16, DFF + (2 * gi + sgi) * 256, 256)
                    for c in range(2):
                        cc = 2 * sgi + c
                        pa = [self.pbank("a") for _ in range(NH)]
                        pb = [self.pbank("a") for _ in range(NH)]
                        for W, Wb, pp in ((A, Ab, pa), (B, Bb, pb)):
                            for k in range(16):
                                for h in range(NH):
                                    S.I("pe", lambda e, W=W, k=k, h=h, c=c, p=pp[h]: e.matmul(
                                        out=p.t[:, :], lhsT=W[:, k, c * 128:(c + 1) * 128], rhs=hT.t[:, k, h * 512:(h + 1) * 512],
                                        start=(k == 0), stop=(k == 15)), [Wb, hT.s(k, h)], [pp[h]], inc=(k == 15))
                        for h in range(NH):
                            tt = tmp[ti % 2]
                            ti += 1
                            S.I("act", lambda e, tt=tt, p=pa[h]: e.activation(out=tt.t[:, :], in_=p.t[:, :], func=AF.Silu),
                                [pa[h]], [tt])
                            S.I("dve", lambda e, tt=tt, p=pb[h], cc=cc, h=h, at=at: e.tensor_tensor(
                                out=at.t[:, cc, h * 512:(h + 1) * 512], in0=tt.t[:, :], in1=p.t[:, :], op=ALU.mult),
                                [tt, pb[h]], [at.s(cc, h)])
                Os = [self.wload(wo, l, (2 * gi + sgi) * 256, 2, 0, D) for sgi in range(2)]
                for j in range(16):
                    for h in range(NH):
                        py = self.pbank("y")
                        for kc in range(4):
                            O, Ob = Os[kc // 2]
                            S.I("pe", lambda e, O=O, kc=kc, j=j, h=h, py=py, at=at: e.matmul(
                                out=py.t[:, :], lhsT=O[:, kc % 2, j * 128:(j + 1) * 128], rhs=at.t[:, kc, h * 512:(h + 1) * 512],
                                start=(kc == 0), stop=(kc == 3)), [Ob, at.s(kc, h)], [py], inc=(kc == 3))
                        S.I("dve", lambda e, j=j, h=h, py=py: e.scalar_tensor_tensor(
                            out=xt.t[:, j, h * 512:(h + 1) * 512], in0=py.t[:, :], scalar=0.5,
                            in1=xt.t[:, j, h * 512:(h + 1) * 512], op0=ALU.mult, op1=ALU.add), [py], [xt.s(j, h)])
            self.store_x(xt, blk, 0, NH)
            S.barrier()

    def stage_final(self):
        S = self.S
        self.set_pools(a=range(8))
        with ExitStack() as es:
            xt = self.sb(es, "fin_x", [128, 16, 512], F32)
            sq = [self.sb(es, f"fin_sq{i}", [128, 512], BF16) for i in range(4)]
            rs = self.sb(es, "fin_rs", [128, 512], F32)
            yt = self.sb(es, "fin_y", [128, 16, 512], F32)
            ot = self.sb(es, "fin_o", [128, 4, D], F32)
            g = self.vec("final_norm")
            for gi in range(T // 512):
                S.D("sp", "dma_ld", xt.t[:, :, :], self.xT[:, gi * 512:(gi + 1) * 512].rearrange("(k p) t -> p k t", p=128),
                    reads=[self.dbuf(("xT", gi))], writes=[xt])
                self.rstd_bc(xt, 0, sq, rs)
                for k in range(16):
                    S.I("dve", lambda e, k=k: e.scalar_tensor_tensor(out=yt.t[:, k, :], in0=xt.t[:, k, :], scalar=g[:, k:k + 1],
                                                                     in1=rs.t[:, :], op0=ALU.mult, op1=ALU.mult),
                        [xt, rs, self.vecs], [yt])
                for j in range(4):
                    for kk in range(4):
                        pb = self.pbank("a")
                        for q in range(4):
                            k = kk * 4 + q
                            S.I("pe", lambda e, pb=pb, j=j, k=k, q=q: e.transpose(
                                out=pb.t[:, q * 128:(q + 1) * 128], in_=yt.t[:, k, j * 128:(j + 1) * 128], identity=self.ident),
                                [yt, self.cst], [pb], inc=(q == 3))
                        if kk % 2:
                            S.I("act", lambda e, pb=pb, j=j, kk=kk: e.copy(out=ot.t[:, j, kk * 512:(kk + 1) * 512], in_=pb.t[:, :]),
                                [pb], [ot])
                        else:
                            S.I("dve", lambda e, pb=pb, j=j, kk=kk: e.tensor_copy(out=ot.t[:, j, kk * 512:(kk + 1) * 512], in_=pb.t[:, :]),
                                [pb], [ot])
                S.D("sp", "dma_st", self.out_d[gi * 512:(gi + 1) * 512, :].rearrange("(j p) d -> p j d", p=128), ot.t[:, :, :],
                    reads=[ot], writes=[self.dbuf(("out", gi))])
            S.barrier()


VEC_ITEMS = [("ffn1_norm", 16), ("mix_norm", 16), ("xa_norm", 16), ("ffn2_norm", 16), ("xa_mem_norm", 16),
             ("conv_b", 16), ("conv_ln_g", 16), ("conv_ln_b", 16), ("gla_gate_b", 8), ("branch_gate_b", 32),
             ("conv_w", 16 * 31)]


def vec_layout():
    off = {}
    o = 0
    for l in range(L):
        for n, w in VEC_ITEMS:
            off[(n, l)] = (o, w)
            o += w
    for n, w in [("final_norm", 16), ("eps", 1), ("one", 1), ("rmask", 4), ("rmaskc", 4), ("ronehot", 4)]:
        off[(n, None)] = (o, w)
        o += w
    return off, o


def pm(v, k):
    return np.ascontiguousarray(np.asarray(v, np.float32).reshape(k, 128).T)


def build_vecs(inp, rank):
    off, n = vec_layout()
    out = np.zeros((128, n), np.float32)
    for l in range(L):
        for name, w in VEC_ITEMS:
            o, _ = off[(name, l)]
            if name == "conv_w":
                cw = np.asarray(inp["conv_w"][l], np.float32)
                out[:, o:o + w] = cw.reshape(31, 16, 128).transpose(2, 1, 0).reshape(128, 16 * 31)
            else:
                out[:, o:o + w] = pm(inp[name][l], w)
    o, _ = off[("final_norm", None)]
    out[:, o:o + 16] = pm(inp["final_norm"], 16)
    out[:, off[("eps", None)][0]] = EPS
    out[:, off[("one", None)][0]] = 1.0
    for p in range(4):
        out[:, off[("rmask", None)][0] + p] = 1.0 if p < rank else 0.0
        out[:, off[("rmaskc", None)][0] + p] = 0.0 if p < rank else 1.0
        out[:, off[("ronehot", None)][0] + p] = 1.0 if p == rank - 1 else 0.0
    return out


def build_cst():
    c = np.zeros((128, 128 + 64 + TB), np.float32)
    c[:, 0:128] = np.eye(128, dtype=np.float32)
    jj, ii = np.meshgrid(np.arange(64), np.arange(64), indexing="ij")
    c[0:64, 128:192] = (jj <= ii).astype(np.float32)
    cm = np.ones(TB, np.float32)
    cm[0::64] = 0.0
    c[:, 192:] = cm[None, :]
    return c


def pack_slabs(inp, specs):
    wall = np.zeros((max(1, len(specs)), 128, SLOT), np.float32)
    for i, (name, l, r0, nkc, c0, ncols) in enumerate(specs):
        w = inp[name][l]
        blk = np.asarray(w[r0:r0 + nkc * 128, c0:c0 + ncols], np.float32)
        wall[i, :, :nkc * ncols] = blk.reshape(nkc, 128, ncols).transpose(1, 0, 2).reshape(128, nkc * ncols)
    return wall


_CACHE = {}


def get_program(debug=None):
    key = repr(sorted((debug or {}).items()))
    if key not in _CACHE:
        plan = Prog(plan=None, debug=debug).build()
        prog = Prog(plan=plan, debug=debug)
        nc = prog.build()
        _CACHE[key] = (nc, plan)
    return _CACHE[key]


def kernel(debug=None, ncores=NCORE, **inp):
    nc, plan = get_program(debug)
    x = np.asarray(inp["x"], np.float32)
    mem = np.asarray(inp["mem"], np.float32)
    wall = pack_slabs(inp, plan["specs"])
    cst = build_cst()
    gnb = np.ascontiguousarray(np.broadcast_to(np.asarray(inp["gla_out_norm"], np.float32)[:, None, :], (L, 128, D)))
    gw2 = np.ascontiguousarray(np.asarray(inp["gla_gate_w2"], np.float32))
    in_maps = []
    for c in range(ncores):
        b, r = c // 4, c % 4
        in_maps.append({
            "x_in": np.ascontiguousarray(x[b, r * T:(r + 1) * T, :]),
            "mem_in": np.ascontiguousarray(mem[b]),
            "vecs": build_vecs(inp, r),
            "cst": cst, "gnb": gnb, "gw2": gw2, "wall": wall,
        })
    res = run_bass_kernel_spmd(nc, in_maps, core_ids=list(range(ncores)))
    if debug and debug.get("raw"):
        return res.results
    out = np.zeros((2, 8192, D), np.float32)
    for c in range(ncores):
        b, r = c // 4, c % 4
        out[b, r * T:(r + 1) * T, :] = np.asarray(res.results[c]["out"], np.float32)
    return out
```

```python
import numpy as np
import concourse.bass as bass
import concourse.mybir as mybir
from concourse.bass_utils import run_bass_kernel_spmd
from contextlib import ExitStack

F32 = mybir.dt.float32
BF16 = mybir.dt.bfloat16
AF = mybir.ActivationFunctionType
ALU = mybir.AluOpType
AXX = mybir.AxisListType.X

L = 2
D = 2048
DFF = 5632
DIN = 14352
NCORE = 8
T = 2048
TB = 1024
NH = TB // 512
NBLK = T // TB
NCHUNK = T // 64
HALO = 32
SLOT = 4096
NSLOT = 8
LOOKAHEAD = 6
EPS = 1e-6
C_Q, C_K, C_V, C_R, C_G, C_UA, C_UG, C_G0, C_G1 = 0, 1024, 2048, 4096, 4112, 6160, 8208, 10256, 12304


class Buf:
    __slots__ = ("name", "w", "r")

    def __init__(self, name=""):
        self.name = name
        self.w = None
        self.r = {}


class Tile:
    def __init__(self, t, name=""):
        self.t = t
        self.b = Buf(name)
        self.subs = {}

    def __getitem__(self, idx):
        return self.t[idx]

    def s(self, *key):
        if key not in self.subs:
            nb = Buf(f"{self.b.name}{key}")
            nb.w = self.b.w
            nb.r = dict(self.b.r)
            self.subs[key] = nb
        return self.subs[key]

    def all(self):
        return [self.b] + list(self.subs.values())


def flat(xs):
    out = []
    for x in xs:
        if isinstance(x, Tile):
            out.extend(x.all())
        elif isinstance(x, (list, tuple)):
            out.extend(flat(x))
        else:
            out.append(x)
    return out


class Sched:
    ENG = ("pe", "act", "dve", "pool", "sp")

    def __init__(self, sem_names, pools=None):
        self.cnt = {k: 0 for k in sem_names}
        self.pools = pools or {}
        self.pool_idx = {k: 0 for k in self.pools}
        self.known = {e: {} for e in self.ENG}
        self.stream = {e: [] for e in self.ENG}
        self.noinc = {e: False for e in self.ENG}

    def _deps(self, eng, reads, writes):
        deps = {}

        def add(k, v):
            if eng == "pe" and k == "pe":
                return
            if deps.get(k, 0) < v:
                deps[k] = v

        for b in reads:
            if b.w is not None:
                add(*b.w)
        for b in writes:
            if b.w is not None:
                add(*b.w)
            for k, v in b.r.items():
                add(k, v)
        return deps

    def waits(self, eng, deps):
        kn = self.known[eng]
        for k, v in deps.items():
            if kn.get(k, 0) < v:
                kn[k] = v
                self.stream[eng].append(("w", k, v))

    @staticmethod
    def _mark(tok, reads, writes):
        k, v = tok
        for b in reads:
            if b.r.get(k, 0) < v:
                b.r[k] = v
        for b in writes:
            b.w = tok
            b.r = {}

    def I(self, eng, fn, reads=(), writes=(), inc=True):
        reads = flat(reads)
        writes = flat(writes)
        self.waits(eng, self._deps(eng, reads, writes))
        if inc:
            self.cnt[eng] += 1
            v = self.cnt[eng]
            self.stream[eng].append(("i", fn, eng, 1))
            self.noinc[eng] = False
        else:
            v = self.cnt[eng] + 1
            self.stream[eng].append(("n", fn))
            self.noinc[eng] = True
        self._mark((eng, v), reads, writes)

    def X(self, eng, semk, amount, fn, reads=(), writes=()):
        reads = flat(reads)
        writes = flat(writes)
        self.waits(eng, self._deps(eng, reads, writes))
        self.cnt[semk] += amount
        self.stream[eng].append(("i", fn, semk, amount))
        self._mark((semk, self.cnt[semk]), reads, writes)
        return (semk, self.cnt[semk])

    def D(self, q, semk, out, in_, reads=(), writes=(), extra_deps=None, **kw):
        reads = flat(reads)
        writes = flat(writes)
        deps = self._deps(q, reads, writes)
        if semk in self.pools:
            pname = semk
            lst = self.pools[pname]
            semk = lst[self.pool_idx[pname] % len(lst)]
            self.pool_idx[pname] += 1
            if self.cnt[semk] > 0:
                deps[semk] = max(deps.get(semk, 0), self.cnt[semk])
        if extra_deps:
            for k, v in extra_deps.items():
                if deps.get(k, 0) < v:
                    deps[k] = v
        self.waits(q, deps)
        self.cnt[semk] += 16
        v = self.cnt[semk]
        self.stream[q].append(("i", lambda e: e.dma_start(out=out, in_=in_, **kw), semk, 16))
        self._mark((semk, v), reads, writes)
        return (semk, v)

    def barrier(self, engs=("pe", "act", "dve", "sp"), skip=("pool",), skip_w=True):
        toks = {k: v for k, v in self.cnt.items() if v > 0 and k not in skip and not (skip_w and k.startswith("wq"))}
        for e in engs:
            self.waits(e, toks)

    def replay(self, eng, e, sems):
        for it in self.stream[eng]:
            if it[0] == "w":
                e.wait_ge(sems[it[1]], it[2])
            elif it[0] == "i":
                it[1](e).then_inc(sems[it[2]], it[3])
            else:
                it[1](e)


class Prog:
    def __init__(self, plan=None, debug=None):
        self.plan = plan
        self.dry = plan is None
        self.debug = debug or {}
        self.slab_ids = {}
        self.slab_specs = []
        self.wseq = []
        self.wfree = []
        self.wtok = []
        self.w_emitted = 0
        self.w_req = 0

    def slab_id(self, spec):
        if spec not in self.slab_ids:
            self.slab_ids[spec] = len(self.slab_specs)
            self.slab_specs.append(spec)
        return self.slab_ids[spec]

    def _emit_wdma(self, n):
        sid = self.plan["seq"][n]
        slot = self.wslots[n % NSLOT]
        deps = self.plan["free"][n]
        self.S.D("pool", f"wq{n % NSLOT}", slot.t[:, :], self.wall[sid], extra_deps=deps)

    def wload(self, name, l, r0, nkc, c0, ncols):
        spec = (name, l, r0, nkc, c0, ncols)
        assert nkc * ncols <= SLOT
        sid = self.slab_id(spec)
        n = self.w_req
        self.w_req += 1
        slot = self.wslots[n % NSLOT]
        if self.dry:
            self.wseq.append(sid)
            free = {}
            if slot.b.w is not None:
                free[slot.b.w[0]] = slot.b.w[1]
            for k, v in slot.b.r.items():
                if free.get(k, 0) < v:
                    free[k] = v
            for k in list(free):
                if k.startswith("wq"):
                    free.pop(k)
            self.wfree.append(free)
            self.S.cnt[f"wq{n % NSLOT}"] += 16
        else:
            assert self.plan["seq"][n] == sid
            while self.w_emitted < min(len(self.plan["seq"]), n + 1 + LOOKAHEAD):
                self._emit_wdma(self.w_emitted)
                self.w_emitted += 1
        slot.b.w = (f"wq{n % NSLOT}", 16 * (n // NSLOT + 1))
        slot.b.r = {}
        view = slot.t[:, 0:nkc * ncols].rearrange("p (k c) -> p k c", k=nkc)
        return view, slot.b

    def sb(self, es, name, shape, dt):
        self.uid = getattr(self, "uid", 0) + 1
        name = f"{name}_{self.uid}"
        t = es.enter_context(self.nc.sbuf_tensor(name, list(shape), dt))
        return Tile(t, name)

    def pbank(self, pool):
        i = self.pidx[pool]
        self.pidx[pool] = (i + 1) % len(self.ppool[pool])
        return self.ps[self.ppool[pool][i]]

    def set_pools(self, **pools):
        self.ppool = {k: list(v) for k, v in pools.items()}
        self.pidx = {k: 0 for k in pools}

    def dbuf(self, key):
        if key not in self.dbufs:
            self.dbufs[key] = Buf(str(key))
        return self.dbufs[key]

    def vec(self, name, l=None):
        off, w = self.vec_off[(name, l)]
        return self.vecs.t[:, off:off + w]

    def build(self):
        nc = bass.Bass("TRN2", target_bir_lowering=False)
        self.nc = nc
        pools = {"dma_ld": [f"ld{i:02d}" for i in range(12)], "dma_st": [f"st{i:02d}" for i in range(8)]}
        sem_names = ["pe", "act", "dve", "pool", "sp", "cc"] + [f"wq{i}" for i in range(NSLOT)] + pools["dma_ld"] + pools["dma_st"]
        self.S = S = Sched(sem_names, pools)
        self.dbufs = {}
        dt = nc.dram_tensor
        self.x_in = dt("x_in", [T, D], F32, kind="ExternalInput").ap()
        self.mem_in = dt("mem_in", [256, D], F32, kind="ExternalInput").ap()
        self.vec_off, nvc = vec_layout()
        self.vecs_d = dt("vecs", [128, nvc], F32, kind="ExternalInput").ap()
        self.cst_d = dt("cst", [128, 128 + 64 + TB], F32, kind="ExternalInput").ap()
        self.gnb_d = dt("gnb", [L, 128, D], F32, kind="ExternalInput").ap()
        self.gw2_d = dt("gw2", [L, 16, 1024], F32, kind="ExternalInput").ap()
        nslab = max(1, self.plan["nslab"]) if not self.dry else 1
        self.wall = dt("wall", [nslab, 128, SLOT], F32, kind="ExternalInput").ap()
        self.out_d = dt("out", [T, D], F32, kind="ExternalOutput").ap()
        ik = "ExternalOutput" if self.debug.get("dump") else "Internal"
        self.xT = dt("xT", [D, T], F32, kind=ik).ap()
        if self.debug.get("upto", "all") not in ("s0", "none", "s0only"):
            self.qT_d = dt("qT_s", [1024, T], BF16, kind=ik).ap()
            self.kT_d = dt("kT_s", [1024, T], BF16, kind=ik).ap()
            self.kTok_d = dt("kTok_s", [T, 1024], BF16, kind=ik).ap()
            self.vTok_d = dt("vTok_s", [T, D], BF16, kind=ik).ap()
            self.sgn_d = dt("sgn_s", [T, D], BF16, kind=ik).ap()
            self.cT_d = dt("cT_s", [D, HALO + T], F32, kind=ik).ap()
            self.g0T_d = dt("g0T_s", [D, T], BF16, kind=ik).ap()
            self.g1T_d = dt("g1T_s", [D, T], BF16, kind=ik).ap()
            self.dec_d = dt("dec_s", [128, 8, NCHUNK], F32, kind=ik).ap()
            self.exS_in = [[dt(f"exS_in{l}_{q}", [512, 512], F32, kind="Internal").ap() for q in range(2)] for l in range(L)]
            self.exS_out = [[dt(f"exS_out{l}_{q}", [4 * 512, 512], F32, kind="Internal").ap() for q in range(2)] for l in range(L)]
            self.exM_in = [dt(f"exM_in{l}", [D + 128, 32], F32, kind="Internal").ap() for l in range(L)]
            self.exM_out = [dt(f"exM_out{l}", [4 * (D + 128), 32], F32, kind="Internal").ap() for l in range(L)]
            self.st_d = dt("st_s", [128, 8, 512], F32, kind=ik).ap()
        if self.debug.get("dump"):
            self.dbg_og = dt("dbg_og", [D, T], BF16, kind="ExternalOutput").ap()
            self.dbg_cn = dt("dbg_cn", [D, T], BF16, kind="ExternalOutput").ap()
            self.dbg_m = dt("dbg_m", [D, T], BF16, kind="ExternalOutput").ap()
        self.dbg_d = None
        if self.debug.get("dbg_shape"):
            self.dbg_d = dt("dbg", list(self.debug["dbg_shape"]), F32, kind="ExternalOutput").ap()

        with ExitStack() as es:
            self.sems = {k: es.enter_context(nc.semaphore(k)) for k in sem_names}
            self.ps = [Tile(es.enter_context(nc.psum_tensor(f"ps{i}", [128, 512], F32)), f"ps{i}") for i in range(8)]
            self.wslots = [self.sb(es, f"wslot{i}", [128, SLOT], BF16) for i in range(NSLOT)]
            self.vecs = self.sb(es, "vecs_sb", [128, nvc], F32)
            self.cst = self.sb(es, "cst_sb", [128, 128 + 64 + TB], F32)
            self.identb = self.sb(es, "identb", [128, 128], BF16)
            self.onesb = self.sb(es, "onesb", [128, 128], BF16)
            self.kmT = [self.sb(es, f"kmT{l}", [128, 4, 256], BF16) for l in range(L)]
            self.vm = [self.sb(es, f"vm{l}", [128, 2, 512], BF16) for l in range(L)]
            block = es.enter_context(nc.Block())
            self.set_pools(a=range(8))
            S.D("sp", "dma_ld", self.vecs.t[:, :], self.vecs_d, writes=[self.vecs])
            S.D("sp", "dma_ld", self.cst.t[:, :], self.cst_d, writes=[self.cst])
            self.ident = self.cst.t[:, 0:128]
            self.mask64 = self.cst.t[0:64, 128:192]
            self.cmask = self.cst.t[:, 192:192 + TB]
            S.I("dve", lambda e: e.tensor_copy(out=self.identb.t[:, :], in_=self.ident), [self.cst], [self.identb])
            S.I("dve", lambda e: e.memset(self.onesb.t[:, :], 1.0), [], [self.onesb])

            if self.debug.get("upto", "all") != "all":
                junk = self.sb(es, "junk", [128, 64], F32)
                for ap in (self.x_in[0:128, 0:8], self.mem_in[0:128, 0:8], self.gnb_d[0, :, 0:8], self.gw2_d[0, :, 0:8],
                           self.wall[0, :, 0:8]):
                    S.D("sp", "dma_ld", junk.t[0:ap.shape[0], 0:8], ap, writes=[junk])
            self.body(es)

            S.barrier(engs=("pe", "act", "dve", "sp", "pool"), skip=(), skip_w=False)
            if not self.dry:
                @block.sync
                def _(e):
                    S.replay("sp", e, self.sems)

                @block.tensor
                def _(e):
                    S.replay("pe", e, self.sems)

                @block.scalar
                def _(e):
                    S.replay("act", e, self.sems)

                @block.vector
                def _(e):
                    S.replay("dve", e, self.sems)

                @block.gpsimd
                def _(e):
                    S.replay("pool", e, self.sems)
        for e in Sched.ENG:
            assert not S.noinc[e], e
        if self.dry:
            return {"seq": self.wseq, "free": self.wfree, "nslab": len(self.slab_specs), "specs": self.slab_specs}
        return nc

    def body(self, es):
        upto = self.debug.get("upto", "all")
        if upto == "none":
            return
        self.stage_s0()
        if upto == "s0only":
            return
        if upto == "s0":
            return self.stage_final()
        if upto != "ffn1":
            self.stage_mem()
        nl = self.debug.get("layers", L)
        for l in range(nl):
            for blk in range(NBLK):
                self.stage_ffn(l, blk, "ffn1")
                if upto == "ffn1":
                    continue
                self.stage_mix1(l, blk)
            if upto == "ffn1":
                break
            self.stage_pass1(l)
            self.stage_exchange(l)
            for blk in range(NBLK):
                self.stage_mix2(l, blk)
                if upto == "mix2":
                    continue
                self.stage_xattn(l, blk)
                if upto == "xattn":
                    continue
                self.stage_ffn(l, blk, "ffn2")
            if upto in ("mix2", "xattn"):
                break
        self.stage_final()

    def stage_s0(self):
        S, nc = self.S, self.nc
        self.set_pools(a=range(8))
        with ExitStack() as es:
            xin = [self.sb(es, f"s0_x{i}", [128, 4, D], F32) for i in range(1)]
            xo = [self.sb(es, f"s0_o{i}", [128, 16, 512], F32) for i in range(2)]
            for g in range(T // 512):
                xi, xt = xin[0], xo[g % 2]
                S.D("sp", "dma_ld", xi.t[:, :, :], self.x_in[g * 512:(g + 1) * 512, :].rearrange("(j p) d -> p j d", p=128),
                    writes=[xi])
                for k in range(16):
                    pb = self.pbank("a")
                    for j in range(4):
                        S.I("pe", lambda e, pb=pb, xi=xi, j=j, k=k: e.transpose(
                            out=pb.t[:, j * 128:(j + 1) * 128], in_=xi.t[:, j, k * 128:(k + 1) * 128], identity=self.ident),
                            [xi, self.cst], [pb], inc=(j == 3))
                    eng = "act" if k % 2 else "dve"
                    if eng == "act":
                        S.I("act", lambda e, pb=pb, xt=xt, k=k: e.copy(out=xt.t[:, k, :], in_=pb.t[:, :]), [pb], [xt])
                    else:
                        S.I("dve", lambda e, pb=pb, xt=xt, k=k: e.tensor_copy(out=xt.t[:, k, :], in_=pb.t[:, :]), [pb], [xt])
                S.D("sp", "dma_st", self.xT[:, g * 512:(g + 1) * 512].rearrange("(k p) t -> p k t", p=128), xt.t[:, :, :],
                    reads=[xt], writes=[self.dbuf(("xT", g))])
            S.barrier()

    def rstd_bc(self, xt, h, sq, rs, nchunks=16, scale=1.0 / D):
        S = self.S
        pb = self.pbank("a")
        for k in range(nchunks):
            sqk = sq[k % len(sq)]
            S.I("act", lambda e, k=k, sqk=sqk: e.activation(out=sqk.t[:, :], in_=xt.t[:, k, h * 512:(h + 1) * 512], func=AF.Square),
                [xt.s(k, h)], [sqk])
            S.I("pe", lambda e, k=k, pb=pb, sqk=sqk: e.matmul(out=pb.t[:, :], lhsT=self.onesb.t[:, :], rhs=sqk.t[:, :],
                                                              start=(k == 0), stop=(k == nchunks - 1)),
                [sqk, self.onesb], [pb], inc=True)
        S.I("act", lambda e, pb=pb: e.activation(out=rs.t[:, :], in_=pb.t[:, :], func=AF.Sqrt, scale=scale,
                                                 bias=self.vec("eps")), [pb, self.vecs], [rs])
        S.I("dve", lambda e: e.reciprocal(out=rs.t[:, :], in_=rs.t[:, :]), [], [rs])


    def load_x(self, xt, blk, h0, nh):
        for h in range(nh):
            gi = blk * NH + h0 + h
            self.S.D("sp", "dma_ld", xt.t[:, :, h * 512:(h + 1) * 512],
                     self.xT[:, gi * 512:(gi + 1) * 512].rearrange("(k p) t -> p k t", p=128),
                     reads=[self.dbuf(("xT", gi))], writes=[xt.s(k, h) for k in range(16)])

    def store_x(self, xt, blk, h0, nh):
        for h in range(nh):
            gi = blk * NH + h0 + h
            self.S.D("sp", "dma_st", self.xT[:, gi * 512:(gi + 1) * 512].rearrange("(k p) t -> p k t", p=128),
                     xt.t[:, :, h * 512:(h + 1) * 512], reads=[xt.s(k, h) for k in range(16)], writes=[self.dbuf(("xT", gi))])

    def make_hT(self, xt, hT, g, nh, sq, rs, hoff=0):
        S = self.S
        for h in range(nh):
            self.rstd_bc(xt, h, sq, rs)
            ho = hoff + h
            for k in range(16):
                S.I("dve", lambda e, k=k, h=h, ho=ho: e.scalar_tensor_tensor(
                    out=hT.t[:, k, ho * 512:(ho + 1) * 512], in0=xt.t[:, k, h * 512:(h + 1) * 512], scalar=g[:, k:k + 1],
                    in1=rs.t[:, :], op0=ALU.mult, op1=ALU.mult), [xt.s(k, h), rs, self.vecs], [hT.s(k, ho)])

    def mm_group(self, pb, lhs_fn, rhs_fn, nk, reads, M=128, N=512, po=0):
        for k in range(nk):
            self.S.I("pe", lambda e, k=k: e.matmul(out=pb.t[po:po + M, 0:N], lhsT=lhs_fn(k), rhs=rhs_fn(k),
                                                  start=(k == 0), stop=(k == nk - 1)),
                     reads, [pb], inc=(k == nk - 1))


    def ld(self, dst_ap, src_ap, tile):
        self.S.D("sp", "dma_ld", dst_ap, src_ap, writes=[tile])

    def st(self, dst_ap, src_ap, tile):
        self.S.D("sp", "dma_st", dst_ap, src_ap, reads=[tile])

    def bview(self, pb):
        return pb.t[:, :].bitcast(BF16)

    def fm_out(self, W, Wb, c, hT, nh, pool="a", hoff=0):
        S = self.S
        pbs = [self.pbank(pool) for _ in range(nh)]
        for k in range(16):
            for h in range(nh):
                ho = hoff + h
                S.I("pe", lambda e, k=k, h=h, ho=ho, p=pbs[h]: e.matmul(
                    out=p.t[:, :], lhsT=W[:, k, c * 128:(c + 1) * 128], rhs=hT.t[:, k, ho * 512:(ho + 1) * 512],
                    start=(k == 0), stop=(k == 15)), [Wb, hT.s(k, ho)], [pbs[h]], inc=(k == 15))
        return pbs

    def stage_mem(self):
        S = self.S
        self.set_pools(a=range(8))
        with ExitStack() as es:
            mt = self.sb(es, "mem_x", [128, 2, D], F32)
            mn = self.sb(es, "mem_n", [128, 2, D], F32)
            junk = self.sb(es, "mem_j", [128, D], BF16)
            ssq = self.sb(es, "mem_ss", [128, 2], F32)
            mhT = [self.sb(es, f"mem_hT{l}", [128, 16, 256], BF16) for l in range(L)]
            self.ld(mt.t[:, :, :], self.mem_in.rearrange("(j p) d -> p j d", p=128), mt)
            for j in range(2):
                S.I("act", lambda e, j=j: e.activation(out=junk.t[:, :], in_=mt.t[:, j, :], func=AF.Square,
                                                        accum_out=ssq.t[:, j:j + 1]), [mt], [junk, ssq])
            S.I("dve", lambda e: e.tensor_scalar(out=ssq.t[:, :], in0=ssq.t[:, :], scalar1=1.0 / D, scalar2=EPS,
                                                 op0=ALU.mult, op1=ALU.add), [], [ssq])
            S.I("act", lambda e: e.activation(out=ssq.t[:, :], in_=ssq.t[:, :], func=AF.Sqrt), [], [ssq])
            S.I("dve", lambda e: e.reciprocal(out=ssq.t[:, :], in_=ssq.t[:, :]), [], [ssq])
            for j in range(2):
                S.I("dve", lambda e, j=j: e.tensor_scalar(out=mn.t[:, j, :], in0=mt.t[:, j, :], scalar1=ssq.t[:, j:j + 1],
                                                          scalar2=None, op0=ALU.mult), [mt, ssq], [mn])
            for k in range(16):
                pb = self.pbank("a")
                for j in range(2):
                    S.I("pe", lambda e, pb=pb, j=j, k=k: e.transpose(out=pb.t[:, j * 128:(j + 1) * 128],
                                                                     in_=mn.t[:, j, k * 128:(k + 1) * 128], identity=self.ident),
                        [mn, self.cst], [pb], inc=(j == 1))
                for l in range(L):
                    g = self.vec("xa_mem_norm", l)
                    S.I("dve", lambda e, pb=pb, l=l, k=k, g=g: e.tensor_scalar(out=mhT[l].t[:, k, :], in0=pb.t[:, 0:256],
                                                                              scalar1=g[:, k:k + 1], scalar2=None, op0=ALU.mult),
                        [pb, self.vecs], [mhT[l]])
            for l in range(L):
                for s2 in range(2):
                    W, Wb = self.wload("xa_w_kv", l, 0, 16, s2 * 256, 256)
                    for c in range(2):
                        hd = 2 * s2 + c
                        pb = self.pbank("a")
                        for k in range(16):
                            S.I("pe", lambda e, W=W, k=k, c=c, pb=pb, l=l: e.matmul(
                                out=pb.t[:, 0:256], lhsT=W[:, k, c * 128:(c + 1) * 128], rhs=mhT[l].t[:, k, :],
                                start=(k == 0), stop=(k == 15)), [Wb, mhT[l]], [pb], inc=(k == 15))
                        S.I("act", lambda e, pb=pb, l=l, hd=hd: e.copy(out=self.kmT[l].t[:, hd, :], in_=pb.t[:, 0:256]),
                            [pb], [self.kmT[l]])
                for s2 in range(2):
                    W, Wb = self.wload("xa_w_kv", l, 0, 16, 512 + s2 * 256, 256)
                    for mc in range(2):
                        pb = self.pbank("a")
                        for k in range(16):
                            S.I("pe", lambda e, W=W, k=k, mc=mc, pb=pb, l=l: e.matmul(
                                out=pb.t[:, 0:256], lhsT=mhT[l].t[:, k, mc * 128:(mc + 1) * 128], rhs=W[:, k, :],
                                start=(k == 0), stop=(k == 15)), [Wb, mhT[l]], [pb], inc=(k == 15))
                        S.I("act", lambda e, pb=pb, l=l, mc=mc, s2=s2: e.copy(
                            out=self.vm[l].t[:, mc, s2 * 256:(s2 + 1) * 256], in_=pb.t[:, 0:256]), [pb], [self.vm[l]])
            S.barrier()

    def stage_mix1(self, l, blk):
        S = self.S
        tok0 = blk * TB
        NT = TB // 128
        with ExitStack() as es:
            hT = self.sb(es, "m1_h", [128, 16, TB], BF16)
            ebq = self.sb(es, "m1_ebq", [128, 8, TB], BF16)
            ebk = self.sb(es, "m1_ebk", [128, 8, TB], BF16)
            self.set_pools(a=range(0, 6), t=range(6, 8))
            with ExitStack() as es2:
                xt = self.sb(es2, "m1_x", [128, 16, 512], F32)
                sq = [self.sb(es2, f"m1_sq{i}", [128, 512], BF16) for i in range(4)]
                rs = self.sb(es2, "m1_rs", [128, 512], F32)
                for h in range(NH):
                    self.load_x(xt, blk, h, 1)
                    self.make_hT(xt, hT, self.vec("mix_norm", l), 1, sq, rs, hoff=h)
                S.barrier()
            gw2 = self.sb(es, "m1_gw2", [16, 1024], F32)
            gnb = self.sb(es, "m1_gnb", [128, D], F32)
            ngb = self.sb(es, "m1_ngb", [128, 8], F32)
            rT = self.sb(es, "m1_rT", [16, TB], F32)
            dect = self.sb(es, "m1_dec", [128, 8, TB // 64], F32)
            self.ld(gw2.t[:, :], self.gw2_d[l], gw2)
            self.ld(gnb.t[:, :], self.gnb_d[l], gnb)
            S.I("dve", lambda e: e.tensor_scalar(out=ngb.t[:, :], in0=self.vec("gla_gate_b", l), scalar1=-1.0, scalar2=None,
                                                 op0=ALU.mult), [self.vecs], [ngb])
            R, Rb = self.wload("mix_w_in", l, 0, 16, C_R, 16)
            for h in range(NH):
                pb = self.pbank("a")
                for k in range(16):
                    S.I("pe", lambda e, k=k, h=h, pb=pb: e.matmul(out=pb.t[0:16, :], lhsT=R[:, k, 0:16],
                                                                  rhs=hT.t[:, k, h * 512:(h + 1) * 512],
                                                                  start=(k == 0), stop=(k == 15)), [Rb, hT], [pb], inc=(k == 15))
                S.I("act", lambda e, h=h, pb=pb: e.copy(out=rT.t[0:16, h * 512:(h + 1) * 512], in_=pb.t[0:16, :]), [pb], [rT])
            with ExitStack() as es2:
                vtl = [self.sb(es2, f"m1_vt{i}", [128, NT, 256], BF16) for i in range(2)]
                stl = [self.sb(es2, f"m1_st{i}", [128, 2, 256], F32) for i in range(2)]
                spt = [self.sb(es2, f"m1_sp{i}", [128, TB], F32) for i in range(2)]
                cum = [self.sb(es2, f"m1_cum{i}", [128, TB], F32) for i in range(2)]
                et = [self.sb(es2, f"m1_et{i}", [128, 512], F32) for i in range(2)]
                def decay(dkc):
                    sp, cm = spt[dkc % 2], cum[dkc % 2]
                    for h in range(NH):
                        pb = self.pbank("a")
                        ee = et[h % 2]
                        S.I("pe", lambda e, dkc=dkc, h=h, pb=pb: e.matmul(
                            out=pb.t[:, :], lhsT=gw2.t[0:16, dkc * 128:(dkc + 1) * 128], rhs=rT.t[0:16, h * 512:(h + 1) * 512],
                            start=True, stop=True), [gw2, rT], [pb])
                        S.I("act", lambda e, dkc=dkc, pb=pb, ee=ee: e.activation(out=ee.t[:, :], in_=pb.t[:, :], func=AF.Exp,
                                                                                 scale=-1.0, bias=ngb.t[:, dkc:dkc + 1]),
                            [pb, ngb], [ee])
                        S.I("act", lambda e, h=h, sp=sp, ee=ee: e.activation(out=sp.t[:, h * 512:(h + 1) * 512], in_=ee.t[:, :],
                                                                             func=AF.Ln, scale=1.0, bias=self.vec("one")),
                            [ee, self.vecs], [sp])
                    S.I("dve", lambda e, sp=sp, cm=cm: e.tensor_tensor_scan(out=cm.t[:, :], data0=self.cmask, data1=sp.t[:, :],
                                                                            initial=0.0, op0=ALU.mult, op1=ALU.add),
                        [sp, self.cst], [cm])
                    S.I("act", lambda e, cm=cm, dkc=dkc: e.activation(out=ebq.t[:, dkc, :], in_=cm.t[:, :], func=AF.Exp,
                                                                      scale=-1.0 / 16), [cm], [ebq.s(dkc)])
                    S.I("act", lambda e, cm=cm, dkc=dkc: e.activation(out=ebk.t[:, dkc, :], in_=cm.t[:, :], func=AF.Exp,
                                                                      scale=1.0 / 16), [cm], [ebk.s(dkc)])
                    S.I("act", lambda e, cm=cm, dkc=dkc: e.activation(
                        out=dect.t[:, dkc, :], in_=cm.t[:, :].rearrange("p (c t) -> p c t", t=64)[:, :, 63], func=AF.Exp,
                        scale=-1.0 / 16), [cm], [dect])

                ti = 0
                for which in ("v", "g"):
                    cbase = C_V if which == "v" else C_G
                    dst = self.vTok_d if which == "v" else self.sgn_d
                    for s8 in range(8):
                        if which == "v":
                            decay(s8)
                        W, Wb = self.wload("mix_w_in", l, 0, 16, cbase + s8 * 256, 256)
                        vt = vtl[s8 % 2]
                        for jp in range(NT // 2):
                            pb = self.pbank("a")
                            for jj in range(2):
                                j = 2 * jp + jj
                                for k in range(16):
                                    S.I("pe", lambda e, W=W, k=k, j=j, jj=jj, pb=pb: e.matmul(
                                        out=pb.t[:, jj * 256:(jj + 1) * 256], lhsT=hT.t[:, k, j * 128:(j + 1) * 128], rhs=W[:, k, :],
                                        start=(k == 0), stop=(k == 15)), [Wb, hT], [pb], inc=(k == 15))
                            if which == "v":
                                S.I("act", lambda e, vt=vt, jp=jp, pb=pb: e.copy(
                                    out=vt.t[:, 2 * jp:2 * jp + 2, :], in_=pb.t[:, :].rearrange("p (j c) -> p j c", j=2)),
                                    [pb], [vt.s(jp)])
                            else:
                                tt = stl[ti % 2]
                                ti += 1
                                S.I("act", lambda e, tt=tt, pb=pb: e.activation(
                                    out=tt.t[:, :, :], in_=pb.t[:, :].rearrange("p (j c) -> p j c", j=2), func=AF.Silu), [pb], [tt])
                                for jj in range(2):
                                    S.I("dve", lambda e, tt=tt, vt=vt, jp=jp, jj=jj, s8=s8: e.tensor_tensor(
                                        out=vt.t[:, 2 * jp + jj, :], in0=tt.t[:, jj, :], in1=gnb.t[:, s8 * 256:(s8 + 1) * 256],
                                        op=ALU.mult), [tt, gnb], [vt.s(jp)])
                        self.st(dst[tok0:tok0 + TB, s8 * 256:(s8 + 1) * 256].rearrange("(j p) c -> p j c", p=128), vt.t[:, :, :], vt)
                    if which == "v":
                        self.st(self.dec_d[:, :, blk * (TB // 64):(blk + 1) * (TB // 64)], dect.t[:, :, :], dect)
                S.barrier()
            with ExitStack() as es2:
                ob = [self.sb(es2, f"m1_ob{i}", [128, 512], BF16) for i in range(4)]
                ktl = [self.sb(es2, f"m1_kt{i}", [128, 4, 128], BF16) for i in range(2)]
                oi = 0
                for which in ("q", "k"):
                    cbase = C_Q if which == "q" else C_K
                    for s4 in range(4):
                        W, Wb = self.wload("mix_w_in", l, 0, 16, cbase + s4 * 256, 256)
                        for c in range(2):
                            dkc = 2 * s4 + c
                            pbs = self.fm_out(W, Wb, c, hT, NH)
                            for h in range(NH):
                                o = ob[oi % 4]
                                oi += 1
                                cols = slice(tok0 + h * 512, tok0 + (h + 1) * 512)
                                if which == "q":
                                    S.I("dve", lambda e, o=o, p=pbs[h], dkc=dkc, h=h: e.scalar_tensor_tensor(
                                        out=o.t[:, :], in0=p.t[:, :], scalar=0.0625, in1=ebq.t[:, dkc, h * 512:(h + 1) * 512],
                                        op0=ALU.mult, op1=ALU.mult), [pbs[h], ebq.s(dkc)], [o])
                                    self.st(self.qT_d[dkc * 128:(dkc + 1) * 128, cols], o.t[:, :], o)
                                else:
                                    S.I("dve", lambda e, o=o, p=pbs[h], dkc=dkc, h=h: e.tensor_tensor(
                                        out=o.t[:, :], in0=p.t[:, :], in1=ebk.t[:, dkc, h * 512:(h + 1) * 512], op=ALU.mult),
                                        [pbs[h], ebk.s(dkc)], [o])
                                    self.st(self.kT_d[dkc * 128:(dkc + 1) * 128, cols], o.t[:, :], o)
                                    pt = self.pbank("t")
                                    ptb = self.bview(pt)
                                    for j in range(4):
                                        S.I("pe", lambda e, ptb=ptb, o=o, j=j: e.transpose(
                                            out=ptb[:, j * 128:(j + 1) * 128], in_=o.t[:, j * 128:(j + 1) * 128],
                                            identity=self.identb.t[:, :]), [o, self.identb], [pt], inc=(j == 3))
                                    kt = ktl[oi % 2]
                                    S.I("act", lambda e, kt=kt, ptb=ptb: e.copy(
                                        out=kt.t[:, :, :], in_=ptb[:, 0:512].rearrange("p (j d) -> p j d", j=4)), [pt], [kt])
                                    self.st(self.kTok_d[tok0 + h * 512:tok0 + (h + 1) * 512, dkc * 128:(dkc + 1) * 128]
                                            .rearrange("(j p) d -> p j d", p=128), kt.t[:, :, :], kt)
                S.barrier()
            with ExitStack() as es2:
                ctl = [self.sb(es2, f"m1_ct{i}", [128, 512], F32) for i in range(4)]
                sgt = [self.sb(es2, f"m1_sg{i}", [128, 512], F32) for i in range(2)]
                gtl = [self.sb(es2, f"m1_gt{i}", [128, 512], BF16) for i in range(4)]
                ci = 0
                for s8 in range(8):
                    UA, UAb = self.wload("mix_w_in", l, 0, 16, C_UA + s8 * 256, 256)
                    UG, UGb = self.wload("mix_w_in", l, 0, 16, C_UG + s8 * 256, 256)
                    for c in range(2):
                        ch = 2 * s8 + c
                        pa = self.fm_out(UA, UAb, c, hT, NH)
                        pg = self.fm_out(UG, UGb, c, hT, NH)
                        for h in range(NH):
                            sg, ct = sgt[ci % 2], ctl[ci % 4]
                            ci += 1
                            S.I("act", lambda e, sg=sg, p=pg[h]: e.activation(out=sg.t[:, :], in_=p.t[:, :], func=AF.Sigmoid),
                                [pg[h]], [sg])
                            S.I("dve", lambda e, sg=sg, ct=ct, p=pa[h]: e.tensor_tensor(out=ct.t[:, :], in0=sg.t[:, :], in1=p.t[:, :],
                                                                                          op=ALU.mult), [sg, pa[h]], [ct])
                            self.st(self.cT_d[ch * 128:(ch + 1) * 128, HALO + tok0 + h * 512:HALO + tok0 + (h + 1) * 512],
                                    ct.t[:, :], ct)
                bgb = self.vec("branch_gate_b", l)
                for gi_, (cb_, dst) in enumerate(((C_G0, self.g0T_d), (C_G1, self.g1T_d))):
                    for s8 in range(8):
                        W, Wb = self.wload("mix_w_in", l, 0, 16, cb_ + s8 * 256, 256)
                        for c in range(2):
                            ch = 2 * s8 + c
                            pbs = self.fm_out(W, Wb, c, hT, NH)
                            for h in range(NH):
                                gt = gtl[ci % 4]
                                ci += 1
                                S.I("act", lambda e, gt=gt, p=pbs[h], ch=ch, gi_=gi_: e.activation(
                                    out=gt.t[:, :], in_=p.t[:, :], func=AF.Sigmoid, bias=bgb[:, gi_ * 16 + ch:gi_ * 16 + ch + 1],
                                    scale=1.0), [pbs[h], self.vecs], [gt])
                                self.st(dst[ch * 128:(ch + 1) * 128, tok0 + h * 512:tok0 + (h + 1) * 512], gt.t[:, :], gt)
                S.barrier()
            S.barrier()

    def stage_pass1(self, l):
        S = self.S
        self.set_pools(a=range(8))
        with ExitStack() as es:
            U = self.sb(es, "p1_U", [128, 8, 512], F32)
            dec = self.sb(es, "p1_dec", [128, 8, NCHUNK], F32)
            ktl = [self.sb(es, f"p1_k{i}", [64, 1024], BF16) for i in range(3)]
            vtl = [self.sb(es, f"p1_v{i}", [64, D], BF16) for i in range(3)]
            dtot = self.sb(es, "p1_dt", [128, 8], F32)
            tail = self.sb(es, "p1_tail", [128, 16, 30], F32)
            self.ld(dec.t[:, :, :], self.dec_d, dec)
            for c in range(NCHUNK):
                kt, vt = ktl[c % 3], vtl[c % 3]
                self.ld(kt.t[:, :], self.kTok_d[c * 64:(c + 1) * 64, :], kt)
                self.ld(vt.t[:, :], self.vTok_d[c * 64:(c + 1) * 64, :], vt)
                for i in range(8):
                    hd, half = i // 2, i % 2
                    pb = self.pbank("a")
                    S.I("pe", lambda e, kt=kt, vt=vt, hd=hd, half=half, pb=pb: e.matmul(
                        out=pb.t[:, :], lhsT=kt.t[0:64, hd * 256 + half * 128:hd * 256 + (half + 1) * 128],
                        rhs=vt.t[0:64, hd * 512:(hd + 1) * 512], start=True, stop=True), [kt, vt], [pb])
                    if c == 0:
                        S.I("dve", lambda e, i=i, pb=pb: e.tensor_copy(out=U.t[:, i, :], in_=pb.t[:, :]), [pb], [U.s(i)])
                    else:
                        S.I("dve", lambda e, i=i, pb=pb, c=c: e.scalar_tensor_tensor(
                            out=U.t[:, i, :], in0=U.t[:, i, :], scalar=dec.t[:, i, c - 1:c], in1=pb.t[:, :],
                            op0=ALU.mult, op1=ALU.add), [pb, dec], [U.s(i)])
            for i in range(8):
                S.I("dve", lambda e, i=i: e.tensor_scalar(out=U.t[:, i, :], in0=U.t[:, i, :], scalar1=dec.t[:, i, NCHUNK - 1:NCHUNK],
                                                          scalar2=None, op0=ALU.mult), [dec], [U.s(i)])
            for q in range(2):
                self.st(self.exS_in[l][q].rearrange("(i p) e -> p i e", p=128), U.t[:, 4 * q:4 * q + 4, :], U)
            S.I("dve", lambda e: e.tensor_reduce(out=dtot.t[:, :], in_=dec.t[:, :, :], axis=AXX, op=ALU.mult), [dec], [dtot])
            self.st(self.exM_in[l][D:D + 128, 0:8], dtot.t[:, :], dtot)
            self.ld(tail.t[:, :, :], self.cT_d[:, HALO + T - 30:HALO + T].rearrange("(k p) c -> p k c", p=128), tail)
            self.st(self.exM_in[l][0:D, 0:30].rearrange("(k p) c -> p k c", p=128), tail.t[:, :, :], tail)
            S.barrier()

    def _exchange_zero(self):
        S = self.S
        with ExitStack() as es:
            Sin = self.sb(es, "ex_S", [128, 8, 512], F32)
            hl = self.sb(es, "ex_h", [128, 16, 30], F32)
            S.I("dve", lambda e: e.memset(Sin.t[:, :, :], 0.0), [], [Sin])
            S.I("dve", lambda e: e.memset(hl.t[:, :, :], 0.0), [], [hl])
            self.st(self.st_d, Sin.t[:, :, :], Sin)
            self.st(self.cT_d[:, 2:32].rearrange("(k q) c -> q k c", q=128), hl.t[:, :, :], hl)
            S.barrier()

    def stage_exchange(self, l):
        S = self.S
        groups = [[0, 1, 2, 3], [4, 5, 6, 7]]
        noexch = bool(self.debug.get("noexch"))
        if noexch:
            return self._exchange_zero()
        S.waits("pool", {k: v for k, v in S.cnt.items() if v > 0 and (k in ("dve", "act", "pe", "sp") or k.startswith("st"))})
        pairs = [(self.exS_in[l][q], self.exS_out[l][q]) for q in range(2)] + [(self.exM_in[l], self.exM_out[l])]
        for src, dst in pairs:
            S.X("pool", "cc", 1, lambda e, src=src, dst=dst: e.collective_compute(
                "AllGather", ALU.bypass, replica_groups=groups, ins=[src], outs=[dst]))
        ccv = {"cc": S.cnt["cc"]}
        for eng in ("sp", "dve", "act", "pe"):
            S.waits(eng, ccv)
        R = D + 128
        rmask, rmaskc, roh = self.vec("rmask"), self.vec("rmaskc"), self.vec("ronehot")
        with ExitStack() as es:
            Dp = self.sb(es, "ex_D", [128, 4, 8], F32)
            av = self.sb(es, "ex_a", [128, 4, 8], F32)
            Sin = self.sb(es, "ex_S", [128, 8, 512], F32)
            tl = [self.sb(es, f"ex_t{i}", [128, 8, 512], F32) for i in range(2)]
            hl = self.sb(es, "ex_h", [128, 16, 30], F32)
            ht = [self.sb(es, f"ex_ht{i}", [128, 16, 30], F32) for i in range(2)]
            for p in range(3):
                self.ld(Dp.t[:, p, :], self.exM_out[l][p * R + D:p * R + D + 128, 0:8], Dp)
                S.I("dve", lambda e, p=p: e.tensor_scalar(out=av.t[:, p, :], in0=Dp.t[:, p, :], scalar1=rmask[:, p:p + 1],
                                                          scalar2=rmaskc[:, p:p + 1], op0=ALU.mult, op1=ALU.add),
                    [Dp, self.vecs], [av])
            S.I("dve", lambda e: e.memset(Sin.t[:, :, :], 0.0), [], [Sin])
            S.I("dve", lambda e: e.memset(hl.t[:, :, :], 0.0), [], [hl])
            for p in range(3):
                t = tl[p % 2]
                for q in range(2):
                    self.ld(t.t[:, 4 * q:4 * q + 4, :], self.exS_out[l][q][p * 512:(p + 1) * 512, :].rearrange("(i q) e -> q i e", q=128), t)
                for i in range(8):
                    S.I("dve", lambda e, t=t, i=i, p=p: e.tensor_scalar(out=t.t[:, i, :], in0=t.t[:, i, :], scalar1=rmask[:, p:p + 1],
                                                                        scalar2=None, op0=ALU.mult), [self.vecs], [t.s(i)])
                    S.I("dve", lambda e, t=t, i=i, p=p: e.scalar_tensor_tensor(
                        out=Sin.t[:, i, :], in0=Sin.t[:, i, :], scalar=av.t[:, p, i:i + 1], in1=t.t[:, i, :],
                        op0=ALU.mult, op1=ALU.add), [t.s(i), av], [Sin.s(i)])
                h = ht[p % 2]
                self.ld(h.t[:, :, :], self.exM_out[l][p * R:p * R + D, 0:30].rearrange("(k q) c -> q k c", q=128), h)
                S.I("dve", lambda e, h=h, p=p: e.scalar_tensor_tensor(out=hl.t[:, :, :], in0=h.t[:, :, :], scalar=roh[:, p:p + 1],
                                                                      in1=hl.t[:, :, :], op0=ALU.mult, op1=ALU.add),
                    [h, self.vecs], [hl])
            self.st(self.st_d, Sin.t[:, :, :], Sin)
            self.st(self.cT_d[:, 2:32].rearrange("(k q) c -> q k c", q=128), hl.t[:, :, :], hl)
            S.barrier()


    def stage_mix2(self, l, blk):
        S = self.S
        tok0 = blk * TB
        c0 = blk * (TB // 64)
        with ExitStack() as es:
            ogT = self.sb(es, "m2_og", [128, 16, TB], BF16)
            with ExitStack() as es2:
                U = self.sb(es2, "m2_U", [128, 8, 512], F32)
                Sbfl = [self.sb(es2, f"m2_Sbf{i}", [128, 8, 512], BF16) for i in range(2)]
                dec = self.sb(es2, "m2_dec", [128, 8, NCHUNK], F32)
                qpl = [self.sb(es2, f"m2_q{i}", [128, 8, 128], BF16) for i in range(2)]
                kpl = [self.sb(es2, f"m2_k{i}", [128, 8, 128], BF16) for i in range(2)]
                ktl = [self.sb(es2, f"m2_kt{i}", [64, 2, 1024], BF16) for i in range(2)]
                vtl = [self.sb(es2, f"m2_vt{i}", [64, 2, D], BF16) for i in range(2)]
                sgl = [self.sb(es2, f"m2_sg{i}", [128, D], BF16) for i in range(2)]
                onl = [self.sb(es2, f"m2_on{i}", [128, D], BF16) for i in range(2)]
                scl = [self.sb(es2, f"m2_sc{i}", [64, 4, 64], BF16) for i in range(4)]
                junk = self.sb(es2, "m2_junk", [128, 512], BF16)
                ssl = [self.sb(es2, f"m2_ss{i}", [128, 4], F32) for i in range(2)]
                self.set_pools(s=[4], kv=[5, 6, 7], t=[4])
                po = [self.ps[hd] for hd in range(4)]
                self.ld(dec.t[:, :, :], self.dec_d, dec)
                self.ld(U.t[:, :, :], self.st_d, U)
                Sbf = Sbfl[(c0 - 1) % 2]
                for i in range(8):
                    if blk == 0:
                        S.I("act", lambda e, i=i, Sbf=Sbf: e.copy(out=Sbf.t[:, i, :], in_=U.t[:, i, :]), [U], [Sbf.s(i)])
                    else:
                        S.I("act", lambda e, i=i, Sbf=Sbf: e.activation(out=Sbf.t[:, i, :], in_=U.t[:, i, :], func=AF.Identity,
                                                                        scale=dec.t[:, i, c0 - 1:c0]), [U, dec], [Sbf.s(i)])
                sci = 0
                for pr in range(TB // 128):
                    qp, kp, kt, vt, sg, on, ss = qpl[pr % 2], kpl[pr % 2], ktl[pr % 2], vtl[pr % 2], sgl[pr % 2], onl[pr % 2], ssl[pr % 2]
                    t0 = tok0 + pr * 128
                    self.ld(qp.t[:, :, :], self.qT_d[:, t0:t0 + 128].rearrange("(k p) t -> p k t", p=128), qp)
                    self.ld(kp.t[:, :, :], self.kT_d[:, t0:t0 + 128].rearrange("(k p) t -> p k t", p=128), kp)
                    self.ld(kt.t[:, :, :], self.kTok_d[t0:t0 + 128, :].rearrange("(c p) d -> p c d", p=64), kt)
                    self.ld(vt.t[:, :, :], self.vTok_d[t0:t0 + 128, :].rearrange("(c p) d -> p c d", p=64), vt)
                    self.ld(sg.t[:, :], self.sgn_d[t0:t0 + 128, :], sg)
                    for ci in range(2):
                        c = c0 + 2 * pr + ci
                        lo = ci * 64
                        Sprev, Snew = Sbfl[(c - 1) % 2], Sbfl[c % 2]
                        psb = self.pbank("s")
                        for hd in range(4):
                            for half in range(2):
                                dkc = 2 * hd + half
                                S.I("pe", lambda e, psb=psb, kp=kp, qp=qp, dkc=dkc, lo=lo, half=half, hd=hd: e.matmul(
                                    out=psb.t[0:64, hd * 64:(hd + 1) * 64], lhsT=kp.t[:, dkc, lo:lo + 64], rhs=qp.t[:, dkc, lo:lo + 64],
                                    start=(half == 0), stop=(half == 1)), [kp, qp], [psb], inc=(half == 1))
                        sc = scl[sci % 4]
                        sci += 1
                        S.I("dve", lambda e, sc=sc, psb=psb: e.tensor_tensor(
                            out=sc.t[:, :, :], in0=psb.t[0:64, 0:256].rearrange("p (h i) -> p h i", h=4),
                            in1=self.mask64.unsqueeze(1).to_broadcast([64, 4, 64]), op=ALU.mult), [psb, self.cst], [sc])
                        for hd in range(4):
                            for half in range(2):
                                i = 2 * hd + half
                                pkv = self.pbank("kv")
                                S.I("pe", lambda e, pkv=pkv, kt=kt, vt=vt, ci=ci, hd=hd, half=half: e.matmul(
                                    out=pkv.t[:, :], lhsT=kt.t[0:64, ci, hd * 256 + half * 128:hd * 256 + (half + 1) * 128],
                                    rhs=vt.t[0:64, ci, hd * 512:(hd + 1) * 512], start=True, stop=True), [kt, vt], [pkv])
                                if c == 0:
                                    S.I("dve", lambda e, i=i, pkv=pkv: e.tensor_tensor(out=U.t[:, i, :], in0=U.t[:, i, :], in1=pkv.t[:, :],
                                                                                       op=ALU.add), [pkv], [U.s(i)])
                                else:
                                    S.I("dve", lambda e, i=i, pkv=pkv, c=c: e.scalar_tensor_tensor(
                                        out=U.t[:, i, :], in0=U.t[:, i, :], scalar=dec.t[:, i, c - 1:c], in1=pkv.t[:, :],
                                        op0=ALU.mult, op1=ALU.add), [pkv, dec], [U.s(i)])
                                S.I("act", lambda e, i=i, c=c, Snew=Snew: e.activation(out=Snew.t[:, i, :], in_=U.t[:, i, :], func=AF.Identity,
                                                                                       scale=dec.t[:, i, c:c + 1]), [U.s(i), dec], [Snew.s(i)])
                        for hd in range(4):
                            pob = po[hd]
                            for half in range(2):
                                dkc = 2 * hd + half
                                S.I("pe", lambda e, pob=pob, qp=qp, dkc=dkc, lo=lo, half=half, Sprev=Sprev: e.matmul(
                                    out=pob.t[lo:lo + 64, :], lhsT=qp.t[:, dkc, lo:lo + 64], rhs=Sprev.t[:, dkc, :],
                                    start=(half == 0), stop=False), [qp, Sprev.s(dkc)], [pob], inc=False)
                            S.I("pe", lambda e, pob=pob, sc=sc, vt=vt, ci=ci, hd=hd, lo=lo: e.matmul(
                                out=pob.t[lo:lo + 64, :], lhsT=sc.t[:, hd, :], rhs=vt.t[0:64, ci, hd * 512:(hd + 1) * 512],
                                start=False, stop=True), [sc, vt], [pob], inc=True)
                    for hd in range(4):
                        S.I("act", lambda e, hd=hd, ss=ss: e.activation(out=junk.t[:, :], in_=po[hd].t[:, :], func=AF.Square,
                                                                        accum_out=ss.t[:, hd:hd + 1]), [po[hd]], [junk, ss])
                    S.I("dve", lambda e, ss=ss: e.tensor_scalar(out=ss.t[:, :], in0=ss.t[:, :], scalar1=1.0 / 512, scalar2=EPS,
                                                                op0=ALU.mult, op1=ALU.add), [], [ss])
                    S.I("act", lambda e, ss=ss: e.activation(out=ss.t[:, :], in_=ss.t[:, :], func=AF.Sqrt), [], [ss])
                    S.I("dve", lambda e, ss=ss: e.reciprocal(out=ss.t[:, :], in_=ss.t[:, :]), [], [ss])
                    for hd in range(4):
                        S.I("dve", lambda e, hd=hd, ss=ss, on=on, sg=sg: e.scalar_tensor_tensor(
                            out=on.t[:, hd * 512:(hd + 1) * 512], in0=po[hd].t[:, :], scalar=ss.t[:, hd:hd + 1],
                            in1=sg.t[:, hd * 512:(hd + 1) * 512], op0=ALU.mult, op1=ALU.mult), [po[hd], ss, sg], [on.s(hd)])
                    for e4 in range(4):
                        pt = self.pbank("t")
                        ptb = self.bview(pt)
                        for q in range(4):
                            ec = e4 * 4 + q
                            S.I("pe", lambda e, ptb=ptb, on=on, q=q, ec=ec: e.transpose(
                                out=ptb[:, q * 128:(q + 1) * 128], in_=on.t[:, ec * 128:(ec + 1) * 128], identity=self.identb.t[:, :]),
                                [on.s(e4), self.identb], [pt], inc=(q == 3))
                        S.I("act", lambda e, ptb=ptb, e4=e4, pr=pr: e.copy(
                            out=ogT.t[:, e4 * 4:(e4 + 1) * 4, pr * 128:(pr + 1) * 128],
                            in_=ptb[:, 0:512].rearrange("p (q t) -> p q t", q=4)), [pt], [ogT])
                self.st(self.st_d, U.t[:, :, :], U)
                if self.debug.get("dump"):
                    self.st(self.dbg_og[:, tok0:tok0 + TB].rearrange("(k p) t -> p k t", p=128), ogT.t[:, :, :], ogT)
                S.barrier()
            cw, cb = self.vec("conv_w", l), self.vec("conv_b", l)
            lng, lnb = self.vec("conv_ln_g", l), self.vec("conv_ln_b", l)
            for h in range(NH):
                t0 = tok0 + h * 512
                with ExitStack() as es3:
                    cn = self.sb(es3, "m2_cn", [128, 16, 512], BF16)
                    with ExitStack() as es4:
                        co = self.sb(es4, "m2_co", [128, 16, 512], F32)
                        cin = [self.sb(es4, f"m2_ci{i}", [128, 30 + 512], F32) for i in range(4)]
                        cbf = [self.sb(es4, f"m2_cb{i}", [128, 512], BF16) for i in range(2)]
                        csq = [self.sb(es4, f"m2_cq{i}", [128, 512], BF16) for i in range(2)]
                        mt = self.sb(es4, "m2_mt", [128, 512], F32)
                        vr = self.sb(es4, "m2_vr", [128, 512], F32)
                        self.set_pools(st=[0, 1], c=range(2, 8))
                        ps_s, ps_q = self.pbank("st"), self.pbank("st")
                        cbl = [self.sb(es4, f"m2_cbh{i}", [128, 30 + 512], BF16) for i in range(2)]
                        dgl = [self.sb(es4, f"m2_dg{i}", [128, 31, 128], BF16) for i in range(2)]
                        for k in range(16):
                            ci_, cbh, dg = cin[k % 4], cbl[k % 2], dgl[k % 2]
                            self.ld(ci_.t[:, :], self.cT_d[k * 128:(k + 1) * 128, HALO + t0 - 30:HALO + t0 + 512], ci_)
                            S.I("dve", lambda e, ci_=ci_, cbh=cbh: e.tensor_copy(out=cbh.t[:, :], in_=ci_.t[:, :]), [ci_], [cbh])
                            for tap in range(31):
                                if tap % 2:
                                    S.I("act", lambda e, dg=dg, tap=tap, k=k: e.activation(
                                        out=dg.t[:, tap, :], in_=self.identb.t[:, :], func=AF.Copy,
                                        scale=cw[:, k * 31 + tap:k * 31 + tap + 1]), [self.identb, self.vecs], [dg.s(tap)])
                                else:
                                    S.I("dve", lambda e, dg=dg, tap=tap, k=k: e.tensor_scalar(
                                        out=dg.t[:, tap, :], in0=self.identb.t[:, :], scalar1=cw[:, k * 31 + tap:k * 31 + tap + 1],
                                        scalar2=None, op0=ALU.mult), [self.identb, self.vecs], [dg.s(tap)])
                            pc = self.pbank("c")
                            for tap in range(31):
                                S.I("pe", lambda e, pc=pc, dg=dg, cbh=cbh, tap=tap: e.matmul(
                                    out=pc.t[:, :], lhsT=dg.t[:, tap, :], rhs=cbh.t[:, tap:tap + 512],
                                    start=(tap == 0), stop=(tap == 30)), [dg.s(tap), cbh], [pc], inc=(tap == 30))
                            S.I("act", lambda e, pc=pc, k=k: e.activation(out=co.t[:, k, :], in_=pc.t[:, :], func=AF.Identity,
                                                                          bias=cb[:, k:k + 1], scale=1.0), [pc, self.vecs], [co.s(k)])
                            b1, b2 = cbf[k % 2], csq[k % 2]
                            S.I("dve", lambda e, k=k, b1=b1: e.tensor_copy(out=b1.t[:, :], in_=co.t[:, k, :]), [co.s(k)], [b1])
                            S.I("act", lambda e, k=k, b2=b2: e.activation(out=b2.t[:, :], in_=co.t[:, k, :], func=AF.Square),
                                [co.s(k)], [b2])
                            S.I("pe", lambda e, k=k, b1=b1: e.matmul(out=ps_s.t[:, :], lhsT=self.onesb.t[:, :], rhs=b1.t[:, :],
                                                                     start=(k == 0), stop=(k == 15)), [b1, self.onesb], [ps_s])
                            S.I("pe", lambda e, k=k, b2=b2: e.matmul(out=ps_q.t[:, :], lhsT=self.onesb.t[:, :], rhs=b2.t[:, :],
                                                                     start=(k == 0), stop=(k == 15)), [b2, self.onesb], [ps_q])
                        S.I("act", lambda e: e.activation(out=mt.t[:, :], in_=ps_s.t[:, :], func=AF.Copy, scale=1.0 / D), [ps_s], [mt])
                        S.I("dve", lambda e: e.tensor_tensor(out=vr.t[:, :], in0=mt.t[:, :], in1=mt.t[:, :], op=ALU.mult), [mt], [vr])
                        S.I("dve", lambda e: e.scalar_tensor_tensor(out=vr.t[:, :], in0=ps_q.t[:, :], scalar=1.0 / D, in1=vr.t[:, :],
                                                                    op0=ALU.mult, op1=ALU.subtract), [ps_q], [vr])
                        S.I("act", lambda e: e.activation(out=vr.t[:, :], in_=vr.t[:, :], func=AF.Sqrt, bias=self.vec("eps"), scale=1.0),
                            [self.vecs], [vr])
                        S.I("dve", lambda e: e.reciprocal(out=vr.t[:, :], in_=vr.t[:, :]), [], [vr])
                        for k in range(16):
                            S.I("dve", lambda e, k=k: e.tensor_tensor(out=co.t[:, k, :], in0=co.t[:, k, :], in1=mt.t[:, :],
                                                                      op=ALU.subtract), [mt], [co.s(k)])
                            S.I("dve", lambda e, k=k: e.tensor_tensor(out=co.t[:, k, :], in0=co.t[:, k, :], in1=vr.t[:, :],
                                                                      op=ALU.mult), [vr], [co.s(k)])
                            S.I("act", lambda e, k=k: e.activation(out=cn.t[:, k, :], in_=co.t[:, k, :], func=AF.Silu,
                                                                   scale=lng[:, k:k + 1], bias=lnb[:, k:k + 1]),
                                [co.s(k), self.vecs], [cn.s(k)])
                        S.barrier()
                    m = self.sb(es3, "m2_m", [128, 16, 512], BF16)
                    xt = self.sb(es3, "m2_x", [128, 16, 512], F32)
                    g0l = [self.sb(es3, f"m2_g0{i}", [128, 512], BF16) for i in range(2)]
                    g1l = [self.sb(es3, f"m2_g1{i}", [128, 512], BF16) for i in range(2)]
                    t1l = [self.sb(es3, f"m2_t1{i}", [128, 512], F32) for i in range(2)]
                    t2l = [self.sb(es3, f"m2_t2{i}", [128, 512], F32) for i in range(2)]
                    self.set_pools(a=range(8))
                    self.load_x(xt, blk, h, 1)
                    for s8 in range(8):
                        GP, GPb = self.wload("gla_proj", l, 0, 16, s8 * 256, 256)
                        CP, CPb = self.wload("conv_proj", l, 0, 16, s8 * 256, 256)
                        for c in range(2):
                            j = 2 * s8 + c
                            pa = self.fm_out(GP, GPb, c, ogT, 1, hoff=h)[0]
                            pb = self.fm_out(CP, CPb, c, cn, 1)[0]
                            g0, g1, t1, t2 = g0l[j % 2], g1l[j % 2], t1l[j % 2], t2l[j % 2]
                            self.ld(g0.t[:, :], self.g0T_d[j * 128:(j + 1) * 128, t0:t0 + 512], g0)
                            self.ld(g1.t[:, :], self.g1T_d[j * 128:(j + 1) * 128, t0:t0 + 512], g1)
                            S.I("dve", lambda e, t1=t1, pa=pa, g0=g0: e.tensor_tensor(out=t1.t[:, :], in0=pa.t[:, :], in1=g0.t[:, :],
                                                                                      op=ALU.mult), [pa, g0], [t1])
                            S.I("dve", lambda e, t2=t2, pb=pb, g1=g1: e.tensor_tensor(out=t2.t[:, :], in0=pb.t[:, :], in1=g1.t[:, :],
                                                                                      op=ALU.mult), [pb, g1], [t2])
                            S.I("dve", lambda e, t1=t1, t2=t2, j=j: e.tensor_tensor(out=m.t[:, j, :], in0=t1.t[:, :], in1=t2.t[:, :],
                                                                                    op=ALU.add), [t1, t2], [m.s(j, 0)])
                    if self.debug.get("dump"):
                        self.st(self.dbg_cn[:, t0:t0 + 512].rearrange("(k p) t -> p k t", p=128), cn.t[:, :, :], cn)
                        self.st(self.dbg_m[:, t0:t0 + 512].rearrange("(k p) t -> p k t", p=128), m.t[:, :, :], m)
                    for s8 in range(8):
                        WO, WOb = self.wload("mix_w_out", l, 0, 16, s8 * 256, 256)
                        for c in range(2):
                            j = 2 * s8 + c
                            py = self.fm_out(WO, WOb, c, m, 1)[0]
                            S.I("dve", lambda e, j=j, py=py: e.tensor_tensor(out=xt.t[:, j, :], in0=py.t[:, :], in1=xt.t[:, j, :],
                                                                             op=ALU.add), [py], [xt.s(j, 0)])
                    self.store_x(xt, blk, h, 1)
                    S.barrier()
            S.barrier()

    def stage_xattn(self, l, blk):
        S = self.S
        for h in range(NH):
            with ExitStack() as es:
                xt = self.sb(es, "xa_x", [128, 16, 512], F32)
                hT = self.sb(es, "xa_h", [128, 16, 512], BF16)
                sq = [self.sb(es, f"xa_sq{i}", [128, 512], BF16) for i in range(4)]
                rs = self.sb(es, "xa_rs", [128, 512], F32)
                qT = self.sb(es, "xa_q", [128, 4, 512], BF16)
                pTl = [self.sb(es, f"xa_pT{i}", [128, 2, 512], BF16) for i in range(2)]
                oT = self.sb(es, "xa_o", [128, 4, 512], BF16)
                pel = [self.sb(es, f"xa_pe{i}", [128, 256], F32) for i in range(2)]
                pnl = [self.sb(es, f"xa_pn{i}", [128, 256], BF16) for i in range(2)]
                sml = [self.sb(es, f"xa_sm{i}", [128, 4], F32) for i in range(4)]
                self.set_pools(a=range(0, 4), s=range(4, 6), t=range(6, 8))
                self.load_x(xt, blk, h, 1)
                self.make_hT(xt, hT, self.vec("xa_norm", l), 1, sq, rs)
                for s2 in range(2):
                    W, Wb = self.wload("xa_w_q", l, 0, 16, s2 * 256, 256)
                    for c in range(2):
                        hd = 2 * s2 + c
                        pb = self.fm_out(W, Wb, c, hT, 1)[0]
                        S.I("act", lambda e, pb=pb, hd=hd: e.activation(out=qT.t[:, hd, :], in_=pb.t[:, :], func=AF.Copy,
                                                                        scale=128 ** -0.5), [pb], [qT.s(hd)])
                it = 0
                for hd in range(4):
                    pT = pTl[hd % 2]
                    for j in range(4):
                        psb = self.pbank("s")
                        pe_, pn, sm = pel[it % 2], pnl[it % 2], sml[it % 4]
                        it += 1
                        S.I("pe", lambda e, psb=psb, hd=hd, j=j: e.matmul(out=psb.t[:, 0:256], lhsT=qT.t[:, hd, j * 128:(j + 1) * 128],
                                                                          rhs=self.kmT[l].t[:, hd, :], start=True, stop=True),
                            [qT.s(hd), self.kmT[l]], [psb])
                        S.I("dve", lambda e, psb=psb, sm=sm: e.tensor_reduce(out=sm.t[:, 0:1], in_=psb.t[:, 0:256], axis=AXX, op=ALU.max,
                                                                             negate=True), [psb], [sm])
                        S.I("act", lambda e, psb=psb, sm=sm, pe_=pe_: e.activation(out=pe_.t[:, :], in_=psb.t[:, 0:256], func=AF.Exp,
                                                                                   bias=sm.t[:, 0:1], scale=1.0, accum_out=sm.t[:, 1:2]),
                            [psb, sm], [pe_, sm])
                        S.I("dve", lambda e, sm=sm: e.reciprocal(out=sm.t[:, 2:3], in_=sm.t[:, 1:2]), [], [sm])
                        S.I("dve", lambda e, sm=sm, pe_=pe_, pn=pn: e.tensor_scalar(out=pn.t[:, :], in0=pe_.t[:, :], scalar1=sm.t[:, 2:3],
                                                                                    scalar2=None, op0=ALU.mult), [pe_, sm], [pn])
                        pt = self.pbank("t")
                        ptb = self.bview(pt)
                        for mc in range(2):
                            S.I("pe", lambda e, ptb=ptb, pn=pn, mc=mc: e.transpose(out=ptb[:, mc * 128:(mc + 1) * 128],
                                                                                   in_=pn.t[:, mc * 128:(mc + 1) * 128],
                                                                                   identity=self.identb.t[:, :]),
                                [pn, self.identb], [pt], inc=(mc == 1))
                        S.I("act", lambda e, ptb=ptb, pT=pT, j=j: e.copy(out=pT.t[:, :, j * 128:(j + 1) * 128],
                                                                         in_=ptb[:, 0:256].rearrange("p (m t) -> p m t", m=2)),
                            [pt], [pT])
                    po = self.pbank("a")
                    for mc in range(2):
                        S.I("pe", lambda e, po=po, mc=mc, hd=hd, pT=pT: e.matmul(out=po.t[:, :], lhsT=self.vm[l].t[:, mc, hd * 128:(hd + 1) * 128],
                                                                                 rhs=pT.t[:, mc, :], start=(mc == 0), stop=(mc == 1)),
                            [self.vm[l], pT], [po], inc=(mc == 1))
                    S.I("dve", lambda e, po=po, hd=hd: e.tensor_copy(out=oT.t[:, hd, :], in_=po.t[:, :]), [po], [oT.s(hd)])
                for s2 in range(2):
                    W, Wb = self.wload("xa_w_out", l, 0, 4, s2 * 1024, 1024)
                    for jj in range(8):
                        j = s2 * 8 + jj
                        py = self.pbank("a")
                        for kc in range(4):
                            S.I("pe", lambda e, py=py, W=W, kc=kc, jj=jj: e.matmul(out=py.t[:, :], lhsT=W[:, kc, jj * 128:(jj + 1) * 128],
                                                                                   rhs=oT.t[:, kc, :], start=(kc == 0), stop=(kc == 3)),
                                [Wb, oT.s(kc)], [py], inc=(kc == 3))
                        S.I("dve", lambda e, j=j, py=py: e.tensor_tensor(out=xt.t[:, j, :], in0=py.t[:, :], in1=xt.t[:, j, :], op=ALU.add),
                            [py], [xt.s(j, 0)])
                self.store_x(xt, blk, h, 1)
                S.barrier()

    def stage_ffn(self, l, blk, which):
        S = self.S
        wi, wo, gn = which + "_w_in", which + "_w_out", which + "_norm"
        with ExitStack() as es:
            xt = self.sb(es, "ffn_x", [128, 16, TB], F32)
            hT = self.sb(es, "ffn_h", [128, 16, TB], BF16)
            sq = [self.sb(es, f"ffn_sq{i}", [128, 512], BF16) for i in range(4)]
            rs = self.sb(es, "ffn_rs", [128, 512], F32)
            act = [self.sb(es, f"ffn_act{i}", [128, 4, TB], BF16) for i in range(2)]
            tmp = [self.sb(es, f"ffn_tmp{i}", [128, 512], F32) for i in range(2)]
            self.set_pools(a=range(0, 4), y=range(4, 8))
            self.load_x(xt, blk, 0, NH)
            self.make_hT(xt, hT, self.vec(gn, l), NH, sq, rs)
            ngrp = DFF // 512
            ti = 0
            for gi in range(ngrp):
                at = act[gi % 2]
                for sgi in range(2):
                    A, Ab = self.wload(wi, l, 0, 16, (2 * gi + sgi) * 256, 256)
                    B, Bb = self.wload(wi, l, 0, 16, DFF + (2 * gi + sgi) * 256, 256)
                    for c in range(2):
                        cc = 2 * sgi + c
                        pa = [self.pbank("a") for _ in range(NH)]
                        pb = [self.pbank("a") for _ in range(NH)]
                        for W, Wb, pp in ((A, Ab, pa), (B, Bb, pb)):
                            for k in range(16):
                                for h in range(NH):
                                    S.I("pe", lambda e, W=W, k=k, h=h, c=c, p=pp[h]: e.matmul(
                                        out=p.t[:, :], lhsT=W[:, k, c * 128:(c + 1) * 128], rhs=hT.t[:, k, h * 512:(h + 1) * 512],
                                        start=(k == 0), stop=(k == 15)), [Wb, hT.s(k, h)], [pp[h]], inc=(k == 15))
                        for h in range(NH):
                            tt = tmp[ti % 2]
                            ti += 1
                            S.I("act", lambda e, tt=tt, p=pa[h]: e.activation(out=tt.t[:, :], in_=p.t[:, :], func=AF.Silu),
                                [pa[h]], [tt])
                            S.I("dve", lambda e, tt=tt, p=pb[h], cc=cc, h=h, at=at: e.tensor_tensor(
                                out=at.t[:, cc, h * 512:(h + 1) * 512], in0=tt.t[:, :], in1=p.t[:, :], op=ALU.mult),
                                [tt, pb[h]], [at.s(cc, h)])
                Os = [self.wload(wo, l, (2 * gi + sgi) * 256, 2, 0, D) for sgi in range(2)]
                for j in range(16):
                    for h in range(NH):
                        py = self.pbank("y")
                        for kc in range(4):
                            O, Ob = Os[kc // 2]
                            S.I("pe", lambda e, O=O, kc=kc, j=j, h=h, py=py, at=at: e.matmul(
                                out=py.t[:, :], lhsT=O[:, kc % 2, j * 128:(j + 1) * 128], rhs=at.t[:, kc, h * 512:(h + 1) * 512],
                                start=(kc == 0), stop=(kc == 3)), [Ob, at.s(kc, h)], [py], inc=(kc == 3))
                        S.I("dve", lambda e, j=j, h=h, py=py: e.scalar_tensor_tensor(
                            out=xt.t[:, j, h * 512:(h + 1) * 512], in0=py.t[:, :], scalar=0.5,
                            in1=xt.t[:, j, h * 512:(h + 1) * 512], op0=ALU.mult, op1=ALU.add), [py], [xt.s(j, h)])
            self.store_x(xt, blk, 0, NH)
            S.barrier()

    def stage_final(self):
        S = self.S
        self.set_pools(a=range(8))
        with ExitStack() as es:
            xt = self.sb(es, "fin_x", [128, 16, 512], F32)
            sq = [self.sb(es, f"fin_sq{i}", [128, 512], BF16) for i in range(4)]
            rs = self.sb(es, "fin_rs", [128, 512], F32)
            yt = self.sb(es, "fin_y", [128, 16, 512], F32)
            ot = self.sb(es, "fin_o", [128, 4, D], F32)
            g = self.vec("final_norm")
            for gi in range(T // 512):
                S.D("sp", "dma_ld", xt.t[:, :, :], self.xT[:, gi * 512:(gi + 1) * 512].rearrange("(k p) t -> p k t", p=128),
                    reads=[self.dbuf(("xT", gi))], writes=[xt])
                self.rstd_bc(xt, 0, sq, rs)
                for k in range(16):
                    S.I("dve", lambda e, k=k: e.scalar_tensor_tensor(out=yt.t[:, k, :], in0=xt.t[:, k, :], scalar=g[:, k:k + 1],
                                                                     in1=rs.t[:, :], op0=ALU.mult, op1=ALU.mult),
                        [xt, rs, self.vecs], [yt])
                for j in range(4):
                    for kk in range(4):
                        pb = self.pbank("a")
                        for q in range(4):
                            k = kk * 4 + q
                            S.I("pe", lambda e, pb=pb, j=j, k=k, q=q: e.transpose(
                                out=pb.t[:, q * 128:(q + 1) * 128], in_=yt.t[:, k, j * 128:(j + 1) * 128], identity=self.ident),
                                [yt, self.cst], [pb], inc=(q == 3))
                        if kk % 2:
                            S.I("act", lambda e, pb=pb, j=j, kk=kk: e.copy(out=ot.t[:, j, kk * 512:(kk + 1) * 512], in_=pb.t[:, :]),
                                [pb], [ot])
                        else:
                            S.I("dve", lambda e, pb=pb, j=j, kk=kk: e.tensor_copy(out=ot.t[:, j, kk * 512:(kk + 1) * 512], in_=pb.t[:, :]),
                                [pb], [ot])
                S.D("sp", "dma_st", self.out_d[gi * 512:(gi + 1) * 512, :].rearrange("(j p) d -> p j d", p=128), ot.t[:, :, :],
                    reads=[ot], writes=[self.dbuf(("out", gi))])
            S.barrier()


VEC_ITEMS = [("ffn1_norm", 16), ("mix_norm", 16), ("xa_norm", 16), ("ffn2_norm", 16), ("xa_mem_norm", 16),
             ("conv_b", 16), ("conv_ln_g", 16), ("conv_ln_b", 16), ("gla_gate_b", 8), ("branch_gate_b", 32),
             ("conv_w", 16 * 31)]


def vec_layout():
    off = {}
    o = 0
    for l in range(L):
        for n, w in VEC_ITEMS:
            off[(n, l)] = (o, w)
            o += w
    for n, w in [("final_norm", 16), ("eps", 1), ("one", 1), ("rmask", 4), ("rmaskc", 4), ("ronehot", 4)]:
        off[(n, None)] = (o, w)
        o += w
    return off, o


def pm(v, k):
    return np.ascontiguousarray(np.asarray(v, np.float32).reshape(k, 128).T)


def build_vecs(inp, rank):
    off, n = vec_layout()
    out = np.zeros((128, n), np.float32)
    for l in range(L):
        for name, w in VEC_ITEMS:
            o, _ = off[(name, l)]
            if name == "conv_w":
                cw = np.asarray(inp["conv_w"][l], np.float32)
                out[:, o:o + w] = cw.reshape(31, 16, 128).transpose(2, 1, 0).reshape(128, 16 * 31)
            else:
                out[:, o:o + w] = pm(inp[name][l], w)
    o, _ = off[("final_norm", None)]
    out[:, o:o + 16] = pm(inp["final_norm"], 16)
    out[:, off[("eps", None)][0]] = EPS
    out[:, off[("one", None)][0]] = 1.0
    for p in range(4):
        out[:, off[("rmask", None)][0] + p] = 1.0 if p < rank else 0.0
        out[:, off[("rmaskc", None)][0] + p] = 0.0 if p < rank else 1.0
        out[:, off[("ronehot", None)][0] + p] = 1.0 if p == rank - 1 else 0.0
    return out


def build_cst():
    c = np.zeros((128, 128 + 64 + TB), np.float32)
    c[:, 0:128] = np.eye(128, dtype=np.float32)
    jj, ii = np.meshgrid(np.arange(64), np.arange(64), indexing="ij")
    c[0:64, 128:192] = (jj <= ii).astype(np.float32)
    cm = np.ones(TB, np.float32)
    cm[0::64] = 0.0
    c[:, 192:] = cm[None, :]
    return c


def pack_slabs(inp, specs):
    wall = np.zeros((max(1, len(specs)), 128, SLOT), np.float32)
    for i, (name, l, r0, nkc, c0, ncols) in enumerate(specs):
        w = inp[name][l]
        blk = np.asarray(w[r0:r0 + nkc * 128, c0:c0 + ncols], np.float32)
        wall[i, :, :nkc * ncols] = blk.reshape(nkc, 128, ncols).transpose(1, 0, 2).reshape(128, nkc * ncols)
    return wall


_CACHE = {}


def get_program(debug=None):
    key = repr(sorted((debug or {}).items()))
    if key not in _CACHE:
        plan = Prog(plan=None, debug=debug).build()
        prog = Prog(plan=plan, debug=debug)
        nc = prog.build()
        _CACHE[key] = (nc, plan)
    return _CACHE[key]


def kernel(debug=None, ncores=NCORE, **inp):
    nc, plan = get_program(debug)
    x = np.asarray(inp["x"], np.float32)
    mem = np.asarray(inp["mem"], np.float32)
    wall = pack_slabs(inp, plan["specs"])
    cst = build_cst()
    gnb = np.ascontiguousarray(np.broadcast_to(np.asarray(inp["gla_out_norm"], np.float32)[:, None, :], (L, 128, D)))
    gw2 = np.ascontiguousarray(np.asarray(inp["gla_gate_w2"], np.float32))
    in_maps = []
    for c in range(ncores):
        b, r = c // 4, c % 4
        in_maps.append({
            "x_in": np.ascontiguousarray(x[b, r * T:(r + 1) * T, :]),
            "mem_in": np.ascontiguousarray(mem[b]),
            "vecs": build_vecs(inp, r),
            "cst": cst, "gnb": gnb, "gw2": gw2, "wall": wall,
        })
    res = run_bass_kernel_spmd(nc, in_maps, core_ids=list(range(ncores)))
    if debug and debug.get("raw"):
        return res.results
    out = np.zeros((2, 8192, D), np.float32)
    for c in range(ncores):
        b, r = c // 4, c % 4
        out[b, r * T:(r + 1) * T, :] = np.asarray(res.results[c]["out"], np.float32)
    return out
```

```python
import numpy as np
import concourse.bass as bass
import concourse.mybir as mybir
from concourse.bass_utils import run_bass_kernel_spmd
from contextlib import ExitStack

F32 = mybir.dt.float32
BF16 = mybir.dt.bfloat16
AF = mybir.ActivationFunctionType
ALU = mybir.AluOpType
AXX = mybir.AxisListType.X

L = 2
D = 2048
DFF = 5632
DIN = 14352
NCORE = 8
T = 2048
TB = 1024
NH = TB // 512
NBLK = T // TB
NCHUNK = T // 64
HALO = 32
SLOT = 4096
NSLOT = 8
LOOKAHEAD = 7
EPS = 1e-6
C_Q, C_K, C_V, C_R, C_G, C_UA, C_UG, C_G0, C_G1 = 0, 1024, 2048, 4096, 4112, 6160, 8208, 10256, 12304


class Buf:
    __slots__ = ("name", "w", "r")

    def __init__(self, name=""):
        self.name = name
        self.w = None
        self.r = {}


class Tile:
    def __init__(self, t, name=""):
        self.t = t
        self.b = Buf(name)
        self.subs = {}

    def __getitem__(self, idx):
        return self.t[idx]

    def s(self, *key):
        if key not in self.subs:
            nb = Buf(f"{self.b.name}{key}")
            nb.w = self.b.w
            nb.r = dict(self.b.r)
            self.subs[key] = nb
        return self.subs[key]

    def all(self):
        return [self.b] + list(self.subs.values())


def flat(xs):
    out = []
    for x in xs:
        if isinstance(x, Tile):
            out.extend(x.all())
        elif isinstance(x, (list, tuple)):
            out.extend(flat(x))
        else:
            out.append(x)
    return out


class Sched:
    ENG = ("pe", "act", "dve", "pool", "sp")

    def __init__(self, sem_names, pools=None):
        self.cnt = {k: 0 for k in sem_names}
        self.pools = pools or {}
        self.pool_idx = {k: 0 for k in self.pools}
        self.known = {e: {} for e in self.ENG}
        self.stream = {e: [] for e in self.ENG}
        self.noinc = {e: False for e in self.ENG}

    def _deps(self, eng, reads, writes):
        deps = {}

        def add(k, v):
            if eng == "pe" and k == "pe":
                return
            if deps.get(k, 0) < v:
                deps[k] = v

        for b in reads:
            if b.w is not None:
                add(*b.w)
        for b in writes:
            if b.w is not None:
                add(*b.w)
            for k, v in b.r.items():
                add(k, v)
        return deps

    def waits(self, eng, deps):
        kn = self.known[eng]
        for k, v in deps.items():
            if kn.get(k, 0) < v:
                kn[k] = v
                self.stream[eng].append(("w", k, v))

    @staticmethod
    def _mark(tok, reads, writes):
        k, v = tok
        for b in reads:
            if b.r.get(k, 0) < v:
                b.r[k] = v
        for b in writes:
            b.w = tok
            b.r = {}

    def I(self, eng, fn, reads=(), writes=(), inc=True):
        reads = flat(reads)
        writes = flat(writes)
        self.waits(eng, self._deps(eng, reads, writes))
        if inc:
            self.cnt[eng] += 1
            v = self.cnt[eng]
            self.stream[eng].append(("i", fn, eng, 1))
            self.noinc[eng] = False
        else:
            v = self.cnt[eng] + 1
            self.stream[eng].append(("n", fn))
            self.noinc[eng] = True
        self._mark((eng, v), reads, writes)

    def X(self, eng, semk, amount, fn, reads=(), writes=()):
        reads = flat(reads)
        writes = flat(writes)
        self.waits(eng, self._deps(eng, reads, writes))
        self.cnt[semk] += amount
        self.stream[eng].append(("i", fn, semk, amount))
        self._mark((semk, self.cnt[semk]), reads, writes)
        return (semk, self.cnt[semk])

    def D(self, q, semk, out, in_, reads=(), writes=(), extra_deps=None, **kw):
        reads = flat(reads)
        writes = flat(writes)
        deps = self._deps(q, reads, writes)
        if semk in self.pools:
            pname = semk
            lst = self.pools[pname]
            semk = lst[self.pool_idx[pname] % len(lst)]
            self.pool_idx[pname] += 1
            if self.cnt[semk] > 0:
                deps[semk] = max(deps.get(semk, 0), self.cnt[semk])
        if extra_deps:
            for k, v in extra_deps.items():
                if deps.get(k, 0) < v:
                    deps[k] = v
        self.waits(q, deps)
        self.cnt[semk] += 16
        v = self.cnt[semk]
        self.stream[q].append(("i", lambda e: e.dma_start(out=out, in_=in_, **kw), semk, 16))
        self._mark((semk, v), reads, writes)
        return (semk, v)

    def barrier(self, engs=("pe", "act", "dve", "sp"), skip=("pool",), skip_w=True):
        toks = {k: v for k, v in self.cnt.items() if v > 0 and k not in skip and not (skip_w and k.startswith("wq"))}
        for e in engs:
            self.waits(e, toks)

    def replay(self, eng, e, sems):
        for it in self.stream[eng]:
            if it[0] == "w":
                e.wait_ge(sems[it[1]], it[2])
            elif it[0] == "i":
                it[1](e).then_inc(sems[it[2]], it[3])
            else:
                it[1](e)


class Prog:
    def __init__(self, plan=None, debug=None):
        self.plan = plan
        self.dry = plan is None
        self.debug = debug or {}
        self.slab_ids = {}
        self.slab_specs = []
        self.wseq = []
        self.wfree = []
        self.wtok = []
        self.w_emitted = 0
        self.w_req = 0

    def slab_id(self, spec):
        if spec not in self.slab_ids:
            self.slab_ids[spec] = len(self.slab_specs)
            self.slab_specs.append(spec)
        return self.slab_ids[spec]

    def _emit_wdma(self, n):
        sid = self.plan["seq"][n]
        slot = self.wslots[n % NSLOT]
        deps = self.plan["free"][n]
        self.S.D("pool", f"wq{n % NSLOT}", slot.t[:, :], self.wall[sid], extra_deps=deps)

    def wload(self, name, l, r0, nkc, c0, ncols):
        spec = (name, l, r0, nkc, c0, ncols)
        assert nkc * ncols <= SLOT
        sid = self.slab_id(spec)
        n = self.w_req
        self.w_req += 1
        slot = self.wslots[n % NSLOT]
        if self.dry:
            self.wseq.append(sid)
            free = {}
            if slot.b.w is not None:
                free[slot.b.w[0]] = slot.b.w[1]
            for k, v in slot.b.r.items():
                if free.get(k, 0) < v:
                    free[k] = v
            for k in list(free):
                if k.startswith("wq"):
                    free.pop(k)
            self.wfree.append(free)
            self.S.cnt[f"wq{n % NSLOT}"] += 16
        else:
            assert self.plan["seq"][n] == sid
            while self.w_emitted < min(len(self.plan["seq"]), n + 1 + LOOKAHEAD):
                self._emit_wdma(self.w_emitted)
                self.w_emitted += 1
        slot.b.w = (f"wq{n % NSLOT}", 16 * (n // NSLOT + 1))
        slot.b.r = {}
        view = slot.t[:, 0:nkc * ncols].rearrange("p (k c) -> p k c", k=nkc)
        return view, slot.b

    def sb(self, es, name, shape, dt):
        self.uid = getattr(self, "uid", 0) + 1
        name = f"{name}_{self.uid}"
        t = es.enter_context(self.nc.sbuf_tensor(name, list(shape), dt))
        return Tile(t, name)

    def pbank(self, pool):
        i = self.pidx[pool]
        self.pidx[pool] = (i + 1) % len(self.ppool[pool])
        return self.ps[self.ppool[pool][i]]

    def set_pools(self, **pools):
        self.ppool = {k: list(v) for k, v in pools.items()}
        self.pidx = {k: 0 for k in pools}

    def dbuf(self, key):
        if key not in self.dbufs:
            self.dbufs[key] = Buf(str(key))
        return self.dbufs[key]

    def vec(self, name, l=None):
        off, w = self.vec_off[(name, l)]
        return self.vecs.t[:, off:off + w]

    def build(self):
        nc = bass.Bass("TRN2", target_bir_lowering=False)
        self.nc = nc
        pools = {"dma_ld": [f"ld{i:02d}" for i in range(12)], "dma_st": [f"st{i:02d}" for i in range(8)]}
        sem_names = ["pe", "act", "dve", "pool", "sp", "cc"] + [f"wq{i}" for i in range(NSLOT)] + pools["dma_ld"] + pools["dma_st"]
        self.S = S = Sched(sem_names, pools)
        self.dbufs = {}
        dt = nc.dram_tensor
        self.x_in = dt("x_in", [T, D], F32, kind="ExternalInput").ap()
        self.mem_in = dt("mem_in", [256, D], F32, kind="ExternalInput").ap()
        self.vec_off, nvc = vec_layout()
        self.vecs_d = dt("vecs", [128, nvc], F32, kind="ExternalInput").ap()
        self.cst_d = dt("cst", [128, 128 + 64 + TB], F32, kind="ExternalInput").ap()
        self.gnb_d = dt("gnb", [L, 128, D], F32, kind="ExternalInput").ap()
        self.gw2_d = dt("gw2", [L, 16, 1024], F32, kind="ExternalInput").ap()
        nslab = max(1, self.plan["nslab"]) if not self.dry else 1
        self.wall = dt("wall", [nslab, 128, SLOT], F32, kind="ExternalInput").ap()
        self.out_d = dt("out", [T, D], F32, kind="ExternalOutput").ap()
        ik = "ExternalOutput" if self.debug.get("dump") else "Internal"
        self.xT = dt("xT", [D, T], F32, kind=ik).ap()
        if self.debug.get("upto", "all") not in ("s0", "none", "s0only"):
            self.qT_d = dt("qT_s", [1024, T], BF16, kind=ik).ap()
            self.kT_d = dt("kT_s", [1024, T], BF16, kind=ik).ap()
            self.kTok_d = dt("kTok_s", [T, 1024], BF16, kind=ik).ap()
            self.vTok_d = dt("vTok_s", [T, D], BF16, kind=ik).ap()
            self.sgn_d = dt("sgn_s", [T, D], BF16, kind=ik).ap()
            self.cT_d = dt("cT_s", [D, HALO + T], F32, kind=ik).ap()
            self.g0T_d = dt("g0T_s", [D, T], BF16, kind=ik).ap()
            self.g1T_d = dt("g1T_s", [D, T], BF16, kind=ik).ap()
            self.dec_d = dt("dec_s", [128, 8, NCHUNK], F32, kind=ik).ap()
            self.exS_in = [[dt(f"exS_in{l}_{q}", [256, 512], F32, kind="Internal").ap() for q in range(4)] for l in range(L)]
            self.exS_out = [[dt(f"exS_out{l}_{q}", [4 * 256, 512], F32, kind="Internal").ap() for q in range(4)] for l in range(L)]
            self.exM_in = [dt(f"exM_in{l}", [D + 128, 32], F32, kind="Internal").ap() for l in range(L)]
            self.exM_out = [dt(f"exM_out{l}", [4 * (D + 128), 32], F32, kind="Internal").ap() for l in range(L)]
            self.st_d = dt("st_s", [128, 8, 512], F32, kind=ik).ap()
        if self.debug.get("dump"):
            self.dbg_og = dt("dbg_og", [D, T], BF16, kind="ExternalOutput").ap()
            self.dbg_cn = dt("dbg_cn", [D, T], BF16, kind="ExternalOutput").ap()
            self.dbg_m = dt("dbg_m", [D, T], BF16, kind="ExternalOutput").ap()
        self.dbg_d = None
        if self.debug.get("dbg_shape"):
            self.dbg_d = dt("dbg", list(self.debug["dbg_shape"]), F32, kind="ExternalOutput").ap()

        with ExitStack() as es:
            self.sems = {k: es.enter_context(nc.semaphore(k)) for k in sem_names}
            self.ps = [Tile(es.enter_context(nc.psum_tensor(f"ps{i}", [128, 512], F32)), f"ps{i}") for i in range(8)]
            self.wslots = [self.sb(es, f"wslot{i}", [128, SLOT], BF16) for i in range(NSLOT)]
            self.vecs = self.sb(es, "vecs_sb", [128, nvc], F32)
            self.cst = self.sb(es, "cst_sb", [128, 128 + 64 + TB], F32)
            self.identb = self.sb(es, "identb", [128, 128], BF16)
            self.onesb = self.sb(es, "onesb", [128, 128], BF16)
            self.kmT = [self.sb(es, f"kmT{l}", [128, 4, 256], BF16) for l in range(L)]
            self.vm = [self.sb(es, f"vm{l}", [128, 2, 512], BF16) for l in range(L)]
            block = es.enter_context(nc.Block())
            self.set_pools(a=range(8))
            S.D("sp", "dma_ld", self.vecs.t[:, :], self.vecs_d, writes=[self.vecs])
            S.D("sp", "dma_ld", self.cst.t[:, :], self.cst_d, writes=[self.cst])
            self.ident = self.cst.t[:, 0:128]
            self.mask64 = self.cst.t[0:64, 128:192]
            self.cmask = self.cst.t[:, 192:192 + TB]
            S.I("dve", lambda e: e.tensor_copy(out=self.identb.t[:, :], in_=self.ident), [self.cst], [self.identb])
            S.I("dve", lambda e: e.memset(self.onesb.t[:, :], 1.0), [], [self.onesb])

            if self.debug.get("upto", "all") != "all":
                junk = self.sb(es, "junk", [128, 64], F32)
                for ap in (self.x_in[0:128, 0:8], self.mem_in[0:128, 0:8], self.gnb_d[0, :, 0:8], self.gw2_d[0, :, 0:8],
                           self.wall[0, :, 0:8]):
                    S.D("sp", "dma_ld", junk.t[0:ap.shape[0], 0:8], ap, writes=[junk])
            self.body(es)

            S.barrier(engs=("pe", "act", "dve", "sp", "pool"), skip=(), skip_w=False)
            if not self.dry:
                @block.sync
                def _(e):
                    S.replay("sp", e, self.sems)

                @block.tensor
                def _(e):
                    S.replay("pe", e, self.sems)

                @block.scalar
                def _(e):
                    S.replay("act", e, self.sems)

                @block.vector
                def _(e):
                    S.replay("dve", e, self.sems)

                @block.gpsimd
                def _(e):
                    S.replay("pool", e, self.sems)
        for e in Sched.ENG:
            assert not S.noinc[e], e
        if self.dry:
            return {"seq": self.wseq, "free": self.wfree, "nslab": len(self.slab_specs), "specs": self.slab_specs}
        return nc

    def body(self, es):
        upto = self.debug.get("upto", "all")
        if upto == "none":
            return
        self.stage_s0()
        if upto == "s0only":
            return
        if upto == "s0":
            return self.stage_final()
        if upto != "ffn1":
            self.stage_mem()
        nl = self.debug.get("layers", L)
        for l in range(nl):
            for blk in range(NBLK):
                self.stage_ffn(l, blk, "ffn1")
                if upto == "ffn1":
                    continue
                self.stage_mix1(l, blk)
            if upto == "ffn1":
                break
            self.stage_pass1(l)
            self.stage_exchange(l)
            for blk in range(NBLK):
                self.stage_mix2(l, blk)
                if upto == "mix2":
                    continue
                self.stage_xattn(l, blk)
                if upto == "xattn":
                    continue
                self.stage_ffn(l, blk, "ffn2")
            if upto in ("mix2", "xattn"):
                break
        self.stage_final()

    def stage_s0(self):
        S, nc = self.S, self.nc
        self.set_pools(a=range(8))
        with ExitStack() as es:
            xin = [self.sb(es, f"s0_x{i}", [128, 4, D], F32) for i in range(1)]
            xo = [self.sb(es, f"s0_o{i}", [128, 16, 512], F32) for i in range(2)]
            for g in range(T // 512):
                xi, xt = xin[0], xo[g % 2]
                S.D("sp", "dma_ld", xi.t[:, :, :], self.x_in[g * 512:(g + 1) * 512, :].rearrange("(j p) d -> p j d", p=128),
                    writes=[xi])
                for k in range(16):
                    pb = self.pbank("a")
                    for j in range(4):
                        S.I("pe", lambda e, pb=pb, xi=xi, j=j, k=k: e.transpose(
                            out=pb.t[:, j * 128:(j + 1) * 128], in_=xi.t[:, j, k * 128:(k + 1) * 128], identity=self.ident),
                            [xi, self.cst], [pb], inc=(j == 3))
                    eng = "act" if k % 2 else "dve"
                    if eng == "act":
                        S.I("act", lambda e, pb=pb, xt=xt, k=k: e.copy(out=xt.t[:, k, :], in_=pb.t[:, :]), [pb], [xt])
                    else:
                        S.I("dve", lambda e, pb=pb, xt=xt, k=k: e.tensor_copy(out=xt.t[:, k, :], in_=pb.t[:, :]), [pb], [xt])
                S.D("sp", "dma_st", self.xT[:, g * 512:(g + 1) * 512].rearrange("(k p) t -> p k t", p=128), xt.t[:, :, :],
                    reads=[xt], writes=[self.dbuf(("xT", g))])
            S.barrier()

    def rstd_bc(self, xt, h, sq, rs, nchunks=16, scale=1.0 / D):
        S = self.S
        pb = self.pbank("a")
        for k in range(nchunks):
            sqk = sq[k % len(sq)]
            S.I("act", lambda e, k=k, sqk=sqk: e.activation(out=sqk.t[:, :], in_=xt.t[:, k, h * 512:(h + 1) * 512], func=AF.Square),
                [xt.s(k, h)], [sqk])
            S.I("pe", lambda e, k=k, pb=pb, sqk=sqk: e.matmul(out=pb.t[:, :], lhsT=self.onesb.t[:, :], rhs=sqk.t[:, :],
                                                              start=(k == 0), stop=(k == nchunks - 1)),
                [sqk, self.onesb], [pb], inc=True)
        S.I("act", lambda e, pb=pb: e.activation(out=rs.t[:, :], in_=pb.t[:, :], func=AF.Sqrt, scale=scale,
                                                 bias=self.vec("eps")), [pb, self.vecs], [rs])
        S.I("dve", lambda e: e.reciprocal(out=rs.t[:, :], in_=rs.t[:, :]), [], [rs])


    def load_x(self, xt, blk, h0, nh):
        for h in range(nh):
            gi = blk * NH + h0 + h
            self.S.D("sp", "dma_ld", xt.t[:, :, h * 512:(h + 1) * 512],
                     self.xT[:, gi * 512:(gi + 1) * 512].rearrange("(k p) t -> p k t", p=128),
                     reads=[self.dbuf(("xT", gi))], writes=[xt.s(k, h) for k in range(16)])

    def store_x(self, xt, blk, h0, nh):
        for h in range(nh):
            gi = blk * NH + h0 + h
            self.S.D("sp", "dma_st", self.xT[:, gi * 512:(gi + 1) * 512].rearrange("(k p) t -> p k t", p=128),
                     xt.t[:, :, h * 512:(h + 1) * 512], reads=[xt.s(k, h) for k in range(16)], writes=[self.dbuf(("xT", gi))])

    def make_hT(self, xt, hT, g, nh, sq, rs, hoff=0):
        S = self.S
        for h in range(nh):
            self.rstd_bc(xt, h, sq, rs)
            ho = hoff + h
            for k in range(16):
                S.I("dve", lambda e, k=k, h=h, ho=ho: e.scalar_tensor_tensor(
                    out=hT.t[:, k, ho * 512:(ho + 1) * 512], in0=xt.t[:, k, h * 512:(h + 1) * 512], scalar=g[:, k:k + 1],
                    in1=rs.t[:, :], op0=ALU.mult, op1=ALU.mult), [xt.s(k, h), rs, self.vecs], [hT.s(k, ho)])

    def mm_group(self, pb, lhs_fn, rhs_fn, nk, reads, M=128, N=512, po=0):
        for k in range(nk):
            self.S.I("pe", lambda e, k=k: e.matmul(out=pb.t[po:po + M, 0:N], lhsT=lhs_fn(k), rhs=rhs_fn(k),
                                                  start=(k == 0), stop=(k == nk - 1)),
                     reads, [pb], inc=(k == nk - 1))


    def ld(self, dst_ap, src_ap, tile):
        self.S.D("sp", "dma_ld", dst_ap, src_ap, writes=[tile])

    def st(self, dst_ap, src_ap, tile):
        self.S.D("sp", "dma_st", dst_ap, src_ap, reads=[tile])

    def bview(self, pb):
        return pb.t[:, :].bitcast(BF16)

    def fm_out(self, W, Wb, c, hT, nh, pool="a", hoff=0):
        S = self.S
        pbs = [self.pbank(pool) for _ in range(nh)]
        for k in range(16):
            for h in range(nh):
                ho = hoff + h
                S.I("pe", lambda e, k=k, h=h, ho=ho, p=pbs[h]: e.matmul(
                    out=p.t[:, :], lhsT=W[:, k, c * 128:(c + 1) * 128], rhs=hT.t[:, k, ho * 512:(ho + 1) * 512],
                    start=(k == 0), stop=(k == 15)), [Wb, hT.s(k, ho)], [pbs[h]], inc=(k == 15))
        return pbs

    def stage_mem(self):
        S = self.S
        self.set_pools(a=range(8))
        with ExitStack() as es:
            mt = self.sb(es, "mem_x", [128, 2, D], F32)
            mn = self.sb(es, "mem_n", [128, 2, D], F32)
            junk = self.sb(es, "mem_j", [128, D], BF16)
            ssq = self.sb(es, "mem_ss", [128, 2], F32)
            mhT = [self.sb(es, f"mem_hT{l}", [128, 16, 256], BF16) for l in range(L)]
            self.ld(mt.t[:, :, :], self.mem_in.rearrange("(j p) d -> p j d", p=128), mt)
            for j in range(2):
                S.I("act", lambda e, j=j: e.activation(out=junk.t[:, :], in_=mt.t[:, j, :], func=AF.Square,
                                                        accum_out=ssq.t[:, j:j + 1]), [mt], [junk, ssq])
            S.I("dve", lambda e: e.tensor_scalar(out=ssq.t[:, :], in0=ssq.t[:, :], scalar1=1.0 / D, scalar2=EPS,
                                                 op0=ALU.mult, op1=ALU.add), [], [ssq])
            S.I("act", lambda e: e.activation(out=ssq.t[:, :], in_=ssq.t[:, :], func=AF.Sqrt), [], [ssq])
            S.I("dve", lambda e: e.reciprocal(out=ssq.t[:, :], in_=ssq.t[:, :]), [], [ssq])
            for j in range(2):
                S.I("dve", lambda e, j=j: e.tensor_scalar(out=mn.t[:, j, :], in0=mt.t[:, j, :], scalar1=ssq.t[:, j:j + 1],
                                                          scalar2=None, op0=ALU.mult), [mt, ssq], [mn])
            for k in range(16):
                pb = self.pbank("a")
                for j in range(2):
                    S.I("pe", lambda e, pb=pb, j=j, k=k: e.transpose(out=pb.t[:, j * 128:(j + 1) * 128],
                                                                     in_=mn.t[:, j, k * 128:(k + 1) * 128], identity=self.ident),
                        [mn, self.cst], [pb], inc=(j == 1))
                for l in range(L):
                    g = self.vec("xa_mem_norm", l)
                    S.I("dve", lambda e, pb=pb, l=l, k=k, g=g: e.tensor_scalar(out=mhT[l].t[:, k, :], in0=pb.t[:, 0:256],
                                                                              scalar1=g[:, k:k + 1], scalar2=None, op0=ALU.mult),
                        [pb, self.vecs], [mhT[l]])
            for l in range(L):
                for s2 in range(2):
                    W, Wb = self.wload("xa_w_kv", l, 0, 16, s2 * 256, 256)
                    for c in range(2):
                        hd = 2 * s2 + c
                        pb = self.pbank("a")
                        for k in range(16):
                            S.I("pe", lambda e, W=W, k=k, c=c, pb=pb, l=l: e.matmul(
                                out=pb.t[:, 0:256], lhsT=W[:, k, c * 128:(c + 1) * 128], rhs=mhT[l].t[:, k, :],
                                start=(k == 0), stop=(k == 15)), [Wb, mhT[l]], [pb], inc=(k == 15))
                        S.I("act", lambda e, pb=pb, l=l, hd=hd: e.copy(out=self.kmT[l].t[:, hd, :], in_=pb.t[:, 0:256]),
                            [pb], [self.kmT[l]])
                for s2 in range(2):
                    W, Wb = self.wload("xa_w_kv", l, 0, 16, 512 + s2 * 256, 256)
                    for mc in range(2):
                        pb = self.pbank("a")
                        for k in range(16):
                            S.I("pe", lambda e, W=W, k=k, mc=mc, pb=pb, l=l: e.matmul(
                                out=pb.t[:, 0:256], lhsT=mhT[l].t[:, k, mc * 128:(mc + 1) * 128], rhs=W[:, k, :],
                                start=(k == 0), stop=(k == 15)), [Wb, mhT[l]], [pb], inc=(k == 15))
                        S.I("act", lambda e, pb=pb, l=l, mc=mc, s2=s2: e.copy(
                            out=self.vm[l].t[:, mc, s2 * 256:(s2 + 1) * 256], in_=pb.t[:, 0:256]), [pb], [self.vm[l]])
            S.barrier()

    def stage_mix1(self, l, blk):
        S = self.S
        tok0 = blk * TB
        NT = TB // 128
        with ExitStack() as es:
            hT = self.sb(es, "m1_h", [128, 16, TB], BF16)
            ebq = self.sb(es, "m1_ebq", [128, 8, TB], BF16)
            ebk = self.sb(es, "m1_ebk", [128, 8, TB], BF16)
            self.set_pools(a=range(0, 6), t=range(6, 8))
            with ExitStack() as es2:
                xt = self.sb(es2, "m1_x", [128, 16, 512], F32)
                sq = [self.sb(es2, f"m1_sq{i}", [128, 512], BF16) for i in range(4)]
                rs = self.sb(es2, "m1_rs", [128, 512], F32)
                for h in range(NH):
                    self.load_x(xt, blk, h, 1)
                    self.make_hT(xt, hT, self.vec("mix_norm", l), 1, sq, rs, hoff=h)
                S.barrier()
            gw2 = self.sb(es, "m1_gw2", [16, 1024], F32)
            gnb = self.sb(es, "m1_gnb", [128, D], F32)
            ngb = self.sb(es, "m1_ngb", [128, 8], F32)
            rT = self.sb(es, "m1_rT", [16, TB], F32)
            dect = self.sb(es, "m1_dec", [128, 8, TB // 64], F32)
            self.ld(gw2.t[:, :], self.gw2_d[l], gw2)
            self.ld(gnb.t[:, :], self.gnb_d[l], gnb)
            S.I("dve", lambda e: e.tensor_scalar(out=ngb.t[:, :], in0=self.vec("gla_gate_b", l), scalar1=-1.0, scalar2=None,
                                                 op0=ALU.mult), [self.vecs], [ngb])
            R, Rb = self.wload("mix_w_in", l, 0, 16, C_R, 16)
            for h in range(NH):
                pb = self.pbank("a")
                for k in range(16):
                    S.I("pe", lambda e, k=k, h=h, pb=pb: e.matmul(out=pb.t[0:16, :], lhsT=R[:, k, 0:16],
                                                                  rhs=hT.t[:, k, h * 512:(h + 1) * 512],
                                                                  start=(k == 0), stop=(k == 15)), [Rb, hT], [pb], inc=(k == 15))
                S.I("act", lambda e, h=h, pb=pb: e.copy(out=rT.t[0:16, h * 512:(h + 1) * 512], in_=pb.t[0:16, :]), [pb], [rT])
            with ExitStack() as es2:
                vtl = [self.sb(es2, f"m1_vt{i}", [128, NT, 256], BF16) for i in range(2)]
                stl = [self.sb(es2, f"m1_st{i}", [128, 2, 256], F32) for i in range(2)]
                spt = [self.sb(es2, f"m1_sp{i}", [128, TB], F32) for i in range(2)]
                cum = [self.sb(es2, f"m1_cum{i}", [128, TB], F32) for i in range(2)]
                et = [self.sb(es2, f"m1_et{i}", [128, 512], F32) for i in range(2)]
                def decay(dkc):
                    sp, cm = spt[dkc % 2], cum[dkc % 2]
                    for h in range(NH):
                        pb = self.pbank("a")
                        ee = et[h % 2]
                        S.I("pe", lambda e, dkc=dkc, h=h, pb=pb: e.matmul(
                            out=pb.t[:, :], lhsT=gw2.t[0:16, dkc * 128:(dkc + 1) * 128], rhs=rT.t[0:16, h * 512:(h + 1) * 512],
                            start=True, stop=True), [gw2, rT], [pb])
                        S.I("act", lambda e, dkc=dkc, pb=pb, ee=ee: e.activation(out=ee.t[:, :], in_=pb.t[:, :], func=AF.Exp,
                                                                                 scale=-1.0, bias=ngb.t[:, dkc:dkc + 1]),
                            [pb, ngb], [ee])
                        S.I("act", lambda e, h=h, sp=sp, ee=ee: e.activation(out=sp.t[:, h * 512:(h + 1) * 512], in_=ee.t[:, :],
                                                                             func=AF.Ln, scale=1.0, bias=self.vec("one")),
                            [ee, self.vecs], [sp])
                    S.I("dve", lambda e, sp=sp, cm=cm: e.tensor_tensor_scan(out=cm.t[:, :], data0=self.cmask, data1=sp.t[:, :],
                                                                            initial=0.0, op0=ALU.mult, op1=ALU.add),
                        [sp, self.cst], [cm])
                    S.I("act", lambda e, cm=cm, dkc=dkc: e.activation(out=ebq.t[:, dkc, :], in_=cm.t[:, :], func=AF.Exp,
                                                                      scale=-1.0 / 16), [cm], [ebq.s(dkc)])
                    S.I("act", lambda e, cm=cm, dkc=dkc: e.activation(out=ebk.t[:, dkc, :], in_=cm.t[:, :], func=AF.Exp,
                                                                      scale=1.0 / 16), [cm], [ebk.s(dkc)])
                    S.I("act", lambda e, cm=cm, dkc=dkc: e.activation(
                        out=dect.t[:, dkc, :], in_=cm.t[:, :].rearrange("p (c t) -> p c t", t=64)[:, :, 63], func=AF.Exp,
                        scale=-1.0 / 16), [cm], [dect])

                ti = 0
                for which in ("v", "g"):
                    cbase = C_V if which == "v" else C_G
                    dst = self.vTok_d if which == "v" else self.sgn_d
                    for s8 in range(8):
                        if which == "v":
                            decay(s8)
                        W, Wb = self.wload("mix_w_in", l, 0, 16, cbase + s8 * 256, 256)
                        vt = vtl[s8 % 2]
                        for jp in range(NT // 2):
                            pb = self.pbank("a")
                            for jj in range(2):
                                j = 2 * jp + jj
                                for k in range(16):
                                    S.I("pe", lambda e, W=W, k=k, j=j, jj=jj, pb=pb: e.matmul(
                                        out=pb.t[:, jj * 256:(jj + 1) * 256], lhsT=hT.t[:, k, j * 128:(j + 1) * 128], rhs=W[:, k, :],
                                        start=(k == 0), stop=(k == 15)), [Wb, hT], [pb], inc=(k == 15))
                            if which == "v":
                                S.I("act", lambda e, vt=vt, jp=jp, pb=pb: e.copy(
                                    out=vt.t[:, 2 * jp:2 * jp + 2, :], in_=pb.t[:, :].rearrange("p (j c) -> p j c", j=2)),
                                    [pb], [vt.s(jp)])
                            else:
                                tt = stl[ti % 2]
                                ti += 1
                                S.I("act", lambda e, tt=tt, pb=pb: e.activation(
                                    out=tt.t[:, :, :], in_=pb.t[:, :].rearrange("p (j c) -> p j c", j=2), func=AF.Silu), [pb], [tt])
                                for jj in range(2):
                                    S.I("dve", lambda e, tt=tt, vt=vt, jp=jp, jj=jj, s8=s8: e.tensor_tensor(
                                        out=vt.t[:, 2 * jp + jj, :], in0=tt.t[:, jj, :], in1=gnb.t[:, s8 * 256:(s8 + 1) * 256],
                                        op=ALU.mult), [tt, gnb], [vt.s(jp)])
                        self.st(dst[tok0:tok0 + TB, s8 * 256:(s8 + 1) * 256].rearrange("(j p) c -> p j c", p=128), vt.t[:, :, :], vt)
                    if which == "v":
                        self.st(self.dec_d[:, :, blk * (TB // 64):(blk + 1) * (TB // 64)], dect.t[:, :, :], dect)
                S.barrier()
            with ExitStack() as es2:
                ob = [self.sb(es2, f"m1_ob{i}", [128, 512], BF16) for i in range(4)]
                ktl = [self.sb(es2, f"m1_kt{i}", [128, 4, 128], BF16) for i in range(2)]
                oi = 0
                for which in ("q", "k"):
                    cbase = C_Q if which == "q" else C_K
                    for s4 in range(4):
                        W, Wb = self.wload("mix_w_in", l, 0, 16, cbase + s4 * 256, 256)
                        for c in range(2):
                            dkc = 2 * s4 + c
                            pbs = self.fm_out(W, Wb, c, hT, NH)
                            for h in range(NH):
                                o = ob[oi % 4]
                                oi += 1
                                cols = slice(tok0 + h * 512, tok0 + (h + 1) * 512)
                                if which == "q":
                                    S.I("dve", lambda e, o=o, p=pbs[h], dkc=dkc, h=h: e.scalar_tensor_tensor(
                                        out=o.t[:, :], in0=p.t[:, :], scalar=0.0625, in1=ebq.t[:, dkc, h * 512:(h + 1) * 512],
                                        op0=ALU.mult, op1=ALU.mult), [pbs[h], ebq.s(dkc)], [o])
                                    self.st(self.qT_d[dkc * 128:(dkc + 1) * 128, cols], o.t[:, :], o)
                                else:
                                    S.I("dve", lambda e, o=o, p=pbs[h], dkc=dkc, h=h: e.tensor_tensor(
                                        out=o.t[:, :], in0=p.t[:, :], in1=ebk.t[:, dkc, h * 512:(h + 1) * 512], op=ALU.mult),
                                        [pbs[h], ebk.s(dkc)], [o])
                                    self.st(self.kT_d[dkc * 128:(dkc + 1) * 128, cols], o.t[:, :], o)
                                    pt = self.pbank("t")
                                    ptb = self.bview(pt)
                                    for j in range(4):
                                        S.I("pe", lambda e, ptb=ptb, o=o, j=j: e.transpose(
                                            out=ptb[:, j * 128:(j + 1) * 128], in_=o.t[:, j * 128:(j + 1) * 128],
                                            identity=self.identb.t[:, :]), [o, self.identb], [pt], inc=(j == 3))
                                    kt = ktl[oi % 2]
                                    S.I("act", lambda e, kt=kt, ptb=ptb: e.copy(
                                        out=kt.t[:, :, :], in_=ptb[:, 0:512].rearrange("p (j d) -> p j d", j=4)), [pt], [kt])
                                    self.st(self.kTok_d[tok0 + h * 512:tok0 + (h + 1) * 512, dkc * 128:(dkc + 1) * 128]
                                            .rearrange("(j p) d -> p j d", p=128), kt.t[:, :, :], kt)
                S.barrier()
            with ExitStack() as es2:
                ctl = [self.sb(es2, f"m1_ct{i}", [128, 512], F32) for i in range(4)]
                sgt = [self.sb(es2, f"m1_sg{i}", [128, 512], F32) for i in range(2)]
                gtl = [self.sb(es2, f"m1_gt{i}", [128, 512], BF16) for i in range(4)]
                ci = 0
                for s8 in range(8):
                    UA, UAb = self.wload("mix_w_in", l, 0, 16, C_UA + s8 * 256, 256)
                    UG, UGb = self.wload("mix_w_in", l, 0, 16, C_UG + s8 * 256, 256)
                    for c in range(2):
                        ch = 2 * s8 + c
                        pa = self.fm_out(UA, UAb, c, hT, NH)
                        pg = self.fm_out(UG, UGb, c, hT, NH)
                        for h in range(NH):
                            sg, ct = sgt[ci % 2], ctl[ci % 4]
                            ci += 1
                            S.I("act", lambda e, sg=sg, p=pg[h]: e.activation(out=sg.t[:, :], in_=p.t[:, :], func=AF.Sigmoid),
                                [pg[h]], [sg])
                            S.I("dve", lambda e, sg=sg, ct=ct, p=pa[h]: e.tensor_tensor(out=ct.t[:, :], in0=sg.t[:, :], in1=p.t[:, :],
                                                                                          op=ALU.mult), [sg, pa[h]], [ct])
                            self.st(self.cT_d[ch * 128:(ch + 1) * 128, HALO + tok0 + h * 512:HALO + tok0 + (h + 1) * 512],
                                    ct.t[:, :], ct)
                bgb = self.vec("branch_gate_b", l)
                for gi_, (cb_, dst) in enumerate(((C_G0, self.g0T_d), (C_G1, self.g1T_d))):
                    for s8 in range(8):
                        W, Wb = self.wload("mix_w_in", l, 0, 16, cb_ + s8 * 256, 256)
                        for c in range(2):
                            ch = 2 * s8 + c
                            pbs = self.fm_out(W, Wb, c, hT, NH)
                            for h in range(NH):
                                gt = gtl[ci % 4]
                                ci += 1
                                S.I("act", lambda e, gt=gt, p=pbs[h], ch=ch, gi_=gi_: e.activation(
                                    out=gt.t[:, :], in_=p.t[:, :], func=AF.Sigmoid, bias=bgb[:, gi_ * 16 + ch:gi_ * 16 + ch + 1],
                                    scale=1.0), [pbs[h], self.vecs], [gt])
                                self.st(dst[ch * 128:(ch + 1) * 128, tok0 + h * 512:tok0 + (h + 1) * 512], gt.t[:, :], gt)
                S.barrier()
            S.barrier()

    def stage_pass1(self, l):
        S = self.S
        self.set_pools(a=range(8))
        with ExitStack() as es:
            U = self.sb(es, "p1_U", [128, 8, 512], F32)
            dec = self.sb(es, "p1_dec", [128, 8, NCHUNK], F32)
            ktl = [self.sb(es, f"p1_k{i}", [64, 1024], BF16) for i in range(3)]
            vtl = [self.sb(es, f"p1_v{i}", [64, D], BF16) for i in range(3)]
            dtot = self.sb(es, "p1_dt", [128, 8], F32)
            tail = self.sb(es, "p1_tail", [128, 16, 30], F32)
            self.ld(dec.t[:, :, :], self.dec_d, dec)
            for c in range(NCHUNK):
                kt, vt = ktl[c % 3], vtl[c % 3]
                self.ld(kt.t[:, :], self.kTok_d[c * 64:(c + 1) * 64, :], kt)
                self.ld(vt.t[:, :], self.vTok_d[c * 64:(c + 1) * 64, :], vt)
                for i in range(8):
                    hd, half = i // 2, i % 2
                    pb = self.pbank("a")
                    S.I("pe", lambda e, kt=kt, vt=vt, hd=hd, half=half, pb=pb: e.matmul(
                        out=pb.t[:, :], lhsT=kt.t[0:64, hd * 256 + half * 128:hd * 256 + (half + 1) * 128],
                        rhs=vt.t[0:64, hd * 512:(hd + 1) * 512], start=True, stop=True), [kt, vt], [pb])
                    if c == 0:
                        S.I("dve", lambda e, i=i, pb=pb: e.tensor_copy(out=U.t[:, i, :], in_=pb.t[:, :]), [pb], [U.s(i)])
                    else:
                        S.I("dve", lambda e, i=i, pb=pb, c=c: e.scalar_tensor_tensor(
                            out=U.t[:, i, :], in0=U.t[:, i, :], scalar=dec.t[:, i, c - 1:c], in1=pb.t[:, :],
                            op0=ALU.mult, op1=ALU.add), [pb, dec], [U.s(i)])
            for i in range(8):
                S.I("dve", lambda e, i=i: e.tensor_scalar(out=U.t[:, i, :], in0=U.t[:, i, :], scalar1=dec.t[:, i, NCHUNK - 1:NCHUNK],
                                                          scalar2=None, op0=ALU.mult), [dec], [U.s(i)])
            for q in range(4):
                self.st(self.exS_in[l][q].rearrange("(i p) e -> p i e", p=128), U.t[:, 2 * q:2 * q + 2, :], U)
            S.I("dve", lambda e: e.tensor_reduce(out=dtot.t[:, :], in_=dec.t[:, :, :], axis=AXX, op=ALU.mult), [dec], [dtot])
            self.st(self.exM_in[l][D:D + 128, 0:8], dtot.t[:, :], dtot)
            self.ld(tail.t[:, :, :], self.cT_d[:, HALO + T - 30:HALO + T].rearrange("(k p) c -> p k c", p=128), tail)
            self.st(self.exM_in[l][0:D, 0:30].rearrange("(k p) c -> p k c", p=128), tail.t[:, :, :], tail)
            S.barrier()

    def _exchange_zero(self):
        S = self.S
        with ExitStack() as es:
            Sin = self.sb(es, "ex_S", [128, 8, 512], F32)
            hl = self.sb(es, "ex_h", [128, 16, 30], F32)
            S.I("dve", lambda e: e.memset(Sin.t[:, :, :], 0.0), [], [Sin])
            S.I("dve", lambda e: e.memset(hl.t[:, :, :], 0.0), [], [hl])
            self.st(self.st_d, Sin.t[:, :, :], Sin)
            self.st(self.cT_d[:, 2:32].rearrange("(k q) c -> q k c", q=128), hl.t[:, :, :], hl)
            S.barrier()

    def stage_exchange(self, l):
        S = self.S
        groups = [[0, 1, 2, 3], [4, 5, 6, 7]]
        noexch = bool(self.debug.get("noexch"))
        if noexch:
            return self._exchange_zero()
        S.waits("pool", {k: v for k, v in S.cnt.items() if v > 0 and (k in ("dve", "act", "pe", "sp") or k.startswith("st"))})
        pairs = [(self.exS_in[l][q], self.exS_out[l][q]) for q in range(4)] + [(self.exM_in[l], self.exM_out[l])]
        for src, dst in pairs:
            S.X("pool", "cc", 1, lambda e, src=src, dst=dst: e.collective_compute(
                "AllGather", ALU.bypass, replica_groups=groups, ins=[src], outs=[dst]))
        ccv = {"cc": S.cnt["cc"]}
        for eng in ("sp", "dve", "act", "pe"):
            S.waits(eng, ccv)
        R = D + 128
        rmask, rmaskc, roh = self.vec("rmask"), self.vec("rmaskc"), self.vec("ronehot")
        with ExitStack() as es:
            Dp = self.sb(es, "ex_D", [128, 4, 8], F32)
            av = self.sb(es, "ex_a", [128, 4, 8], F32)
            Sin = self.sb(es, "ex_S", [128, 8, 512], F32)
            tl = [self.sb(es, f"ex_t{i}", [128, 8, 512], F32) for i in range(2)]
            hl = self.sb(es, "ex_h", [128, 16, 30], F32)
            ht = [self.sb(es, f"ex_ht{i}", [128, 16, 30], F32) for i in range(2)]
            for p in range(3):
                self.ld(Dp.t[:, p, :], self.exM_out[l][p * R + D:p * R + D + 128, 0:8], Dp)
                S.I("dve", lambda e, p=p: e.tensor_scalar(out=av.t[:, p, :], in0=Dp.t[:, p, :], scalar1=rmask[:, p:p + 1],
                                                          scalar2=rmaskc[:, p:p + 1], op0=ALU.mult, op1=ALU.add),
                    [Dp, self.vecs], [av])
            S.I("dve", lambda e: e.memset(Sin.t[:, :, :], 0.0), [], [Sin])
            S.I("dve", lambda e: e.memset(hl.t[:, :, :], 0.0), [], [hl])
            for p in range(3):
                t = tl[p % 2]
                for q in range(4):
                    self.ld(t.t[:, 2 * q:2 * q + 2, :], self.exS_out[l][q][p * 256:(p + 1) * 256, :].rearrange("(i q) e -> q i e", q=128), t)
                for i in range(8):
                    S.I("dve", lambda e, t=t, i=i, p=p: e.tensor_scalar(out=t.t[:, i, :], in0=t.t[:, i, :], scalar1=rmask[:, p:p + 1],
                                                                        scalar2=None, op0=ALU.mult), [self.vecs], [t.s(i)])
                    S.I("dve", lambda e, t=t, i=i, p=p: e.scalar_tensor_tensor(
                        out=Sin.t[:, i, :], in0=Sin.t[:, i, :], scalar=av.t[:, p, i:i + 1], in1=t.t[:, i, :],
                        op0=ALU.mult, op1=ALU.add), [t.s(i), av], [Sin.s(i)])
                h = ht[p % 2]
                self.ld(h.t[:, :, :], self.exM_out[l][p * R:p * R + D, 0:30].rearrange("(k q) c -> q k c", q=128), h)
                S.I("dve", lambda e, h=h, p=p: e.scalar_tensor_tensor(out=hl.t[:, :, :], in0=h.t[:, :, :], scalar=roh[:, p:p + 1],
                                                                      in1=hl.t[:, :, :], op0=ALU.mult, op1=ALU.add),
                    [h, self.vecs], [hl])
            self.st(self.st_d, Sin.t[:, :, :], Sin)
            self.st(self.cT_d[:, 2:32].rearrange("(k q) c -> q k c", q=128), hl.t[:, :, :], hl)
            S.barrier()


    def stage_mix2(self, l, blk):
        S = self.S
        tok0 = blk * TB
        c0 = blk * (TB // 64)
        with ExitStack() as es:
            ogT = self.sb(es, "m2_og", [128, 16, TB], BF16)
            with ExitStack() as es2:
                U = self.sb(es2, "m2_U", [128, 8, 512], F32)
                Sbfl = [self.sb(es2, f"m2_Sbf{i}", [128, 8, 512], BF16) for i in range(2)]
                dec = self.sb(es2, "m2_dec", [128, 8, NCHUNK], F32)
                qpl = [self.sb(es2, f"m2_q{i}", [128, 8, 128], BF16) for i in range(2)]
                kpl = [self.sb(es2, f"m2_k{i}", [128, 8, 128], BF16) for i in range(2)]
                ktl = [self.sb(es2, f"m2_kt{i}", [64, 2, 1024], BF16) for i in range(2)]
                vtl = [self.sb(es2, f"m2_vt{i}", [64, 2, D], BF16) for i in range(2)]
                sgl = [self.sb(es2, f"m2_sg{i}", [128, D], BF16) for i in range(2)]
                onl = [self.sb(es2, f"m2_on{i}", [128, D], BF16) for i in range(2)]
                scl = [self.sb(es2, f"m2_sc{i}", [64, 4, 64], BF16) for i in range(4)]
                junk = self.sb(es2, "m2_junk", [128, 512], BF16)
                ssl = [self.sb(es2, f"m2_ss{i}", [128, 4], F32) for i in range(2)]
                self.set_pools(s=[4], kv=[5, 6, 7], t=[4])
                po = [self.ps[hd] for hd in range(4)]
                self.ld(dec.t[:, :, :], self.dec_d, dec)
                self.ld(U.t[:, :, :], self.st_d, U)
                Sbf = Sbfl[(c0 - 1) % 2]
                for i in range(8):
                    if blk == 0:
                        S.I("act", lambda e, i=i, Sbf=Sbf: e.copy(out=Sbf.t[:, i, :], in_=U.t[:, i, :]), [U], [Sbf.s(i)])
                    else:
                        S.I("act", lambda e, i=i, Sbf=Sbf: e.activation(out=Sbf.t[:, i, :], in_=U.t[:, i, :], func=AF.Identity,
                                                                        scale=dec.t[:, i, c0 - 1:c0]), [U, dec], [Sbf.s(i)])
                sci = 0
                for pr in range(TB // 128):
                    qp, kp, kt, vt, sg, on, ss = qpl[pr % 2], kpl[pr % 2], ktl[pr % 2], vtl[pr % 2], sgl[pr % 2], onl[pr % 2], ssl[pr % 2]
                    t0 = tok0 + pr * 128
                    self.ld(qp.t[:, :, :], self.qT_d[:, t0:t0 + 128].rearrange("(k p) t -> p k t", p=128), qp)
                    self.ld(kp.t[:, :, :], self.kT_d[:, t0:t0 + 128].rearrange("(k p) t -> p k t", p=128), kp)
                    self.ld(kt.t[:, :, :], self.kTok_d[t0:t0 + 128, :].rearrange("(c p) d -> p c d", p=64), kt)
                    self.ld(vt.t[:, :, :], self.vTok_d[t0:t0 + 128, :].rearrange("(c p) d -> p c d", p=64), vt)
                    self.ld(sg.t[:, :], self.sgn_d[t0:t0 + 128, :], sg)
                    for ci in range(2):
                        c = c0 + 2 * pr + ci
                        lo = ci * 64
                        Sprev, Snew = Sbfl[(c - 1) % 2], Sbfl[c % 2]
                        psb = self.pbank("s")
                        for hd in range(4):
                            for half in range(2):
                                dkc = 2 * hd + half
                                S.I("pe", lambda e, psb=psb, kp=kp, qp=qp, dkc=dkc, lo=lo, half=half, hd=hd: e.matmul(
                                    out=psb.t[0:64, hd * 64:(hd + 1) * 64], lhsT=kp.t[:, dkc, lo:lo + 64], rhs=qp.t[:, dkc, lo:lo + 64],
                                    start=(half == 0), stop=(half == 1)), [kp, qp], [psb], inc=(half == 1))
                        sc = scl[sci % 4]
                        sci += 1
                        S.I("dve", lambda e, sc=sc, psb=psb: e.tensor_tensor(
                            out=sc.t[:, :, :], in0=psb.t[0:64, 0:256].rearrange("p (h i) -> p h i", h=4),
                            in1=self.mask64.unsqueeze(1).to_broadcast([64, 4, 64]), op=ALU.mult), [psb, self.cst], [sc])
                        for hd in range(4):
                            for half in range(2):
                                i = 2 * hd + half
                                pkv = self.pbank("kv")
                                S.I("pe", lambda e, pkv=pkv, kt=kt, vt=vt, ci=ci, hd=hd, half=half: e.matmul(
                                    out=pkv.t[:, :], lhsT=kt.t[0:64, ci, hd * 256 + half * 128:hd * 256 + (half + 1) * 128],
                                    rhs=vt.t[0:64, ci, hd * 512:(hd + 1) * 512], start=True, stop=True), [kt, vt], [pkv])
                                if c == 0:
                                    S.I("dve", lambda e, i=i, pkv=pkv: e.tensor_tensor(out=U.t[:, i, :], in0=U.t[:, i, :], in1=pkv.t[:, :],
                                                                                       op=ALU.add), [pkv], [U.s(i)])
                                else:
                                    S.I("dve", lambda e, i=i, pkv=pkv, c=c: e.scalar_tensor_tensor(
                                        out=U.t[:, i, :], in0=U.t[:, i, :], scalar=dec.t[:, i, c - 1:c], in1=pkv.t[:, :],
                                        op0=ALU.mult, op1=ALU.add), [pkv, dec], [U.s(i)])
                                S.I("act", lambda e, i=i, c=c, Snew=Snew: e.activation(out=Snew.t[:, i, :], in_=U.t[:, i, :], func=AF.Identity,
                                                                                       scale=dec.t[:, i, c:c + 1]), [U.s(i), dec], [Snew.s(i)])
                        for hd in range(4):
                            pob = po[hd]
                            for half in range(2):
                                dkc = 2 * hd + half
                                S.I("pe", lambda e, pob=pob, qp=qp, dkc=dkc, lo=lo, half=half, Sprev=Sprev: e.matmul(
                                    out=pob.t[lo:lo + 64, :], lhsT=qp.t[:, dkc, lo:lo + 64], rhs=Sprev.t[:, dkc, :],
                                    start=(half == 0), stop=False), [qp, Sprev.s(dkc)], [pob], inc=False)
                            S.I("pe", lambda e, pob=pob, sc=sc, vt=vt, ci=ci, hd=hd, lo=lo: e.matmul(
                                out=pob.t[lo:lo + 64, :], lhsT=sc.t[:, hd, :], rhs=vt.t[0:64, ci, hd * 512:(hd + 1) * 512],
                                start=False, stop=True), [sc, vt], [pob], inc=True)
                    for hd in range(4):
                        S.I("act", lambda e, hd=hd, ss=ss: e.activation(out=junk.t[:, :], in_=po[hd].t[:, :], func=AF.Square,
                                                                        accum_out=ss.t[:, hd:hd + 1]), [po[hd]], [junk, ss])
                    S.I("dve", lambda e, ss=ss: e.tensor_scalar(out=ss.t[:, :], in0=ss.t[:, :], scalar1=1.0 / 512, scalar2=EPS,
                                                                op0=ALU.mult, op1=ALU.add), [], [ss])
                    S.I("act", lambda e, ss=ss: e.activation(out=ss.t[:, :], in_=ss.t[:, :], func=AF.Sqrt), [], [ss])
                    S.I("dve", lambda e, ss=ss: e.reciprocal(out=ss.t[:, :], in_=ss.t[:, :]), [], [ss])
                    for hd in range(4):
                        S.I("dve", lambda e, hd=hd, ss=ss, on=on, sg=sg: e.scalar_tensor_tensor(
                            out=on.t[:, hd * 512:(hd + 1) * 512], in0=po[hd].t[:, :], scalar=ss.t[:, hd:hd + 1],
                            in1=sg.t[:, hd * 512:(hd + 1) * 512], op0=ALU.mult, op1=ALU.mult), [po[hd], ss, sg], [on.s(hd)])
                    for e4 in range(4):
                        pt = self.pbank("t")
                        ptb = self.bview(pt)
                        for q in range(4):
                            ec = e4 * 4 + q
                            S.I("pe", lambda e, ptb=ptb, on=on, q=q, ec=ec: e.transpose(
                                out=ptb[:, q * 128:(q + 1) * 128], in_=on.t[:, ec * 128:(ec + 1) * 128], identity=self.identb.t[:, :]),
                                [on.s(e4), self.identb], [pt], inc=(q == 3))
                        S.I("act", lambda e, ptb=ptb, e4=e4, pr=pr: e.copy(
                            out=ogT.t[:, e4 * 4:(e4 + 1) * 4, pr * 128:(pr + 1) * 128],
                            in_=ptb[:, 0:512].rearrange("p (q t) -> p q t", q=4)), [pt], [ogT])
                self.st(self.st_d, U.t[:, :, :], U)
                if self.debug.get("dump"):
                    self.st(self.dbg_og[:, tok0:tok0 + TB].rearrange("(k p) t -> p k t", p=128), ogT.t[:, :, :], ogT)
                S.barrier()
            cw, cb = self.vec("conv_w", l), self.vec("conv_b", l)
            lng, lnb = self.vec("conv_ln_g", l), self.vec("conv_ln_b", l)
            for h in range(NH):
                t0 = tok0 + h * 512
                with ExitStack() as es3:
                    cn = self.sb(es3, "m2_cn", [128, 16, 512], BF16)
                    with ExitStack() as es4:
                        co = self.sb(es4, "m2_co", [128, 16, 512], F32)
                        cin = [self.sb(es4, f"m2_ci{i}", [128, 30 + 512], F32) for i in range(4)]
                        cbf = [self.sb(es4, f"m2_cb{i}", [128, 512], BF16) for i in range(2)]
                        csq = [self.sb(es4, f"m2_cq{i}", [128, 512], BF16) for i in range(2)]
                        mt = self.sb(es4, "m2_mt", [128, 512], F32)
                        vr = self.sb(es4, "m2_vr", [128, 512], F32)
                        self.set_pools(st=[0, 1], c=range(2, 8))
                        ps_s, ps_q = self.pbank("st"), self.pbank("st")
                        cbl = [self.sb(es4, f"m2_cbh{i}", [128, 30 + 512], BF16) for i in range(2)]
                        dgl = [self.sb(es4, f"m2_dg{i}", [128, 31, 128], BF16) for i in range(2)]
                        for k in range(16):
                            ci_, cbh, dg = cin[k % 4], cbl[k % 2], dgl[k % 2]
                            self.ld(ci_.t[:, :], self.cT_d[k * 128:(k + 1) * 128, HALO + t0 - 30:HALO + t0 + 512], ci_)
                            S.I("dve", lambda e, ci_=ci_, cbh=cbh: e.tensor_copy(out=cbh.t[:, :], in_=ci_.t[:, :]), [ci_], [cbh])
                            for tap in range(31):
                                if tap % 2:
                                    S.I("act", lambda e, dg=dg, tap=tap, k=k: e.activation(
                                        out=dg.t[:, tap, :], in_=self.identb.t[:, :], func=AF.Copy,
                                        scale=cw[:, k * 31 + tap:k * 31 + tap + 1]), [self.identb, self.vecs], [dg.s(tap)])
                                else:
                                    S.I("dve", lambda e, dg=dg, tap=tap, k=k: e.tensor_scalar(
                                        out=dg.t[:, tap, :], in0=self.identb.t[:, :], scalar1=cw[:, k * 31 + tap:k * 31 + tap + 1],
                                        scalar2=None, op0=ALU.mult), [self.identb, self.vecs], [dg.s(tap)])
                            pc = self.pbank("c")
                            for tap in range(31):
                                S.I("pe", lambda e, pc=pc, dg=dg, cbh=cbh, tap=tap: e.matmul(
                                    out=pc.t[:, :], lhsT=dg.t[:, tap, :], rhs=cbh.t[:, tap:tap + 512],
                                    start=(tap == 0), stop=(tap == 30)), [dg.s(tap), cbh], [pc], inc=(tap == 30))
                            S.I("act", lambda e, pc=pc, k=k: e.activation(out=co.t[:, k, :], in_=pc.t[:, :], func=AF.Identity,
                                                                          bias=cb[:, k:k + 1], scale=1.0), [pc, self.vecs], [co.s(k)])
                            b1, b2 = cbf[k % 2], csq[k % 2]
                            S.I("dve", lambda e, k=k, b1=b1: e.tensor_copy(out=b1.t[:, :], in_=co.t[:, k, :]), [co.s(k)], [b1])
                            S.I("act", lambda e, k=k, b2=b2: e.activation(out=b2.t[:, :], in_=co.t[:, k, :], func=AF.Square),
                                [co.s(k)], [b2])
                            S.I("pe", lambda e, k=k, b1=b1: e.matmul(out=ps_s.t[:, :], lhsT=self.onesb.t[:, :], rhs=b1.t[:, :],
                                                                     start=(k == 0), stop=(k == 15)), [b1, self.onesb], [ps_s])
                            S.I("pe", lambda e, k=k, b2=b2: e.matmul(out=ps_q.t[:, :], lhsT=self.onesb.t[:, :], rhs=b2.t[:, :],
                                                                     start=(k == 0), stop=(k == 15)), [b2, self.onesb], [ps_q])
                        S.I("act", lambda e: e.activation(out=mt.t[:, :], in_=ps_s.t[:, :], func=AF.Copy, scale=1.0 / D), [ps_s], [mt])
                        S.I("dve", lambda e: e.tensor_tensor(out=vr.t[:, :], in0=mt.t[:, :], in1=mt.t[:, :], op=ALU.mult), [mt], [vr])
                        S.I("dve", lambda e: e.scalar_tensor_tensor(out=vr.t[:, :], in0=ps_q.t[:, :], scalar=1.0 / D, in1=vr.t[:, :],
                                                                    op0=ALU.mult, op1=ALU.subtract), [ps_q], [vr])
                        S.I("act", lambda e: e.activation(out=vr.t[:, :], in_=vr.t[:, :], func=AF.Sqrt, bias=self.vec("eps"), scale=1.0),
                            [self.vecs], [vr])
                        S.I("dve", lambda e: e.reciprocal(out=vr.t[:, :], in_=vr.t[:, :]), [], [vr])
                        for k in range(16):
                            S.I("dve", lambda e, k=k: e.tensor_tensor(out=co.t[:, k, :], in0=co.t[:, k, :], in1=mt.t[:, :],
                                                                      op=ALU.subtract), [mt], [co.s(k)])
                            S.I("dve", lambda e, k=k: e.tensor_tensor(out=co.t[:, k, :], in0=co.t[:, k, :], in1=vr.t[:, :],
                                                                      op=ALU.mult), [vr], [co.s(k)])
                            S.I("act", lambda e, k=k: e.activation(out=cn.t[:, k, :], in_=co.t[:, k, :], func=AF.Silu,
                                                                   scale=lng[:, k:k + 1], bias=lnb[:, k:k + 1]),
                                [co.s(k), self.vecs], [cn.s(k)])
                        S.barrier()
                    m = self.sb(es3, "m2_m", [128, 16, 512], BF16)
                    xt = self.sb(es3, "m2_x", [128, 16, 512], F32)
                    g0l = [self.sb(es3, f"m2_g0{i}", [128, 512], BF16) for i in range(2)]
                    g1l = [self.sb(es3, f"m2_g1{i}", [128, 512], BF16) for i in range(2)]
                    t1l = [self.sb(es3, f"m2_t1{i}", [128, 512], F32) for i in range(2)]
                    t2l = [self.sb(es3, f"m2_t2{i}", [128, 512], F32) for i in range(2)]
                    self.set_pools(a=range(8))
                    self.load_x(xt, blk, h, 1)
                    for s8 in range(8):
                        GP, GPb = self.wload("gla_proj", l, 0, 16, s8 * 256, 256)
                        CP, CPb = self.wload("conv_proj", l, 0, 16, s8 * 256, 256)
                        for c in range(2):
                            j = 2 * s8 + c
                            pa = self.fm_out(GP, GPb, c, ogT, 1, hoff=h)[0]
                            pb = self.fm_out(CP, CPb, c, cn, 1)[0]
                            g0, g1, t1, t2 = g0l[j % 2], g1l[j % 2], t1l[j % 2], t2l[j % 2]
                            self.ld(g0.t[:, :], self.g0T_d[j * 128:(j + 1) * 128, t0:t0 + 512], g0)
                            self.ld(g1.t[:, :], self.g1T_d[j * 128:(j + 1) * 128, t0:t0 + 512], g1)
                            S.I("dve", lambda e, t1=t1, pa=pa, g0=g0: e.tensor_tensor(out=t1.t[:, :], in0=pa.t[:, :], in1=g0.t[:, :],
                                                                                      op=ALU.mult), [pa, g0], [t1])
                            S.I("dve", lambda e, t2=t2, pb=pb, g1=g1: e.tensor_tensor(out=t2.t[:, :], in0=pb.t[:, :], in1=g1.t[:, :],
                                                                                      op=ALU.mult), [pb, g1], [t2])
                            S.I("dve", lambda e, t1=t1, t2=t2, j=j: e.tensor_tensor(out=m.t[:, j, :], in0=t1.t[:, :], in1=t2.t[:, :],
                                                                                    op=ALU.add), [t1, t2], [m.s(j, 0)])
                    if self.debug.get("dump"):
                        self.st(self.dbg_cn[:, t0:t0 + 512].rearrange("(k p) t -> p k t", p=128), cn.t[:, :, :], cn)
                        self.st(self.dbg_m[:, t0:t0 + 512].rearrange("(k p) t -> p k t", p=128), m.t[:, :, :], m)
                    for s8 in range(8):
                        WO, WOb = self.wload("mix_w_out", l, 0, 16, s8 * 256, 256)
                        for c in range(2):
                            j = 2 * s8 + c
                            py = self.fm_out(WO, WOb, c, m, 1)[0]
                            S.I("dve", lambda e, j=j, py=py: e.tensor_tensor(out=xt.t[:, j, :], in0=py.t[:, :], in1=xt.t[:, j, :],
                                                                             op=ALU.add), [py], [xt.s(j, 0)])
                    self.store_x(xt, blk, h, 1)
                    S.barrier()
            S.barrier()

    def stage_xattn(self, l, blk):
        S = self.S
        for h in range(NH):
            with ExitStack() as es:
                xt = self.sb(es, "xa_x", [128, 16, 512], F32)
                hT = self.sb(es, "xa_h", [128, 16, 512], BF16)
                sq = [self.sb(es, f"xa_sq{i}", [128, 512], BF16) for i in range(4)]
                rs = self.sb(es, "xa_rs", [128, 512], F32)
                qT = self.sb(es, "xa_q", [128, 4, 512], BF16)
                pTl = [self.sb(es, f"xa_pT{i}", [128, 2, 512], BF16) for i in range(2)]
                oT = self.sb(es, "xa_o", [128, 4, 512], BF16)
                pel = [self.sb(es, f"xa_pe{i}", [128, 256], F32) for i in range(2)]
                pnl = [self.sb(es, f"xa_pn{i}", [128, 256], BF16) for i in range(2)]
                sml = [self.sb(es, f"xa_sm{i}", [128, 4], F32) for i in range(4)]
                self.set_pools(a=range(0, 4), s=range(4, 6), t=range(6, 8))
                self.load_x(xt, blk, h, 1)
                self.make_hT(xt, hT, self.vec("xa_norm", l), 1, sq, rs)
                for s2 in range(2):
                    W, Wb = self.wload("xa_w_q", l, 0, 16, s2 * 256, 256)
                    for c in range(2):
                        hd = 2 * s2 + c
                        pb = self.fm_out(W, Wb, c, hT, 1)[0]
                        S.I("act", lambda e, pb=pb, hd=hd: e.activation(out=qT.t[:, hd, :], in_=pb.t[:, :], func=AF.Copy,
                                                                        scale=128 ** -0.5), [pb], [qT.s(hd)])
                it = 0
                for hd in range(4):
                    pT = pTl[hd % 2]
                    for j in range(4):
                        psb = self.pbank("s")
                        pe_, pn, sm = pel[it % 2], pnl[it % 2], sml[it % 4]
                        it += 1
                        S.I("pe", lambda e, psb=psb, hd=hd, j=j: e.matmul(out=psb.t[:, 0:256], lhsT=qT.t[:, hd, j * 128:(j + 1) * 128],
                                                                          rhs=self.kmT[l].t[:, hd, :], start=True, stop=True),
                            [qT.s(hd), self.kmT[l]], [psb])
                        S.I("dve", lambda e, psb=psb, sm=sm: e.tensor_reduce(out=sm.t[:, 0:1], in_=psb.t[:, 0:256], axis=AXX, op=ALU.max,
                                                                             negate=True), [psb], [sm])
                        S.I("act", lambda e, psb=psb, sm=sm, pe_=pe_: e.activation(out=pe_.t[:, :], in_=psb.t[:, 0:256], func=AF.Exp,
                                                                                   bias=sm.t[:, 0:1], scale=1.0, accum_out=sm.t[:, 1:2]),
                            [psb, sm], [pe_, sm])
                        S.I("dve", lambda e, sm=sm: e.reciprocal(out=sm.t[:, 2:3], in_=sm.t[:, 1:2]), [], [sm])
                        S.I("dve", lambda e, sm=sm, pe_=pe_, pn=pn: e.tensor_scalar(out=pn.t[:, :], in0=pe_.t[:, :], scalar1=sm.t[:, 2:3],
                                                                                    scalar2=None, op0=ALU.mult), [pe_, sm], [pn])
                        pt = self.pbank("t")
                        ptb = self.bview(pt)
                        for mc in range(2):
                            S.I("pe", lambda e, ptb=ptb, pn=pn, mc=mc: e.transpose(out=ptb[:, mc * 128:(mc + 1) * 128],
                                                                                   in_=pn.t[:, mc * 128:(mc + 1) * 128],
                                                                                   identity=self.identb.t[:, :]),
                                [pn, self.identb], [pt], inc=(mc == 1))
                        S.I("act", lambda e, ptb=ptb, pT=pT, j=j: e.copy(out=pT.t[:, :, j * 128:(j + 1) * 128],
                                                                         in_=ptb[:, 0:256].rearrange("p (m t) -> p m t", m=2)),
                            [pt], [pT])
                    po = self.pbank("a")
                    for mc in range(2):
                        S.I("pe", lambda e, po=po, mc=mc, hd=hd, pT=pT: e.matmul(out=po.t[:, :], lhsT=self.vm[l].t[:, mc, hd * 128:(hd + 1) * 128],
                                                                                 rhs=pT.t[:, mc, :], start=(mc == 0), stop=(mc == 1)),
                            [self.vm[l], pT], [po], inc=(mc == 1))
                    S.I("dve", lambda e, po=po, hd=hd: e.tensor_copy(out=oT.t[:, hd, :], in_=po.t[:, :]), [po], [oT.s(hd)])
                for s2 in range(2):
                    W, Wb = self.wload("xa_w_out", l, 0, 4, s2 * 1024, 1024)
                    for jj in range(8):
                        j = s2 * 8 + jj
                        py = self.pbank("a")
                        for kc in range(4):
                            S.I("pe", lambda e, py=py, W=W, kc=kc, jj=jj: e.matmul(out=py.t[:, :], lhsT=W[:, kc, jj * 128:(jj + 1) * 128],
                                                                                   rhs=oT.t[:, kc, :], start=(kc == 0), stop=(kc == 3)),
                                [Wb, oT.s(kc)], [py], inc=(kc == 3))
                        S.I("dve", lambda e, j=j, py=py: e.tensor_tensor(out=xt.t[:, j, :], in0=py.t[:, :], in1=xt.t[:, j, :], op=ALU.add),
                            [py], [xt.s(j, 0)])
                self.store_x(xt, blk, h, 1)
                S.barrier()

    def stage_ffn(self, l, blk, which):
        S = self.S
        wi, wo, gn = which + "_w_in", which + "_w_out", which + "_norm"
        with ExitStack() as es:
            xt = self.sb(es, "ffn_x", [128, 16, TB], F32)
            hT = self.sb(es, "ffn_h", [128, 16, TB], BF16)
            sq = [self.sb(es, f"ffn_sq{i}", [128, 512], BF16) for i in range(4)]
            rs = self.sb(es, "ffn_rs", [128, 512], F32)
            act = [self.sb(es, f"ffn_act{i}", [128, 4, TB], BF16) for i in range(2)]
            tmp = [self.sb(es, f"ffn_tmp{i}", [128, 512], F32) for i in range(2)]
            self.set_pools(a=range(0, 4), y=range(4, 8))
            self.load_x(xt, blk, 0, NH)
            self.make_hT(xt, hT, self.vec(gn, l), NH, sq, rs)
            ngrp = DFF // 512
            ti = 0
            for gi in range(ngrp):
                at = act[gi % 2]
                for sgi in range(2):
                    A, Ab = self.wload(wi, l, 0, 16, (2 * gi + sgi) * 256, 256)
                    B, Bb = self.wload(wi, l, 0, 16, DFF + (2 * gi + sgi) * 256, 256)
                    for c in range(2):
                        cc = 2 * sgi + c
                        pa = [self.pbank("a") for _ in range(NH)]
                        pb = [self.pbank("a") for _ in range(NH)]
                        for W, Wb, pp in ((A, Ab, pa), (B, Bb, pb)):
                            for k in range(16):
                                for h in range(NH):
                                    S.I("pe", lambda e, W=W, k=k, h=h, c=c, p=pp[h]: e.matmul(
                                        out=p.t[:, :], lhsT=W[:, k, c * 128:(c + 1) * 128], rhs=hT.t[:, k, h * 512:(h + 1) * 512],
                                        start=(k == 0), stop=(k == 15)), [Wb, hT.s(k, h)], [pp[h]], inc=(k == 15))
                        for h in range(NH):
                            tt = tmp[ti % 2]
                            ti += 1
                            S.I("act", lambda e, tt=tt, p=pa[h]: e.activation(out=tt.t[:, :], in_=p.t[:, :], func=AF.Silu),
                                [pa[h]], [tt])
                            S.I("dve", lambda e, tt=tt, p=pb[h], cc=cc, h=h, at=at: e.tensor_tensor(
                                out=at.t[:, cc, h * 512:(h + 1) * 512], in0=tt.t[:, :], in1=p.t[:, :], op=ALU.mult),
                                [tt, pb[h]], [at.s(cc, h)])
                Os = [self.wload(wo, l, (2 * gi + sgi) * 256, 2, 0, D) for sgi in range(2)]
                for j in range(16):
                    for h in range(NH):
                        py = self.pbank("y")
                        for kc in range(4):
                            O, Ob = Os[kc // 2]
                            S.I("pe", lambda e, O=O, kc=kc, j=j, h=h, py=py, at=at: e.matmul(
                                out=py.t[:, :], lhsT=O[:, kc % 2, j * 128:(j + 1) * 128], rhs=at.t[:, kc, h * 512:(h + 1) * 512],
                                start=(kc == 0), stop=(kc == 3)), [Ob, at.s(kc, h)], [py], inc=(kc == 3))
                        S.I("dve", lambda e, j=j, h=h, py=py: e.scalar_tensor_tensor(
                            out=xt.t[:, j, h * 512:(h + 1) * 512], in0=py.t[:, :], scalar=0.5,
                            in1=xt.t[:, j, h * 512:(h + 1) * 512], op0=ALU.mult, op1=ALU.add), [py], [xt.s(j, h)])
            self.store_x(xt, blk, 0, NH)
            S.barrier()

    def stage_final(self):
        S = self.S
        self.set_pools(a=range(8))
        with ExitStack() as es:
            xt = self.sb(es, "fin_x", [128, 16, 512], F32)
            sq = [self.sb(es, f"fin_sq{i}", [128, 512], BF16) for i in range(4)]
            rs = self.sb(es, "fin_rs", [128, 512], F32)
            yt = self.sb(es, "fin_y", [128, 16, 512], F32)
            ot = self.sb(es, "fin_o", [128, 4, D], F32)
            g = self.vec("final_norm")
            for gi in range(T // 512):
                S.D("sp", "dma_ld", xt.t[:, :, :], self.xT[:, gi * 512:(gi + 1) * 512].rearrange("(k p) t -> p k t", p=128),
                    reads=[self.dbuf(("xT", gi))], writes=[xt])
                self.rstd_bc(xt, 0, sq, rs)
                for k in range(16):
                    S.I("dve", lambda e, k=k: e.scalar_tensor_tensor(out=yt.t[:, k, :], in0=xt.t[:, k, :], scalar=g[:, k:k + 1],
                                                                     in1=rs.t[:, :], op0=ALU.mult, op1=ALU.mult),
                        [xt, rs, self.vecs], [yt])
                for j in range(4):
                    for kk in range(4):
                        pb = self.pbank("a")
                        for q in range(4):
                            k = kk * 4 + q
                            S.I("pe", lambda e, pb=pb, j=j, k=k, q=q: e.transpose(
                                out=pb.t[:, q * 128:(q + 1) * 128], in_=yt.t[:, k, j * 128:(j + 1) * 128], identity=self.ident),
                                [yt, self.cst], [pb], inc=(q == 3))
                        if kk % 2:
                            S.I("act", lambda e, pb=pb, j=j, kk=kk: e.copy(out=ot.t[:, j, kk * 512:(kk + 1) * 512], in_=pb.t[:, :]),
                                [pb], [ot])
                        else:
                            S.I("dve", lambda e, pb=pb, j=j, kk=kk: e.tensor_copy(out=ot.t[:, j, kk * 512:(kk + 1) * 512], in_=pb.t[:, :]),
                                [pb], [ot])
                S.D("sp", "dma_st", self.out_d[gi * 512:(gi + 1) * 512, :].rearrange("(j p) d -> p j d", p=128), ot.t[:, :, :],
                    reads=[ot], writes=[self.dbuf(("out", gi))])
            S.barrier()


VEC_ITEMS = [("ffn1_norm", 16), ("mix_norm", 16), ("xa_norm", 16), ("ffn2_norm", 16), ("xa_mem_norm", 16),
             ("conv_b", 16), ("conv_ln_g", 16), ("conv_ln_b", 16), ("gla_gate_b", 8), ("branch_gate_b", 32),
             ("conv_w", 16 * 31)]


def vec_layout():
    off = {}
    o = 0
    for l in range(L):
        for n, w in VEC_ITEMS:
            off[(n, l)] = (o, w)
            o += w
    for n, w in [("final_norm", 16), ("eps", 1), ("one", 1), ("rmask", 4), ("rmaskc", 4), ("ronehot", 4)]:
        off[(n, None)] = (o, w)
        o += w
    return off, o


def pm(v, k):
    return np.ascontiguousarray(np.asarray(v, np.float32).reshape(k, 128).T)


def build_vecs(inp, rank):
    off, n = vec_layout()
    out = np.zeros((128, n), np.float32)
    for l in range(L):
        for name, w in VEC_ITEMS:
            o, _ = off[(name, l)]
            if name == "conv_w":
                cw = np.asarray(inp["conv_w"][l], np.float32)
                out[:, o:o + w] = cw.reshape(31, 16, 128).transpose(2, 1, 0).reshape(128, 16 * 31)
            else:
                out[:, o:o + w] = pm(inp[name][l], w)
    o, _ = off[("final_norm", None)]
    out[:, o:o + 16] = pm(inp["final_norm"], 16)
    out[:, off[("eps", None)][0]] = EPS
    out[:, off[("one", None)][0]] = 1.0
    for p in range(4):
        out[:, off[("rmask", None)][0] + p] = 1.0 if p < rank else 0.0
        out[:, off[("rmaskc", None)][0] + p] = 0.0 if p < rank else 1.0
        out[:, off[("ronehot", None)][0] + p] = 1.0 if p == rank - 1 else 0.0
    return out


def build_cst():
    c = np.zeros((128, 128 + 64 + TB), np.float32)
    c[:, 0:128] = np.eye(128, dtype=np.float32)
    jj, ii = np.meshgrid(np.arange(64), np.arange(64), indexing="ij")
    c[0:64, 128:192] = (jj <= ii).astype(np.float32)
    cm = np.ones(TB, np.float32)
    cm[0::64] = 0.0
    c[:, 192:] = cm[None, :]
    return c


def pack_slabs(inp, specs):
    wall = np.zeros((max(1, len(specs)), 128, SLOT), np.float32)
    for i, (name, l, r0, nkc, c0, ncols) in enumerate(specs):
        w = inp[name][l]
        blk = np.asarray(w[r0:r0 + nkc * 128, c0:c0 + ncols], np.float32)
        wall[i, :, :nkc * ncols] = blk.reshape(nkc, 128, ncols).transpose(1, 0, 2).reshape(128, nkc * ncols)
    return wall


_CACHE = {}


def get_program(debug=None):
    key = repr(sorted((debug or {}).items()))
    if key not in _CACHE:
        plan = Prog(plan=None, debug=debug).build()
        prog = Prog(plan=plan, debug=debug)
        nc = prog.build()
        _CACHE[key] = (nc, plan)
    return _CACHE[key]


def kernel(debug=None, ncores=NCORE, **inp):
    nc, plan = get_program(debug)
    x = np.asarray(inp["x"], np.float32)
    mem = np.asarray(inp["mem"], np.float32)
    wall = pack_slabs(inp, plan["specs"])
    cst = build_cst()
    gnb = np.ascontiguousarray(np.broadcast_to(np.asarray(inp["gla_out_norm"], np.float32)[:, None, :], (L, 128, D)))
    gw2 = np.ascontiguousarray(np.asarray(inp["gla_gate_w2"], np.float32))
    in_maps = []
    for c in range(ncores):
        b, r = c // 4, c % 4
        in_maps.append({
            "x_in": np.ascontiguousarray(x[b, r * T:(r + 1) * T, :]),
            "mem_in": np.ascontiguousarray(mem[b]),
            "vecs": build_vecs(inp, r),
            "cst": cst, "gnb": gnb, "gw2": gw2, "wall": wall,
        })
    res = run_bass_kernel_spmd(nc, in_maps, core_ids=list(range(ncores)))
    if debug and debug.get("raw"):
        return res.results
    out = np.zeros((2, 8192, D), np.float32)
    for c in range(ncores):
        b, r = c // 4, c % 4
        out[b, r * T:(r + 1) * T, :] = np.asarray(res.results[c]["out"], np.float32)
    return out
```
